# Optimizing a Trainium2 kernel written in Bass

```python
import math
import jax, jax.numpy as jnp
from jax import lax
import numpy as np


D_MODEL = 1024
BATCH = 4
SEQ = 8192
DEPTH = 1

CHUNK = 64
RWKV_HEADS = 16
RWKV_HEAD_DIM = 64
RWKV_WIDTH = RWKV_HEADS * RWKV_HEAD_DIM
DECAY_LORA = 64
ICLR_LORA = 64
S5_GROUPS = 32
S5_GROUP_DIM = 16
S5_WIDTH = S5_GROUPS * S5_GROUP_DIM
S5_STATE = 64
SHIFT_WIDTH = 3 * RWKV_WIDTH + DECAY_LORA + ICLR_LORA
IN_WIDTH = SHIFT_WIDTH + RWKV_WIDTH + 2 * S5_WIDTH + 2 * D_MODEL
SPLIT_POINTS = (SHIFT_WIDTH,
                SHIFT_WIDTH + RWKV_WIDTH,
                SHIFT_WIDTH + RWKV_WIDTH + S5_WIDTH,
                SHIFT_WIDTH + RWKV_WIDTH + 2 * S5_WIDTH,
                SHIFT_WIDTH + RWKV_WIDTH + 2 * S5_WIDTH + D_MODEL)
RWKV_SPLITS = (RWKV_WIDTH, 2 * RWKV_WIDTH, 3 * RWKV_WIDTH, 3 * RWKV_WIDTH + DECAY_LORA)
RMS_EPS = 1e-6
LNX_EPS = 64e-5
DT_MIN = 1e-3
DT_MAX = 1e-1

kernel_name = 'rwkv7_s5_gated_hybrid_block'


def rms_norm(x, g):
    xf = x.astype(jnp.float32)
    y = xf * lax.rsqrt(jnp.mean(xf * xf, axis=-1, keepdims=True) + RMS_EPS)
    return (y * g.astype(jnp.float32)).astype(x.dtype)


def token_shift(z):
    return jnp.pad(z, ((0, 0), (1, 0), (0, 0)))[:, :-1]


def wkv7(r, w, k, v, a, b):
    bsz, seq, nh, nd = r.shape
    n_chunks = seq // CHUNK

    def to_chunks(t):
        return jnp.moveaxis(t, 1, 0).reshape(n_chunks, CHUNK, bsz, nh, nd)

    def frame_step(S, inp):
        r_t, w_t, k_t, v_t, a_t, b_t = inp
        sa = jnp.einsum('bhij,bhj->bhi', S, a_t)
        S = S * w_t[:, :, None, :] + sa[..., None] * b_t[:, :, None, :] + v_t[..., None] * k_t[:, :, None, :]
        return S, jnp.einsum('bhij,bhj->bhi', S, r_t)

    def chunk_step(S, inp):
        return lax.scan(frame_step, S, inp)

    S0 = jnp.zeros((bsz, nh, nd, nd), jnp.float32)
    _, y = lax.scan(chunk_step, S0, (to_chunks(r), to_chunks(w), to_chunks(k),
                                      to_chunks(v), to_chunks(a), to_chunks(b)))
    return jnp.moveaxis(y.reshape(seq, bsz, nh, nd), 0, 1)


def s5_scan(u, lam_re, lam_im, log_dt, b_re, b_im, c_re, c_im, d_skip):
    f32 = jnp.float32
    lam_re = lam_re.astype(f32)
    lam_im = lam_im.astype(f32)
    dt = jnp.exp(log_dt.astype(f32))[:, None]
    mag = jnp.exp(lam_re * dt)
    ang = lam_im * dt
    ab_re = mag * jnp.cos(ang)
    ab_im = mag * jnp.sin(ang)
    den = lam_re * lam_re + lam_im * lam_im
    nr = ab_re - 1.0
    f_re = (nr * lam_re + ab_im * lam_im) / den
    f_im = (ab_im * lam_re - nr * lam_im) / den
    b_re = b_re.astype(f32)
    b_im = b_im.astype(f32)
    bb_re = f_re[..., None] * b_re - f_im[..., None] * b_im
    bb_im = f_re[..., None] * b_im + f_im[..., None] * b_re
    u_t = jnp.moveaxis(u, 1, 0)
    bu_re = jnp.einsum('lbgh,gph->lbgp', u_t, bb_re)
    bu_im = jnp.einsum('lbgh,gph->lbgp', u_t, bb_im)
    seq = u_t.shape[0]
    a_re = jnp.broadcast_to(ab_re, (seq, 1) + ab_re.shape)
    a_im = jnp.broadcast_to(ab_im, (seq, 1) + ab_im.shape)

    def combine(e1, e2):
        a1r, a1i, b1r, b1i = e1
        a2r, a2i, b2r, b2i = e2
        return (a2r * a1r - a2i * a1i,
                a2r * a1i + a2i * a1r,
                a2r * b1r - a2i * b1i + b2r,
                a2r * b1i + a2i * b1r + b2i)

    _, _, s_re, s_im = lax.associative_scan(combine, (a_re, a_im, bu_re, bu_im), axis=0)
    y = (jnp.einsum('lbgp,ghp->blgh', s_re, c_re.astype(f32))
         - jnp.einsum('lbgp,ghp->blgh', s_im, c_im.astype(f32)))
    return y + d_skip.astype(f32) * u


def setup_inputs(seed: int = 0) -> dict:
    key = jax.random.key(seed)
    ks = jax.random.split(key, 32)
    f32 = jnp.float32
    L = DEPTH

    def nrm(k, shape, scale):
        return scale * jax.random.normal(k, shape, f32)

    x = jax.random.normal(ks[0], (BATCH, SEQ, D_MODEL), f32)
    norm_g = 1.0 + nrm(ks[1], (L, D_MODEL), 0.02)
    w_in = nrm(ks[2], (L, D_MODEL, IN_WIDTH), D_MODEL ** -0.5)
    mu_shift = jax.random.uniform(ks[3], (L, SHIFT_WIDTH), f32, 0.1, 0.9)
    ratio = jnp.arange(RWKV_WIDTH, dtype=f32) / (RWKV_WIDTH - 1)
    w0 = (-7.0 + 5.0 * ratio ** 0.85 + 0.5)[None, :] + nrm(ks[4], (L, RWKV_WIDTH), 0.1)
    w_up = nrm(ks[5], (L, DECAY_LORA, RWKV_WIDTH), 0.5 * DECAY_LORA ** -0.5)
    a0 = nrm(ks[6], (L, RWKV_WIDTH), 0.1)
    a_up = nrm(ks[7], (L, ICLR_LORA, RWKV_WIDTH), 0.5 * ICLR_LORA ** -0.5)
    k_k = 0.85 + nrm(ks[8], (L, RWKV_WIDTH), 0.02)
    k_a = 1.0 + nrm(ks[9], (L, RWKV_WIDTH), 0.02)
    r_k = nrm(ks[10], (L, RWKV_HEADS, RWKV_HEAD_DIM), 0.1)
    lnx_g = 1.0 + nrm(ks[11], (L, RWKV_WIDTH), 0.02)
    lnx_b = nrm(ks[12], (L, RWKV_WIDTH), 0.02)
    n_idx = jnp.arange(S5_STATE, dtype=f32)
    lam_re = -0.5 + nrm(ks[13], (L, S5_GROUPS, S5_STATE), 0.01)
    lam_im = math.pi * n_idx[None, None, :] + nrm(ks[14], (L, S5_GROUPS, S5_STATE), 0.01)
    log_dt = jax.random.uniform(ks[15], (L, S5_GROUPS), f32, math.log(DT_MIN), math.log(DT_MAX))
    b_re = nrm(ks[16], (L, S5_GROUPS, S5_STATE, S5_GROUP_DIM), (2 * S5_GROUP_DIM) ** -0.5)
    b_im = nrm(ks[17], (L, S5_GROUPS, S5_STATE, S5_GROUP_DIM), (2 * S5_GROUP_DIM) ** -0.5)
    c_re = nrm(ks[18], (L, S5_GROUPS, S5_GROUP_DIM, S5_STATE), S5_STATE ** -0.5)
    c_im = nrm(ks[19], (L, S5_GROUPS, S5_GROUP_DIM, S5_STATE), S5_STATE ** -0.5)
    d_skip = nrm(ks[20], (L, S5_GROUPS, S5_GROUP_DIM), 0.5)
    w_glu = nrm(ks[21], (L, S5_WIDTH, 2 * S5_WIDTH), S5_WIDTH ** -0.5)
    b_glu = nrm(ks[22], (L, 2 * S5_WIDTH), 0.02)
    p_a = nrm(ks[23], (L, RWKV_WIDTH, D_MODEL), RWKV_WIDTH ** -0.5)
    p_b = nrm(ks[24], (L, S5_WIDTH, D_MODEL), S5_WIDTH ** -0.5)
    w_out = nrm(ks[25], (L, D_MODEL, D_MODEL), D_MODEL ** -0.5)
    final_g = 1.0 + nrm(ks[26], (D_MODEL,), 0.02)
    return {'x': x, 'norm_g': norm_g, 'w_in': w_in, 'mu_shift': mu_shift, 'w0': w0,
            'w_up': w_up, 'a0': a0, 'a_up': a_up, 'k_k': k_k, 'k_a': k_a, 'r_k': r_k,
            'lnx_g': lnx_g, 'lnx_b': lnx_b, 'lam_re': lam_re, 'lam_im': lam_im,
            'log_dt': log_dt, 'b_re': b_re, 'b_im': b_im, 'c_re': c_re, 'c_im': c_im,
            'd_skip': d_skip, 'w_glu': w_glu, 'b_glu': b_glu, 'p_a': p_a, 'p_b': p_b,
            'w_out': w_out, 'final_g': final_g}


def reference(x, norm_g, w_in, mu_shift, w0, w_up, a0, a_up, k_k, k_a, r_k, lnx_g, lnx_b,
              lam_re, lam_im, log_dt, b_re, b_im, c_re, c_im, d_skip, w_glu, b_glu,
              p_a, p_b, w_out, final_g):
    f32 = jnp.float32
    bsz, seq, _ = x.shape
    H, N = RWKV_HEADS, RWKV_HEAD_DIM

    def heads(t):
        return t.reshape(bsz, seq, H, N).astype(f32)

    for l in range(DEPTH):
        h = rms_norm(x, norm_g[l])
        z = h @ w_in[l]
        zs, gate_a, u, gate_b, m_a, m_b = jnp.split(z, SPLIT_POINTS, axis=-1)

        zs = zs + mu_shift[l] * (token_shift(zs) - zs)
        r, k, v, xw, xa = jnp.split(zs, RWKV_SPLITS, axis=-1)
        w_log = -jax.nn.softplus(-(w0[l] + jnp.tanh(xw) @ w_up[l]).astype(f32)) - 0.5
        decay = jnp.exp(-jnp.exp(w_log))
        iclr = jax.nn.sigmoid((a0[l] + xa @ a_up[l]).astype(f32))
        iclr_h = iclr.reshape(bsz, seq, H, N)
        kk = heads(k * k_k[l])
        kk = kk / jnp.maximum(jnp.sqrt(jnp.sum(kk * kk, axis=-1, keepdims=True)), 1e-12)
        ka = k_a[l].reshape(H, N).astype(f32)
        k_h = heads(k) * (1.0 + (iclr_h - 1.0) * ka)
        r_h = heads(r)
        v_h = heads(v)
        y = wkv7(r_h, decay.reshape(bsz, seq, H, N), k_h, v_h, -kk, kk * iclr_h)
        mu = jnp.mean(y, axis=-1, keepdims=True)
        var = jnp.mean(jnp.square(y - mu), axis=-1, keepdims=True)
        y = (y - mu) * lax.rsqrt(var + LNX_EPS)
        y = y * lnx_g[l].reshape(H, N).astype(f32) + lnx_b[l].reshape(H, N).astype(f32)
        y = y + jnp.sum(r_h * k_h * r_k[l].astype(f32), axis=-1, keepdims=True) * v_h
        y_a = y.reshape(bsz, seq, RWKV_WIDTH).astype(x.dtype) * jax.nn.silu(gate_a)

        ys = s5_scan(u.reshape(bsz, seq, S5_GROUPS, S5_GROUP_DIM).astype(f32),
                     lam_re[l], lam_im[l], log_dt[l], b_re[l], b_im[l], c_re[l], c_im[l], d_skip[l])
        ys = jax.nn.gelu(ys.reshape(bsz, seq, S5_WIDTH).astype(x.dtype))
        g1, g2 = jnp.split(ys @ w_glu[l] + b_glu[l], 2, axis=-1)
        y_b = g1 * jax.nn.sigmoid(g2) * jax.nn.silu(gate_b)

        merged = jax.nn.sigmoid(m_a) * (y_a @ p_a[l]) + jax.nn.sigmoid(m_b) * (y_b @ p_b[l])
        x = x + merged @ w_out[l]
    return rms_norm(x, final_g)
```

```python
import numpy as np
from contextlib import ExitStack
import concourse.bass as bass
import concourse.mybir as mybir
from concourse.bass_utils import run_bass_kernel_spmd

F32, BF16, I32 = mybir.dt.float32, mybir.dt.bfloat16, mybir.dt.int32
AF = mybir.ActivationFunctionType
ALU = mybir.AluOpType
AX = mybir.AxisListType

D = 1024
NCT = 58
C0 = 0.6065306597126334
TWO_PI = 6.283185307179586
SAME_ENG_SYNC = True
MAXV = 30000
NSLOT = 6


class Sched:
    ENGS = ["pe", "act", "dve", "pool", "sp"]
    NINST = 0

    def __init__(self):
        self.ops = []
        self.last_w = {}
        self.readers = {}

    def add(self, eng, fn, r=(), w=(), dma=False):
        idx = len(self.ops)
        deps = set()
        for k in r:
            if k in self.last_w:
                deps.add(self.last_w[k])
        for k in w:
            if k in self.last_w:
                deps.add(self.last_w[k])
            deps.update(self.readers.get(k, ()))
        for k in r:
            self.readers.setdefault(k, []).append(idx)
        for k in w:
            self.last_w[k] = idx
            self.readers[k] = []
        deps.discard(idx)
        self.ops.append(dict(eng=eng, fn=fn, deps=sorted(deps), dma=dma, marked=False))
        return idx

    def emit(self, nc, es, drain=False):
        ops = self.ops
        for o in ops:
            for d in o["deps"]:
                od = ops[d]
                if od["dma"]:
                    od["marked"] = True
                elif od["eng"] != o["eng"]:
                    od["marked"] = True
                elif o["eng"] != "pe" and SAME_ENG_SYNC:
                    od["marked"] = True
                elif o["dma"]:
                    od["marked"] = True
        cnt = {e: 0 for e in self.ENGS}
        slot_cnt = {e: 0 for e in self.ENGS}
        slot_val = {}
        nsem_needed = {e: 0 for e in self.ENGS}
        for o in ops:
            e = o["eng"]
            if o["dma"]:
                s = slot_cnt[e] % NSLOT
                slot_cnt[e] += 1
                key = ("dma", e, s)
                prev = slot_val.get(key, 0)
                o["prev_wait"] = (key, prev) if prev > 0 else None
                slot_val[key] = prev + 16
                o["sig"] = (key, prev + 16)
            elif o["marked"]:
                cnt[e] += 1
                ep = (cnt[e] - 1) // MAXV
                o["sig"] = (("c", e, ep), (cnt[e] - 1) % MAXV + 1)
                nsem_needed[e] = max(nsem_needed[e], ep + 1)
        sems = {}
        for e in self.ENGS:
            for ep in range(nsem_needed[e]):
                sems[("c", e, ep)] = es.enter_context(nc.semaphore(f"s{Sched.NINST}_{e}_{ep}"))
            if slot_cnt[e] > 0:
                for s in range(min(NSLOT, slot_cnt[e])):
                    sems[("dma", e, s)] = es.enter_context(nc.semaphore(f"sd{Sched.NINST}_{e}_{s}"))
        Sched.NINST += 1
        block_cm = nc.Block()
        block = block_cm.__enter__()
        regs = {"pe": block.tensor, "act": block.scalar, "dve": block.vector,
                "pool": block.gpsimd, "sp": block.sync}
        streams = {e: [o for o in ops if o["eng"] == e] for e in self.ENGS}

        def make(e_name):
            def body(e):
                known = {}
                for o in streams[e_name]:
                    waits = {}
                    for d in o["deps"]:
                        od = ops[d]
                        if not od["marked"]:
                            continue
                        if od["eng"] == e_name and not od["dma"]:
                            if e_name == "pe" and not o["dma"]:
                                continue
                            if not SAME_ENG_SYNC and not o["dma"]:
                                continue
                        k, v = od["sig"]
                        waits[k] = max(waits.get(k, 0), v)
                    if o["dma"] and o["prev_wait"] is not None:
                        k, v = o["prev_wait"]
                        waits[k] = max(waits.get(k, 0), v)
                    for k, v in waits.items():
                        if known.get(k, 0) >= v:
                            continue
                        e.wait_ge(sems[k], v)
                        known[k] = v
                    if o["fn"] is None:
                        continue
                    ins = o["fn"](e)
                    if o["dma"]:
                        k, v = o["sig"]
                        ins.then_inc(sems[k], 16)
                    elif o["marked"]:
                        k, v = o["sig"]
                        ins.then_inc(sems[k], 1)
                if drain and e_name == "sp":
                    for k, v in slot_val.items():
                        e.wait_ge(sems[k], v)
            return body
        for e_name in self.ENGS:
            if streams[e_name]:
                regs[e_name](make(e_name))
        block_cm.__exit__(None, None, None)


def build_program(T, debug=False):
    assert T % 256 == 0
    NST = T // 256
    nc = bass.Bass("TRN2", target_bir_lowering=False)
    S = Sched()
    es = ExitStack()

    def din(name, shape, dt=F32):
        return nc.dram_tensor(name, list(shape), dt, kind="ExternalInput").ap()

    x_d = din("x", [T, D])
    norm_g_d = din("norm_g", [1, D])
    w_in_d = din("w_in", [D, 7296])
    mu_d = din("mu_shift", [1, 3200])
    w0_d = din("w0", [1, D]); w_up_d = din("w_up", [64, D])
    a0_d = din("a0", [1, D]); a_up_d = din("a_up", [64, D])
    k_k_d = din("k_k", [1, D]); k_a_d = din("k_a", [1, D]); r_k_d = din("r_k", [1, D])
    lnx_g_d = din("lnx_g", [1, D]); lnx_b_d = din("lnx_b", [1, D])
    lam_re_d = din("lam_re", [32, 64]); lam_im_d = din("lam_im", [32, 64]); log_dt_d = din("log_dt", [32, 1])
    b_re_d = din("b_re", [32, 1024]); b_im_d = din("b_im", [32, 1024])
    c_re_d = din("c_re", [512, 64]); c_im_d = din("c_im", [512, 64])
    d_skip_d = din("d_skip", [1, 512])
    w_glu_d = din("w_glu", [512, D]); b_glu_d = din("b_glu", [1, D])
    p_a_d = din("p_a", [D, D]); p_b_d = din("p_b", [512, D]); w_out_d = din("w_out", [D, D])
    final_g_d = din("final_g", [1, D])
    out_d = nc.dram_tensor("out", [T, D], F32, kind="ExternalOutput").ap()
    wbf_d = nc.dram_tensor("wbf", [NCT, 128, 8 * 128], BF16, kind="Internal").ap()
    s_tab_d = nc.dram_tensor("s_tab", [2, 2048], F32, kind="Internal").ap()
    s_bb_d = nc.dram_tensor("s_bb", [2, 32, 64, 16], F32, kind="Internal").ap()
    s_bm_d = nc.dram_tensor("s_bm", [512, 1024], F32, kind="Internal").ap()
    s_ct_d = nc.dram_tensor("s_ct", [2, 64, 512], F32, kind="Internal").ap()
    dbg = {}

    sb_bytes = [0]

    sb_cache = {}

    def sb(name, shape, dt=F32):
        if name in sb_cache:
            return sb_cache[name]
        n = 1
        for d_ in shape[1:]:
            n *= d_
        sb_bytes[0] += n * (2 if dt == BF16 else 4)
        t_ = es.enter_context(nc.sbuf_tensor(name, list(shape), dt))
        sb_cache[name] = t_
        return t_

    for (n_, sh_, dt_) in [("ones_f", [128, 128], F32), ("ident", [128, 128], BF16), ("m_su", [128, 128], BF16),
                           ("m_iu", [128, 128], BF16), ("blk1", [128, 128], BF16), ("hsel", [128, 2], BF16),
                           ("mu_col", [128, 26], F32), ("w0c", [128, 8], F32), ("a0c", [128, 8], F32), ("kkc", [128, 8], F32),
                           ("kac", [128, 8], F32), ("rkc", [128, 8], F32), ("dsk", [128, 4], F32),
                           ("pa_bf", [128, 8, D], BF16), ("pb_bf", [128, 4, D], BF16), ("wglu_bf", [128, 4, D], BF16),
                           ("wout_bf", [128, 8, D], BF16), ("wup_bf", [64, D], BF16), ("aup_bf", [64, D], BF16),
                           ("lnxg_row", [128, D], BF16), ("lnxb_row", [128, D], BF16), ("bglu_row", [128, D], BF16),
                           ("fg_row", [128, D], F32),
                           ("E_re", [128, 2048], BF16), ("E_im", [128, 2048], BF16), ("D_re", [128, 16, 128], BF16),
                           ("D_im", [128, 16, 128], BF16), ("L_re", [128, 16], F32), ("L_im", [128, 16], F32),
                           ("Cre_pad", [128, 16, 128], BF16), ("Cimn_pad", [128, 16, 128], BF16), ("Bm_bd", [128, 4, 1024], BF16),
                           ("xe_re", [128, 16], F32), ("xe_im", [128, 16], F32)]:
        sb(n_, sh_, dt_)

    es_setup = ExitStack()

    def sbt(name, shape, dt=F32):
        return es_setup.enter_context(nc.sbuf_tensor(name, list(shape), dt))

    def mm(out, lhsT, rhs, start, stop, r, w):
        S.add("pe", lambda e: e.matmul(out, lhsT=lhsT, rhs=rhs, start=start, stop=stop), r, w)

    def act(out, in_, func, r, w, bias=None, scale=None, accum=None, eng="act"):
        kw = {}
        if bias is not None: kw["bias"] = bias
        if scale is not None: kw["scale"] = scale
        if accum is not None: kw["accum_out"] = accum
        S.add("act", lambda e: e.activation(out=out, in_=in_, func=func, **kw), r, w)

    def tt(eng, out, in0, in1, op, r, w):
        S.add(eng, lambda e: e.tensor_tensor(out=out, in0=in0, in1=in1, op=op), r, w)

    def ts(eng, out, in0, s1, op0, r, w, s2=None, op1=None):
        if op1 is None:
            S.add(eng, lambda e: e.tensor_scalar(out=out, in0=in0, scalar1=s1, scalar2=None, op0=op0), r, w)
        else:
            S.add(eng, lambda e: e.tensor_scalar(out=out, in0=in0, scalar1=s1, scalar2=s2, op0=op0, op1=op1), r, w)

    def stt(out, in0, scalar, in1, op0, op1, r, w):
        S.add("dve", lambda e: e.scalar_tensor_tensor(out=out, in0=in0, scalar=scalar, in1=in1, op0=op0, op1=op1), r, w)

    def cp(eng, out, in_, r, w):
        if eng == "act":
            S.add("act", lambda e: e.copy(out=out, in_=in_), r, w)
        else:
            S.add(eng, lambda e: e.tensor_copy(out=out, in_=in_), r, w)

    def dma(q, out, in_, r, w, slow=False):
        if slow:
            S.add(q, lambda e: e.dma_start(out=out, in_=in_, allow_slow_non_contiguous=True), r, w, dma=True)
        else:
            S.add(q, lambda e: e.dma_start(out=out, in_=in_), r, w, dma=True)

    def memset(eng, ap, val, w):
        S.add(eng, lambda e: e.memset(ap, val), (), w)

    def bc_last(ap2d, n):
        return ap2d.unsqueeze(2).to_broadcast([ap2d.shape[0], ap2d.shape[1], n])

    def bc_mid(ap2d, n):
        return ap2d.unsqueeze(1).to_broadcast([ap2d.shape[0], n, ap2d.shape[1]])

    psb = [es.enter_context(nc.psum_tensor(f"psb{i}", [128, 512], F32)) for i in range(8)]
    ps_rr = [0]

    def ps_alloc():
        i = 3 + ps_rr[0] % 5
        ps_rr[0] += 1
        return psb[i], ("ps", i)

    ones_f = sb("ones_f", [128, 128]); ident_f = sbt("ident_f", [128, 128])
    ident = sb("ident", [128, 128], BF16)
    m_su = sb("m_su", [128, 128], BF16)
    m_iu = sb("m_iu", [128, 128], BF16)
    blk1 = sb("blk1", [128, 128], BF16)
    hsel = sb("hsel", [128, 2], BF16)
    tmpc = sbt("tmpc", [128, 128])
    memset("pool", ones_f[:, :], 1.0, ["ones_f"])

    def aff(out_ap, pattern, cmp, base, cm, key, src=None):
        src_ap = ones_f[:, 0:out_ap.shape[1]] if src is None else src
        S.add("pool", lambda e: e.affine_select(out=out_ap, in_=src_ap, pattern=pattern, compare_op=cmp,
                                                fill=0.0, base=base, channel_multiplier=cm),
              ["ones_f"], [key])

    aff(ident_f[:, :], [[-1, 128]], ALU.is_equal, 0, 1, "ident_f")
    cp("pool", ident[:, :], ident_f[:, :], ["ident_f"], ["ident"])
    aff(tmpc[:, :], [[1, 128]], ALU.is_gt, 0, -1, "tmpc")
    cp("pool", m_su[:, :], tmpc[:, :], ["tmpc"], ["m_su"])
    aff(tmpc[:, :], [[1, 128]], ALU.is_ge, 0, -1, "tmpc")
    cp("pool", m_iu[:, :], tmpc[:, :], ["tmpc"], ["m_iu"])
    aff(tmpc[:, 0:64], [[0, 64]], ALU.is_ge, 63, -1, "tmpc")
    aff(tmpc[:, 64:128], [[0, 64]], ALU.is_ge, -64, 1, "tmpc")
    cp("pool", blk1[:, :], tmpc[:, :], ["tmpc"], ["blk1"])
    cp("pool", hsel[:, 0:1], tmpc[:, 0:1], ["tmpc"], ["hsel"])
    cp("pool", hsel[:, 1:2], tmpc[:, 64:65], ["tmpc"], ["hsel"])

    mu_col = sb("mu_col", [128, 26])
    w0c = sb("w0c", [128, 8]); a0c = sb("a0c", [128, 8]); kkc = sb("kkc", [128, 8])
    kac = sb("kac", [128, 8]); rkc = sb("rkc", [128, 8]); dsk = sb("dsk", [128, 4]); gcol = sbt("gcol", [128, 8])
    dma("sp", mu_col[:, 0:24], mu_d[0, 0:3072].rearrange("(c p) -> p c", p=128), [], ["mu_col"], slow=True)
    dma("sp", mu_col[0:64, 24:26], mu_d[0, 3072:3200].rearrange("(c p) -> p c", p=64), [], ["mu_col"], slow=True)
    for (t_, d_, n_) in [(w0c, w0_d, "w0c"), (a0c, a0_d, "a0c"), (kkc, k_k_d, "kkc"), (kac, k_a_d, "kac"),
                         (rkc, r_k_d, "rkc"), (gcol, norm_g_d, "gcol")]:
        dma("sp", t_[:, :], d_[0, :].rearrange("(c p) -> p c", p=128), [], [n_], slow=True)
    dma("sp", dsk[:, :], d_skip_d[0, :].rearrange("(c p) -> p c", p=128), [], ["dsk"], slow=True)
    stage = [sbt(f"stage{i}", [128, 1024]) for i in range(2)]
    stage_bf = [sbt(f"stagebf{i}", [128, 1024], BF16) for i in range(2)]
    st_rr = [0]

    def stage_next():
        i = st_rr[0] % 2
        st_rr[0] += 1
        return i

    pa_bf = sb("pa_bf", [128, 8, D], BF16); pb_bf = sb("pb_bf", [128, 4, D], BF16)
    wglu_bf = sb("wglu_bf", [128, 4, D], BF16); wout_bf = sb("wout_bf", [128, 8, D], BF16)
    wup_bf = sb("wup_bf", [64, D], BF16); aup_bf = sb("aup_bf", [64, D], BF16)

    def load_cast(dst_ap, src_ap, key, parts=128, rkeys=()):
        i = stage_next()
        dma("sp" if i == 0 else "pool", stage[i][0:parts, :], src_ap, list(rkeys), [("stage", i)])
        cp("act" if i == 0 else "dve", dst_ap, stage[i][0:parts, :], [("stage", i)], [key])

    for kc in range(8):
        load_cast(pa_bf[:, kc, :], p_a_d[kc * 128:(kc + 1) * 128, :], "pa_bf")
        load_cast(wout_bf[:, kc, :], w_out_d[kc * 128:(kc + 1) * 128, :], "wout_bf")
    for kc in range(4):
        load_cast(pb_bf[:, kc, :], p_b_d[kc * 128:(kc + 1) * 128, :], "pb_bf")
        load_cast(wglu_bf[:, kc, :], w_glu_d[kc * 128:(kc + 1) * 128, :], "wglu_bf")
    lnxg_row = sb("lnxg_row", [128, D], BF16); lnxb_row = sb("lnxb_row", [128, D], BF16)
    bglu_row = sb("bglu_row", [128, D], BF16); fg_row = sb("fg_row", [128, D])
    for (t_, d_, n_) in [(lnxg_row, lnx_g_d, "lnxg_row"), (lnxb_row, lnx_b_d, "lnxb_row"), (bglu_row, b_glu_d, "bglu_row")]:
        load_cast(t_[:, :], d_[0, :].partition_broadcast(128), n_)
    dma("pool", fg_row[:, :], final_g_d[0, :].partition_broadcast(128), [], ["fg_row"])
    load_cast(wup_bf[:, :], w_up_d[:, :], "wup_bf", parts=64)
    load_cast(aup_bf[:, :], a_up_d[:, :], "aup_bf", parts=64)

    ct_cols = []
    for ct in range(NCT):
        if ct < 24: ct_cols.append((ct * 128, 128))
        elif ct == 24: ct_cols.append((3072, 64))
        elif ct == 25: ct_cols.append((3136, 64))
        else: ct_cols.append((3200 + (ct - 26) * 128, 128))
    for kc in range(8):
        for blk in range(8):
            c0 = blk * 1024
            ncol = min(1024, 7296 - c0)
            i = stage_next()
            q = "sp" if i == 0 else "pool"
            dma(q, stage[i][:, 0:ncol], w_in_d[kc * 128:(kc + 1) * 128, c0:c0 + ncol], [], [("stage", i)])
            ts("dve" if i else "pool", stage_bf[i][:, 0:ncol], stage[i][:, 0:ncol], gcol[:, kc:kc + 1], ALU.mult,
               [("stage", i), "gcol"], [("stagebf", i)])
            for ct in range(NCT):
                cs, cn = ct_cols[ct]
                if cs >= c0 and cs < c0 + ncol:
                    dma(q, wbf_d[ct, :, kc * 128:kc * 128 + cn], stage_bf[i][:, cs - c0:cs - c0 + cn],
                        [("stagebf", i)], [("wbf", ct)])

    E_re = sb("E_re", [128, 2048], BF16); E_im = sb("E_im", [128, 2048], BF16)
    D_re = sb("D_re", [128, 16, 128], BF16); D_im = sb("D_im", [128, 16, 128], BF16)
    L_re = sb("L_re", [128, 16]); L_im = sb("L_im", [128, 16])
    Cre_pad = sb("Cre_pad", [128, 16, 128], BF16); Cimn_pad = sb("Cimn_pad", [128, 16, 128], BF16)
    Bm_bd = sb("Bm_bd", [128, 4, 1024], BF16)
    xe_re = sb("xe_re", [128, 16]); xe_im = sb("xe_im", [128, 16])
    tA = sbt("tA", [128, 512]); tB = sbt("tB", [128, 512]); tC = sbt("tC", [128, 512]); tDd = sbt("tD", [128, 512])
    tI = sbt("tI", [128, 512], I32); tM = sbt("tM", [128, 512])
    s5a = sbt("s5a", [32, 64]); s5b = sbt("s5b", [32, 64]); s5c = sbt("s5c", [32, 64]); s5d = sbt("s5d", [32, 64])
    s5e = sbt("s5e", [32, 64]); s5f = sbt("s5f", [32, 64]); s5g = sbt("s5g", [32, 64]); s5h = sbt("s5h", [32, 64])
    s5i = sbt("s5i", [32, 64]); s5i32 = sbt("s5i32", [32, 64], I32)
    ldt = sbt("ldt", [32, 1]); lre = sbt("lre", [32, 64]); lim = sbt("lim", [32, 64])
    dma("sp", ldt[:, :], log_dt_d[:, :], [], ["ldt"])
    dma("sp", lre[:, :], lam_re_d[:, :], [], ["lre"])
    dma("sp", lim[:, :], lam_im_d[:, :], [], ["lim"])
    act(ldt[:, :], ldt[:, :], AF.Exp, ["ldt"], ["ldt"])
    ts("dve", s5a[:, :], lre[:, :], ldt[:, 0:1], ALU.mult, ["lre", "ldt"], ["s5a"])
    ts("dve", s5b[:, :], lim[:, :], ldt[:, 0:1], ALU.mult, ["lim", "ldt"], ["s5b"], s2=1.0 / TWO_PI, op1=ALU.mult)
    dma("sp", s_tab_d[0, :].rearrange("(g p) -> g p", p=64), s5a[:, :], ["s5a"], ["s_tab"])
    dma("sp", s_tab_d[1, :].rearrange("(g p) -> g p", p=64), s5b[:, :], ["s5b"], ["s_tab"])

    def sincos(turn_ap, sin_out, cos_out, tmps, tkeys, keys_r, key_s, key_c):
        ta, tb, ti = tmps
        ka, kb, ki = tkeys
        cp("dve", ti, turn_ap, keys_r, [ki])
        cp("dve", ta, ti, [ki], [ka])
        tt("dve", ta, turn_ap, ta, ALU.subtract, keys_r + [ka], [ka])
        for thr, sgn in ((0.5, -1.0), (-0.5, 1.0)):
            ts("dve", tb, ta, thr, ALU.is_gt if sgn < 0 else ALU.is_lt, [ka], [kb])
            stt(ta, tb, sgn, ta, ALU.mult, ALU.add, [ka, kb], [ka])
        act(sin_out, ta, AF.Sin, [ka], [key_s], scale=TWO_PI * (1 - 1e-6))
        ts("dve", ta, ta, 0.25, ALU.add, [ka], [ka])
        ts("dve", tb, ta, 0.5, ALU.is_gt, [ka], [kb])
        stt(ta, tb, -1.0, ta, ALU.mult, ALU.add, [ka, kb], [ka])
        act(cos_out, ta, AF.Sin, [ka], [key_c], scale=TWO_PI * (1 - 1e-6))

    sincos(s5b[:, :], s5c[:, :], s5d[:, :], (s5h[:, :], s5i[:, :], s5i32[:, :]), ("s5h", "s5i", "s5i32"), ["s5b"], "s5c", "s5d")
    act(s5e[:, :], s5a[:, :], AF.Exp, ["s5a"], ["s5e"])
    tt("dve", s5d[:, :], s5d[:, :], s5e[:, :], ALU.mult, ["s5d", "s5e"], ["s5d"])
    tt("dve", s5c[:, :], s5c[:, :], s5e[:, :], ALU.mult, ["s5c", "s5e"], ["s5c"])
    tt("dve", s5e[:, :], lre[:, :], lre[:, :], ALU.mult, ["lre"], ["s5e"])
    tt("dve", s5f[:, :], lim[:, :], lim[:, :], ALU.mult, ["lim"], ["s5f"])
    tt("dve", s5e[:, :], s5e[:, :], s5f[:, :], ALU.add, ["s5e", "s5f"], ["s5e"])
    S.add("dve", lambda e: e.reciprocal(out=s5e[:, :], in_=s5e[:, :]), ["s5e"], ["s5e"])
    ts("dve", s5d[:, :], s5d[:, :], -1.0, ALU.add, ["s5d"], ["s5d"])
    tt("dve", s5f[:, :], s5d[:, :], lre[:, :], ALU.mult, ["s5d", "lre"], ["s5f"])
    tt("dve", s5g[:, :], s5c[:, :], lim[:, :], ALU.mult, ["s5c", "lim"], ["s5g"])
    tt("dve", s5f[:, :], s5f[:, :], s5g[:, :], ALU.add, ["s5f", "s5g"], ["s5f"])
    tt("dve", s5f[:, :], s5f[:, :], s5e[:, :], ALU.mult, ["s5f", "s5e"], ["s5f"])
    tt("dve", s5g[:, :], s5c[:, :], lre[:, :], ALU.mult, ["s5c", "lre"], ["s5g"])
    tt("dve", s5h[:, :], s5d[:, :], lim[:, :], ALU.mult, ["s5d", "lim"], ["s5h"])
    tt("dve", s5g[:, :], s5g[:, :], s5h[:, :], ALU.subtract, ["s5g", "s5h"], ["s5g"])
    tt("dve", s5g[:, :], s5g[:, :], s5e[:, :], ALU.mult, ["s5g", "s5e"], ["s5g"])
    fre_b = bc_last(s5f[:, 0:32], 16)
    for hfp in range(2):
        psl = slice(hfp * 32, (hfp + 1) * 32)
        bre, bim, bt1, bt2 = tA[0:32, :], tB[0:32, :], tC[0:32, :], tDd[0:32, :]
        dma("sp", bre, b_re_d[:, hfp * 512:(hfp + 1) * 512], [], ["tA"])
        dma("pool", bim, b_im_d[:, hfp * 512:(hfp + 1) * 512], [], ["tB"])
        v3 = lambda t: t.rearrange("g (p h) -> g p h", h=16)
        fre_b = bc_last(s5f[:, psl], 16); fim_b = bc_last(s5g[:, psl], 16)
        tt("dve", v3(bt1), v3(bre), fre_b, ALU.mult, ["tA", "s5f"], ["tC"])
        tt("dve", v3(bt2), v3(bim), fim_b, ALU.mult, ["tB", "s5g"], ["tD"])
        tt("dve", bt1, bt1, bt2, ALU.subtract, ["tC", "tD"], ["tC"])
        dma("sp", s_bb_d[0, :, psl, :].rearrange("g p h -> g (p h)"), bt1, ["tC"], [("s_bb", 0)])
        tt("dve", v3(bt2), v3(bim), fre_b, ALU.mult, ["tB", "s5f"], ["tD"])
        tt("dve", v3(bre), v3(bre), fim_b, ALU.mult, ["tA", "s5g"], ["tA"])
        tt("dve", bt2, bt2, bre, ALU.add, ["tD", "tA"], ["tD"])
        dma("sp", s_bb_d[1, :, psl, :].rearrange("g p h -> g (p h)"), bt2, ["tD"], [("s_bb", 1)])
    zt = stage_next()
    memset("dve", stage[zt][:, :], 0.0, [("stage", zt)])
    for r4 in range(4):
        dma("sp", s_bm_d[r4 * 128:(r4 + 1) * 128, :], stage[zt][:, :], [("stage", zt)], ["s_bm"])
    for g in range(32):
        gl = g % 8
        for c in range(2):
            dma("sp" if c == 0 else "pool",
                s_bm_d[g * 16:(g + 1) * 16, gl * 128 + c * 64: gl * 128 + c * 64 + 64],
                s_bb_d[c, g].rearrange("p h -> h p"), [("s_bb", c)], ["s_bm"], slow=True)
    for ft in range(4):
        load_cast(Bm_bd[:, ft, :], s_bm_d[ft * 128:(ft + 1) * 128, :], "Bm_bd", rkeys=["s_bm"])
    cld = sbt("cld", [128, 64]); cT = sbt("cT", [64, 512])
    for c, cd in enumerate((c_re_d, c_im_d)):
        for q4 in range(4):
            dma("sp", cld[:, :], cd[q4 * 128:(q4 + 1) * 128, :], [], ["cld"])
            pt, pk = ps_alloc()
            S.add("pe", lambda e, pt=pt: e.transpose(pt[0:64, 0:128], cld[:, :], ident_f[:, :]), ["cld", "ident_f"], [pk])
            cp("act", cT[:, q4 * 128:(q4 + 1) * 128], pt[0:64, 0:128], [pk], ["cT"])
        dma("sp", s_ct_d[c], cT[:, :], ["cT"], [("s_ct", c)])
    cl2 = sbt("cl2", [128, 16, 16])
    memset("pool", Cre_pad[:, :, :], 0.0, ["Cre_pad"])
    memset("pool", Cimn_pad[:, :, :], 0.0, ["Cimn_pad"])
    for c, (dst, key) in enumerate(((Cre_pad, "Cre_pad"), (Cimn_pad, "Cimn_pad"))):
        for g2 in range(2):
            src = s_ct_d[c].rearrange("p (gp g2 h) -> p gp g2 h", g2=2, h=16)[:, :, g2, :]
            dma("sp", cl2[g2 * 64:(g2 + 1) * 64, :, :], src, [("s_ct", c)], ["cl2"])
        for g2 in range(2):
            for q in range(4):
                dstv = dst[g2 * 64:(g2 + 1) * 64, :, :].rearrange("p (ft q) c -> p ft q c", q=4)[:, :, q, q * 32 + g2 * 16: q * 32 + g2 * 16 + 16]
                srcv = cl2[g2 * 64:(g2 + 1) * 64, :, :].rearrange("p (ft q) h -> p ft q h", q=4)[:, :, q, :]
                ts("dve", dstv, srcv, 1.0 if c == 0 else -1.0, ALU.mult, ["cl2"], [key])
    lrd_col = sbt("lrd_col", [128, 16]); trn_col = sbt("trn_col", [128, 16])
    dma("sp", lrd_col[:, :], s_tab_d[0, :].rearrange("(gp q) -> q gp", q=128), ["s_tab"], ["lrd_col"], slow=True)
    dma("sp", trn_col[:, :], s_tab_d[1, :].rearrange("(gp q) -> q gp", q=128), ["s_tab"], ["trn_col"], slow=True)
    nrow = sbt("nrow", [128, 128]); ncol = sbt("ncol", [128, 1])
    S.add("pool", lambda e: e.iota(nrow[:, :], pattern=[[1, 128]], base=-127, channel_multiplier=0,
                                   allow_small_or_imprecise_dtypes=True), [], ["nrow"])
    S.add("pool", lambda e: e.iota(ncol[:, :], pattern=[[0, 1]], base=127, channel_multiplier=-1,
                                   allow_small_or_imprecise_dtypes=True), [], ["ncol"])

    tok = ["tabtok"]
    tE = sbt("tE", [128, 512]); tF = sbt("tF", [128, 512])
    for q in range(4):
        gsl = slice(q * 4, q * 4 + 4)
        tt("dve", tE[:, :].rearrange("p (a b) -> p a b", b=128), bc_mid(nrow[:, :], 4), bc_last(trn_col[:, gsl], 128),
           ALU.mult, ["nrow", "trn_col"] + tok, ["tE"])
        tt("dve", tF[:, :].rearrange("p (a b) -> p a b", b=128), bc_mid(nrow[:, :], 4), bc_last(lrd_col[:, gsl], 128),
           ALU.mult, ["nrow", "lrd_col"] + tok, ["tF"])
        sincos(tE[:, :], tB[:, :], tC[:, :], (tDd[:, :], tA[:, :], tI[:, :]), ("tD", "tA", "tI"), ["tE"], "tB", "tC")
        act(tM[:, :], tF[:, :], AF.Exp, ["tF"], ["tM"])
        tt("dve", D_re[:, gsl, :].rearrange("p a b -> p (a b)"), tC[:, :], tM[:, :], ALU.mult, ["tC", "tM"], ["D_re"] + tok)
        tt("dve", D_im[:, gsl, :].rearrange("p a b -> p (a b)"), tB[:, :], tM[:, :], ALU.mult, ["tB", "tM"], ["D_im"] + tok)
    ts("dve", tE[:, 0:16], trn_col[:, :], 128.0, ALU.mult, ["trn_col"] + tok, ["tE"])
    ts("dve", tF[:, 0:16], lrd_col[:, :], 128.0, ALU.mult, ["lrd_col"] + tok, ["tF"])
    sincos(tE[:, 0:16], tB[:, 0:16], tC[:, 0:16], (tDd[:, 0:16], tA[:, 0:16], tI[:, 0:16]), ("tD", "tA", "tI"), ["tE"], "tB", "tC")
    act(tM[:, 0:16], tF[:, 0:16], AF.Exp, ["tF"], ["tM"])
    tt("dve", L_re[:, :], tC[:, 0:16], tM[:, 0:16], ALU.mult, ["tC", "tM"], ["L_re"] + tok)
    tt("dve", L_im[:, :], tB[:, 0:16], tM[:, 0:16], ALU.mult, ["tB", "tM"], ["L_im"] + tok)
    rowt = stage[0][:, 0:512]; rowl = stage[1][:, 0:512]
    for q in range(4):
        csl = slice(q * 512, (q + 1) * 512)
        dma("sp", rowt, s_tab_d[1, csl].partition_broadcast(128), ["s_tab"] + tok, [("stage", 0)])
        dma("pool", rowl, s_tab_d[0, csl].partition_broadcast(128), ["s_tab"] + tok, [("stage", 1)])
        ts("dve", tE[:, :], rowt, ncol[:, 0:1], ALU.mult, [("stage", 0), "ncol"] + tok, ["tE"])
        ts("dve", tF[:, :], rowl, ncol[:, 0:1], ALU.mult, [("stage", 1), "ncol"] + tok, ["tF"])
        sincos(tE[:, :], tB[:, :], tC[:, :], (tDd[:, :], tA[:, :], tI[:, :]), ("tD", "tA", "tI"), ["tE"], "tB", "tC")
        act(tM[:, :], tF[:, :], AF.Exp, ["tF"], ["tM"])
        tt("dve", E_re[:, csl], tC[:, :], tM[:, :], ALU.mult, ["tC", "tM"], ["E_re"] + tok)
        tt("dve", E_im[:, csl], tB[:, :], tM[:, :], ALU.mult, ["tB", "tM"], ["E_im"] + tok)
    S.emit(nc, es, drain=True)
    es_setup.close()
    S = Sched()
    memset("pool", xe_re[:, :], 0.0, ["xe_re"])
    memset("pool", xe_im[:, :], 0.0, ["xe_im"])

    NXB = 1
    xin = [sb(f"xin{i}", [128, D]) for i in range(NXB)]
    hT = sb("hT", [128, 8, 256], BF16)
    NWB = 2
    wblk = [sb(f"wblk{i}", [128, 8 * 128], BF16) for i in range(NWB)]
    zraw = [sb(f"zraw{i}", [128, 257]) for i in range(2)]
    zcar = sb("zcar", [128, 26])
    z_fm = sb("z_fm", [128, 30, 256], BF16)
    z_tm = sb("z_tm", [128, 2, 3584], BF16)
    zdiff = sb("zdiff", [128, 256])
    memset("pool", zcar[:, :], 0.0, ["zcar"])
    ssq = sb("ssq", [128, 4]);
    S_f = sb("S_f", [128, 8, 64]); S_bd = sb("S_bd", [128, 8, 128], BF16)
    memset("pool", S_f[:, :, :], 0.0, [("S_f", h) for h in range(8)])
    memset("pool", S_bd[:, :, :], 0.0, [("S_bd", h) for h in range(8)])
    NSET = 1
    def mkset(i):
        d = {}
        for nm in ["ld", "icl", "cl", "Wc", "Wi", "Wx", "kkn", "kh", "t1", "t2", "bt"]:
            d[nm] = sb(f"{nm}{i}", [128, 128])
        for nm in ["kk2", "a_p", "r_p", "rkp", "v_bf", "bh_p", "kh_p", "ApT"]:
            d[nm] = sb(f"{nm}{i}", [128, 128], BF16)
        for nm in ["b_bd", "k_bd"]:
            d[nm] = sb(f"{nm}{i}", [128, 2, 128], BF16)
        d["tm4"] = sb(f"tm4{i}", [128, 4, 128], BF16)
        for nm in ["AakT", "ArbT", "ArkT", "Pm", "Qm", "Zm", "Pn", "Qn", "Zn"]:
            d[nm] = sb(f"{nm}{i}", [128, 2, 128], BF16)
        d["W1"] = sb(f"W1{i}", [128, 2, 64], BF16)
        d["Vu"] = sb(f"Vu{i}", [128, 2, 64])
        d["U"] = sb(f"U{i}", [128, 128], BF16)
        d["wl"] = sb(f"wl{i}", [128, 1])
        d["i"] = i
        return d
    sets = [mkset(i) for i in range(NSET)]
    for st_ in sets:
        memset("pool", st_["b_bd"][:, :, :], 0.0, [("b_bd", st_["i"])])
        memset("pool", st_["k_bd"][:, :, :], 0.0, [("k_bd", st_["i"])])
    txw = sb("txw", [64, 128], BF16); xab = sb("xab", [64, 128], BF16)
    rk_sb = sb("rk_sb", [128, 16])
    Vall = sb("Vall", [128, 8, 128], BF16)
    ysb = sb("ysb", [128, D]); ytmp = sb("ytmp", [128, D]); ysq = ytmp
    gst = sb("gst", [128, 4, 16])
    ya_bf = sb("ya_bf", [128, D], BF16); yaT = sb("yaT", [128, 8, 128], BF16)
    xn_bf = ya_bf
    merged_a = sb("merged_a", [128, D], BF16)
    xres = [ytmp]
    u_bf = z_fm
    BUp_re = sb("BUp_re", [128, 2048], BF16); BUp_im = sb("BUp_im", [128, 2048], BF16)
    s5t = [sb(f"s5t{i}", [128, 512]) for i in range(4)]
    X_re = sb("X_re", [128, 16, 128], BF16); X_im = sb("X_im", [128, 16, 128], BF16)
    xq = [sb(f"xq{i}", [128, 4]) for i in range(6)]
    ys_f = s5t[2]; ys_t = s5t[3]; ys_s = s5t[0]
    gel_bf = sb("gel_bf", [128, 4, 128], BF16)
    g1 = s5t[0]; g2 = s5t[1]
    yb_bf = sb("yb_bf", [128, 512], BF16); ybT = sb("ybT", [128, 4, 128], BF16)
    merged_bf = sb("merged_bf", [128, D], BF16); mT = yaT
    junk_bf = merged_bf
    xo = ysb; outt = ytmp

    wb_rr = [0]
    out_keys = []
    for st in range(NST):
        for c in range(2):
            tok0 = st * 256 + c * 128
            xi = (st * 2 + c) % NXB
            dma("sp", xin[xi][:, :], x_d[tok0:tok0 + 128, :], [], [("xin", xi)])
            act(junk_bf[:, :], xin[xi][:, :], AF.Square, [("xin", xi)], ["merged_bf", "ssq0"], accum=ssq[:, 0:1])
            ts("dve", ssq[:, 1:2], ssq[:, 0:1], 1.0 / D, ALU.mult, ["ssq0"], ["ssq1"], s2=1e-6, op1=ALU.add)
            act(ssq[:, 2:3], ssq[:, 1:2], AF.Ln, ["ssq1"], ["ssq2"])
            act(ssq[:, 3:4], ssq[:, 2:3], AF.Exp, ["ssq2"], ["ssq3"], scale=-0.5)
            act(xn_bf[:, :], xin[xi][:, :], AF.Identity, [("xin", xi), "ssq3"], ["ya_bf"], scale=ssq[:, 3:4])
            pt, pk = ps_alloc()
            ptb = pt[:, :].bitcast(BF16)
            for kc in range(8):
                S.add("pe", lambda e, ptb=ptb, kc=kc: e.transpose(ptb[:, kc * 128:(kc + 1) * 128],
                                                                   xn_bf[:, kc * 128:(kc + 1) * 128], ident[:, :]),
                      ["ya_bf", "ident"], [pk])
            cp("dve", hT[:, :, c * 128:(c + 1) * 128], ptb.rearrange("p (k t) -> p k t", t=128), [pk], ["hT"])
        for ct in range(NCT):
            cs, cn = ct_cols[ct]
            wi = wb_rr[0] % NWB
            wb_rr[0] += 1
            wv = wblk[wi][:, :].rearrange("p (k c) -> p k c", c=128)
            dma("sp" if ct % 2 == 0 else "pool", wv[:, :, 0:cn], wbf_d[ct].rearrange("p (k c) -> p k c", c=128)[:, :, 0:cn],
                [("wbf", ct)], [("wblk", wi)])
            fm = ct < 26 or (30 <= ct + 0 and False)
            if ct < 26:
                kind, zi = "zs", ct
            elif ct < 34:
                kind, zi = "ga", ct - 26
            elif ct < 38:
                kind, zi = "u", ct - 34
            elif ct < 42:
                kind, zi = "gb", ct - 38
            elif ct < 50:
                kind, zi = "ma", ct - 42
            else:
                kind, zi = "mb", ct - 50
            if kind in ("zs", "u"):
                pt, pk = ps_alloc()
                for kc in range(8):
                    mm(pt[0:cn, 0:256], wv[:, kc, 0:cn], hT[:, kc, :], kc == 0, kc == 7, [("wblk", wi), "hT"], [pk])
                if kind == "u":
                    cp("act", z_fm[:, 26 + zi, :], pt[:, 0:256], [pk], [("z_fm", 26 + zi)])
                else:
                    zr = zraw[ct % 2]
                    zk = ("zraw", ct % 2)
                    cp("act", zr[0:cn, 1:257], pt[0:cn, 0:256], [pk], [zk])
                    cp("pool", zr[0:cn, 0:1], zcar[0:cn, ct:ct + 1], ["zcar"], [zk])
                    cp("pool", zcar[0:cn, ct:ct + 1], zr[0:cn, 256:257], [zk], ["zcar"])
                    tt("dve", zdiff[0:cn, :], zr[0:cn, 0:256], zr[0:cn, 1:257], ALU.subtract, [zk], ["zdiff"])
                    stt(z_fm[0:cn, zi, :], zdiff[0:cn, :], mu_col[0:cn, ct:ct + 1], zr[0:cn, 1:257], ALU.mult, ALU.add,
                        ["zdiff", zk, "mu_col"], [("z_fm", zi)])
            else:
                off = {"ga": 0, "gb": 1024, "ma": 1536, "mb": 2560}[kind] + zi * 128
                func = AF.Silu if kind in ("ga", "gb") else AF.Sigmoid
                pt, pk = ps_alloc()
                for c in range(2):
                    for kc in range(8):
                        mm(pt[:, c * 128:(c + 1) * 128], hT[:, kc, c * 128:(c + 1) * 128], wv[:, kc, :], kc == 0, kc == 7,
                           [("wblk", wi), "hT"], [pk])
                act(z_tm[:, :, off:off + 128], pt[:, 0:256].rearrange("p (c n) -> p c n", n=128), func, [pk],
                    [("z_tm", kind)])
        for c in range(2):
            tsl = slice(c * 128, (c + 1) * 128)
            tok0 = st * 256 + c * 128
            zr_all = [("z_fm", i) for i in range(30)]
            act(txw[:, :], z_fm[0:64, 24, tsl], AF.Tanh, [("z_fm", 24)], ["txw"])
            cp("pool", xab[:, :], z_fm[0:64, 25, tsl], [("z_fm", 25)], ["xab"])
            rk_pt, rk_pk = psb[2], ("ps", 2)
            y_pt = [(psb[0], ("ps", 0)), (psb[1], ("ps", 1))]
            for hp in range(8):
                W = sets[hp % NSET]
                si = W["i"]
                K = lambda nm: (nm, si)
                r_ap = z_fm[:, hp, tsl]; k_ap = z_fm[:, 8 + hp, tsl]; v_ap = z_fm[:, 16 + hp, tsl]
                fsl = slice(hp * 128, (hp + 1) * 128)
                pq, pqk = ps_alloc()
                mm(pq[:, 0:128], wup_bf[:, fsl], txw[:, :], True, True, ["wup_bf", "txw"], [pqk])
                mm(pq[:, 128:256], aup_bf[:, fsl], xab[:, :], True, True, ["aup_bf", "xab"], [pqk])
                act(W["ld"][:, :], pq[:, 0:128], AF.Sigmoid, [pqk, "w0c"], [K("ld")], bias=w0c[:, hp:hp + 1])
                act(W["icl"][:, :], pq[:, 128:256], AF.Sigmoid, [pqk, "a0c"], [K("icl")], bias=a0c[:, hp:hp + 1])
                S.add("dve", lambda e, W=W: e.tensor_tensor_scan(out=W["cl"][:, :], data0=ones_f[:, :], data1=W["ld"][:, :],
                                                                 initial=0.0, op0=ALU.mult, op1=ALU.add),
                      [K("ld"), "ones_f"], [K("cl")])
                tt("pool", W["t1"][:, :], W["cl"][:, :], W["ld"][:, :], ALU.subtract, [K("cl"), K("ld")], [K("t1")])
                act(W["Wc"][:, :], W["cl"][:, :], AF.Exp, [K("cl")], [K("Wc")], scale=-C0)
                act(W["Wi"][:, :], W["cl"][:, :], AF.Exp, [K("cl")], [K("Wi")], scale=C0)
                act(W["Wx"][:, :], W["t1"][:, :], AF.Exp, [K("t1")], [K("Wx")], scale=-C0)
                cp("pool", W["wl"][:, :], W["Wc"][:, 127:128], [K("Wc")], [K("wl")])
                act(W["kk2"][:, :], k_ap, AF.Square, [("z_fm", 8 + hp), "kkc"], [K("kk2")], scale=kkc[:, hp:hp + 1])
                pn, pnk = ps_alloc()
                mm(pn[:, 0:128], blk1[:, :], W["kk2"][:, :], True, True, ["blk1", K("kk2")], [pnk])
                ts("dve", W["t2"][:, :], pn[:, 0:128], 1e-24, ALU.max, [pnk], [K("t2")])
                act(W["t2"][:, :], W["t2"][:, :], AF.Ln, [K("t2")], [K("t2")])
                act(W["t2"][:, :], W["t2"][:, :], AF.Exp, [K("t2")], [K("t2")], scale=-0.5)
                stt(W["kkn"][:, :], k_ap, kkc[:, hp:hp + 1], W["t2"][:, :], ALU.mult, ALU.mult,
                    [("z_fm", 8 + hp), "kkc", K("t2")], [K("kkn")])
                ts("pool", W["t1"][:, :], W["icl"][:, :], -1.0, ALU.add, [K("icl"), "kac"], [K("t1")],
                   s2=kac[:, hp:hp + 1], op1=ALU.mult)
                stt(W["kh"][:, :], W["t1"][:, :], 1.0, k_ap, ALU.add, ALU.mult, [K("t1"), ("z_fm", 8 + hp)], [K("kh")])
                stt(W["a_p"][:, :], W["kkn"][:, :], -1.0, W["Wx"][:, :], ALU.mult, ALU.mult, [K("kkn"), K("Wx")], [K("a_p")])
                tt("pool", W["bt"][:, :], W["kkn"][:, :], W["icl"][:, :], ALU.mult, [K("kkn"), K("icl")], [K("bt")])
                tt("dve", W["bt"][:, :], W["bt"][:, :], W["Wi"][:, :], ALU.mult, [K("bt"), K("Wi")], [K("bt")])
                tt("pool", W["t2"][:, :], W["kh"][:, :], W["Wi"][:, :], ALU.mult, [K("kh"), K("Wi")], [K("t2")])
                for h2 in range(2):
                    hs = slice(h2 * 64, (h2 + 1) * 64)
                    cp("act", W["b_bd"][hs, h2, :], W["bt"][hs, :], [K("bt")], [("b_bd", si)])
                    cp("act", W["k_bd"][hs, h2, :], W["t2"][hs, :], [K("t2")], [("k_bd", si)])
                ts("dve", W["bh_p"][:, :], W["bt"][:, :], W["wl"][:, 0:1], ALU.mult, [K("bt"), K("wl")], [K("bh_p")])
                ts("pool", W["kh_p"][:, :], W["t2"][:, :], W["wl"][:, 0:1], ALU.mult, [K("t2"), K("wl")], [K("kh_p")])
                tt("dve", W["r_p"][:, :], r_ap, W["Wc"][:, :], ALU.mult, [("z_fm", hp), K("Wc")], [K("r_p")])
                stt(W["rkp"][:, :], r_ap, rkc[:, hp:hp + 1], W["kh"][:, :], ALU.mult, ALU.mult,
                    [("z_fm", hp), "rkc", K("kh")], [K("rkp")])
                mm(rk_pt[:, hp * 2:hp * 2 + 2], W["rkp"][:, :], hsel[:, :], True, True, [K("rkp"), "hsel"], [rk_pk])
                ptm, ptk = ps_alloc()
                ptmb = ptm[:, :].bitcast(BF16)
                for j, (src, key) in enumerate(((W["a_p"][:, :], K("a_p")), (W["bh_p"][:, :], K("bh_p")),
                                                (W["kh_p"][:, :], K("kh_p")), (v_ap, ("z_fm", 16 + hp)))):
                    S.add("pe", lambda e, ptmb=ptmb, j=j, src=src: e.transpose(ptmb[:, j * 128:(j + 1) * 128], src, ident[:, :]),
                          [key, "ident"], [ptk])
                cp("act", W["tm4"][:, :, :].rearrange("p a b -> p (a b)"), ptmb[:, 0:512], [ptk], [K("tm4")])
                cp("pool", Vall[:, hp, :], W["tm4"][:, 3, :], [K("tm4")], [("Vall", hp)])
                pa1, pa1k = ps_alloc()
                pa2, pa2k = ps_alloc()
                for h2 in range(2):
                    mm(pa1[:, h2 * 128:(h2 + 1) * 128], W["b_bd"][:, h2, :], W["a_p"][:, :], True, True, [("b_bd", si), K("a_p")], [pa1k])
                    mm(pa1[:, 256 + h2 * 128:256 + (h2 + 1) * 128], W["k_bd"][:, h2, :], W["a_p"][:, :], True, True, [("k_bd", si), K("a_p")], [pa1k])
                    mm(pa2[:, h2 * 128:(h2 + 1) * 128], W["b_bd"][:, h2, :], W["r_p"][:, :], True, True, [("b_bd", si), K("r_p")], [pa2k])
                    mm(pa2[:, 256 + h2 * 128:256 + (h2 + 1) * 128], W["k_bd"][:, h2, :], W["r_p"][:, :], True, True, [("k_bd", si), K("r_p")], [pa2k])
                msu_b = bc_mid(m_su[:, :], 2); miu_b = bc_mid(m_iu[:, :], 2)
                v2 = lambda ap: ap.rearrange("p (a b) -> p a b", b=128)
                tt("dve", W["Pm"][:, :, :], v2(pa1[:, 0:256]), msu_b, ALU.mult, [pa1k, "m_su"], [K("Pm")])
                tt("dve", W["AakT"][:, :, :], v2(pa1[:, 256:512]), msu_b, ALU.mult, [pa1k, "m_su"], [K("AakT")])
                tt("dve", W["ArbT"][:, :, :], v2(pa2[:, 0:256]), miu_b, ALU.mult, [pa2k, "m_iu"], [K("ArbT")])
                tt("dve", W["ArkT"][:, :, :], v2(pa2[:, 256:512]), miu_b, ALU.mult, [pa2k, "m_iu"], [K("ArkT")])
                pq0, pq0k = ps_alloc()
                pq0b = pq0[:, :].bitcast(BF16)
                for h2 in range(2):
                    S.add("pe", lambda e, pq0b=pq0b, h2=h2, W=W: e.transpose(pq0b[:, h2 * 128:(h2 + 1) * 128], W["Pm"][:, h2, :], ident[:, :]),
                          [K("Pm"), "ident"], [pq0k])
                cp("act", W["Qm"][:, :, :].rearrange("p a b -> p (a b)"), pq0b[:, 0:256], [pq0k], [K("Qm")])
                tt("pool", W["Zm"][:, :, :], W["Pm"][:, :, :], bc_mid(ident[:, :], 2), ALU.add, [K("Pm"), "ident"], [K("Zm")])
                cur = ("Pm", "Qm", "Zm"); nxt = ("Pn", "Qn", "Zn")
                for lvl in range(1, 7):
                    pl, plk = ps_alloc()
                    last = lvl == 6
                    for h2 in range(2):
                        if not last:
                            mm(pl[:, h2 * 128:(h2 + 1) * 128], W[cur[1]][:, h2, :], W[cur[0]][:, h2, :], True, True,
                               [K(cur[0]), K(cur[1])], [plk])
                        mm(pl[:, 256 + h2 * 128:256 + (h2 + 1) * 128], W[cur[0]][:, h2, :], W[cur[1]][:, h2, :], True, True,
                           [K(cur[0]), K(cur[1])], [plk])
                    if not last:
                        cp("act", W[nxt[0]][:, :, :].rearrange("p a b -> p (a b)"), pl[:, 0:256], [plk], [K(nxt[0])])
                    cp("act", W[nxt[1]][:, :, :].rearrange("p a b -> p (a b)"), pl[:, 256:512], [plk], [K(nxt[1])])
                    pz, pzk = ps_alloc()
                    for h2 in range(2):
                        mm(pz[:, h2 * 128:(h2 + 1) * 128], W[nxt[1]][:, h2, :], W[cur[2]][:, h2, :], True, True,
                           [K(nxt[1]), K(cur[2])], [pzk])
                    tt("dve", W[nxt[2]][:, :, :], v2(pz[:, 0:256]), W[cur[2]][:, :, :], ALU.add, [pzk, K(cur[2])], [K(nxt[2])])
                    cur, nxt = nxt, cur
                Tt = W[cur[2]]; Tk = K(cur[2])
                pap, papk = ps_alloc()
                for h2 in range(2):
                    mm(pap[:, h2 * 128:(h2 + 1) * 128], W["tm4"][:, 0, :], Tt[:, h2, :], True, True, [K("tm4"), Tk], [papk])
                for h2 in range(2):
                    hs = slice(h2 * 64, (h2 + 1) * 64)
                    cp("act", W["ApT"][hs, :], pap[hs, h2 * 128:(h2 + 1) * 128], [papk], [K("ApT")])
                pw, pwk = ps_alloc()
                for h2 in range(2):
                    mm(pw[:, h2 * 64:(h2 + 1) * 64], W["AakT"][:, h2, :], W["tm4"][:, 3, h2 * 64:(h2 + 1) * 64], True, True,
                       [K("AakT"), K("tm4")], [pwk])
                cp("act", W["W1"][:, :, :].rearrange("p a b -> p (a b)"), pw[:, 0:128], [pwk], [K("W1")])
                pv, pvk = ps_alloc()
                for h2 in range(2):
                    mm(pv[:, h2 * 64:(h2 + 1) * 64], Tt[:, h2, :], W["W1"][:, h2, :], True, True, [Tk, K("W1")], [pvk])
                cp("act", W["Vu"][:, :, :].rearrange("p a b -> p (a b)"), pv[:, 0:128], [pvk], [K("Vu")])
                p1, p1k = ps_alloc()
                mm(p1[:, 0:128], W["ApT"][:, :], S_bd[:, hp, :], True, True, [K("ApT"), ("S_bd", hp)], [p1k])
                tt("dve", W["U"][:, :], p1[:, 0:128], W["Vu"][:, :, :].rearrange("p a b -> p (a b)"), ALU.add, [p1k, K("Vu")], [K("U")])
                ypt, ypk = y_pt[hp // 4]
                for h2 in range(2):
                    col = (hp % 4) * 128 + h2 * 64
                    hsl = slice(h2 * 64, (h2 + 1) * 64)
                    mm(ypt[:, col:col + 64], W["r_p"][:, :], S_bd[:, hp, hsl], True, False, [K("r_p"), ("S_bd", hp)], [ypk])
                    mm(ypt[:, col:col + 64], W["ArbT"][:, h2, :], W["U"][:, hsl], False, False, [K("ArbT"), K("U")], [ypk])
                    mm(ypt[:, col:col + 64], W["ArkT"][:, h2, :], W["tm4"][:, 3, hsl], False, True, [K("ArkT"), K("tm4")], [ypk])
                p2, p2k = ps_alloc()
                mm(p2[:, 0:128], W["tm4"][:, 1, :], W["U"][:, :], True, False, [K("tm4"), K("U")], [p2k])
                mm(p2[:, 0:128], W["tm4"][:, 2, :], W["tm4"][:, 3, :], False, True, [K("tm4")], [p2k])
                for h2 in range(2):
                    hs = slice(h2 * 64, (h2 + 1) * 64)
                    stt(S_f[hs, hp, :], S_f[hs, hp, :], W["wl"][hs, 0:1], p2[hs, h2 * 64:(h2 + 1) * 64], ALU.mult, ALU.add,
                        [("S_f", hp), K("wl"), p2k], [("S_f", hp)])
                    cp("act", S_bd[hs, hp, h2 * 64:(h2 + 1) * 64], S_f[hs, hp, :], [("S_f", hp)], [("S_bd", hp)])
            ykeys = [y_pt[0][1], y_pt[1][1]]
            cp("act", ysb[:, 0:512], y_pt[0][0][:, :], [ykeys[0]], ["ysb"])
            cp("act", ysb[:, 512:1024], y_pt[1][0][:, :], [ykeys[1]], ["ysb"])
            cp("act", rk_sb[:, :], rk_pt[:, 0:16], [rk_pk], ["rk_sb"])
            y3 = ysb[:, :].rearrange("p (h i) -> p h i", i=64)
            S.add("dve", lambda e, y3=y3: e.tensor_reduce(out=gst[:, 0, :], in_=y3, axis=AX.X, op=ALU.add), ["ysb"], ["gst0"])
            act(ysq[:, :], ysb[:, :], AF.Square, ["ysb"], ["ytmp"])
            S.add("dve", lambda e: e.tensor_reduce(out=gst[:, 1, :], in_=ysq[:, :].rearrange("p (h i) -> p h i", i=64), axis=AX.X, op=ALU.add),
                  ["ytmp"], ["gst1"])
            ts("dve", gst[:, 0, :], gst[:, 0, :], 1.0 / 64, ALU.mult, ["gst0"], ["gst0"])
            tt("dve", gst[:, 2, :], gst[:, 0, :], gst[:, 0, :], ALU.mult, ["gst0"], ["gst2"])
            stt(gst[:, 1, :], gst[:, 1, :], 1.0 / 64, gst[:, 2, :], ALU.mult, ALU.subtract, ["gst1", "gst2"], ["gst1"])
            ts("dve", gst[:, 1, :], gst[:, 1, :], 64e-5, ALU.add, ["gst1"], ["gst1"])
            act(gst[:, 1, :], gst[:, 1, :], AF.Ln, ["gst1"], ["gst1"])
            act(gst[:, 1, :], gst[:, 1, :], AF.Exp, ["gst1"], ["gst1"], scale=-0.5)
            tt("dve", y3, y3, bc_last(gst[:, 0, :], 64), ALU.subtract, ["ysb", "gst0"], ["ysb"])
            tt("dve", y3, y3, bc_last(gst[:, 1, :], 64), ALU.mult, ["ysb", "gst1"], ["ysb"])
            tt("pool", ysb[:, :], ysb[:, :], lnxg_row[:, :], ALU.mult, ["ysb", "lnxg_row"], ["ysb"])
            tt("pool", ysb[:, :], ysb[:, :], lnxb_row[:, :], ALU.add, ["ysb", "lnxb_row"], ["ysb"])
            tt("dve", ytmp[:, :].rearrange("p (h i) -> p h i", i=64), Vall[:, :, :].rearrange("p a (b i) -> p (a b) i", i=64),
               bc_last(rk_sb[:, :], 64), ALU.mult, [("Vall", h) for h in range(8)] + ["rk_sb"], ["ytmp"])
            tt("dve", ysb[:, :], ysb[:, :], ytmp[:, :], ALU.add, ["ysb", "ytmp"], ["ysb"])
            tt("dve", ya_bf[:, :], ysb[:, :], z_tm[:, c, 0:1024], ALU.mult, ["ysb", ("z_tm", "ga")], ["ya_bf"])
            pt, pk = ps_alloc()
            ptb = pt[:, :].bitcast(BF16)
            for kc in range(8):
                S.add("pe", lambda e, ptb=ptb, kc=kc: e.transpose(ptb[:, kc * 128:(kc + 1) * 128], ya_bf[:, kc * 128:(kc + 1) * 128], ident[:, :]),
                      ["ya_bf", "ident"], [pk])
            cp("act", yaT[:, :, :].rearrange("p a b -> p (a b)"), ptb[:, :], [pk], ["yaT"])
            for half in range(2):
                po, pok = ps_alloc()
                for kc in range(8):
                    mm(po[:, :], yaT[:, kc, :], pa_bf[:, kc, half * 512:(half + 1) * 512], kc == 0, kc == 7, ["yaT", "pa_bf"], [pok])
                tt("dve", merged_a[:, half * 512:(half + 1) * 512], po[:, :], z_tm[:, c, 1536 + half * 512:1536 + (half + 1) * 512],
                   ALU.mult, [pok, ("z_tm", "ma")], ["merged_a"])
            for ft in range(4):
                pbu = [ps_alloc(), ps_alloc()]
                for hf in range(2):
                    mm(pbu[hf][0][:, :], z_fm[:, 26 + ft, tsl], Bm_bd[:, ft, hf * 512:(hf + 1) * 512], True, True,
                       [("z_fm", 26 + ft), "Bm_bd"], [pbu[hf][1]])
                for hf in range(2):
                    pv4 = pbu[hf][0][:, :].rearrange("p (g c q) -> p g c q", c=2, q=64)
                    bre_v = pv4[:, :, 0, :]; bim_v = pv4[:, :, 1, :]
                    gbase = ft * 8 + hf * 4
                    esl = slice(gbase * 64, (gbase + 4) * 64)
                    e_re = E_re[:, esl].rearrange("p (g q) -> p g q", q=64); e_im = E_im[:, esl].rearrange("p (g q) -> p g q", q=64)
                    tv = [s5t[i][:, 0:256].rearrange("p (g q) -> p g q", q=64) for i in range(4)]
                    pk_ = pbu[hf][1]
                    tt("dve", tv[0], bre_v, e_re, ALU.mult, [pk_, "E_re"], ["s5t0"])
                    tt("dve", tv[1], bim_v, e_im, ALU.mult, [pk_, "E_im"], ["s5t1"])
                    tt("dve", tv[2], bre_v, e_im, ALU.mult, [pk_, "E_im"], ["s5t2"])
                    tt("dve", tv[3], bim_v, e_re, ALU.mult, [pk_, "E_re"], ["s5t3"])
                    tt("pool", BUp_re[:, esl], s5t[0][:, 0:256], s5t[1][:, 0:256], ALU.subtract, ["s5t0", "s5t1"], ["BUp_re"])
                    tt("pool", BUp_im[:, esl], s5t[2][:, 0:256], s5t[3][:, 0:256], ALU.add, ["s5t2", "s5t3"], ["BUp_im"])
            for q in range(4):
                pxr, pxrk = ps_alloc(); pxi, pxik = ps_alloc()
                for g4 in range(4):
                    gp = q * 4 + g4
                    mm(pxr[:, g4 * 128:(g4 + 1) * 128], BUp_re[:, gp * 128:(gp + 1) * 128], m_iu[:, :], True, True, ["BUp_re", "m_iu"], [pxrk])
                    mm(pxi[:, g4 * 128:(g4 + 1) * 128], BUp_im[:, gp * 128:(gp + 1) * 128], m_iu[:, :], True, True, ["BUp_im", "m_iu"], [pxik])
                gsl = slice(q * 4, q * 4 + 4)
                tt("pool", xq[0][:, :], L_re[:, gsl], xe_re[:, gsl], ALU.mult, ["L_re", "xe_re"], ["xq0"])
                tt("pool", xq[1][:, :], L_im[:, gsl], xe_im[:, gsl], ALU.mult, ["L_im", "xe_im"], ["xq1"])
                tt("pool", xq[2][:, :], L_re[:, gsl], xe_im[:, gsl], ALU.mult, ["L_re", "xe_im"], ["xq2"])
                tt("pool", xq[3][:, :], L_im[:, gsl], xe_re[:, gsl], ALU.mult, ["L_im", "xe_re"], ["xq3"])
                tt("pool", xq[4][:, :], xq[0][:, :], xq[1][:, :], ALU.subtract, ["xq0", "xq1"], ["xq4"])
                tt("pool", xq[5][:, :], xq[2][:, :], xq[3][:, :], ALU.add, ["xq2", "xq3"], ["xq5"])
                for g4 in range(4):
                    gp = q * 4 + g4
                    cs = slice(g4 * 128, (g4 + 1) * 128)
                    stt(s5t[0][:, cs], pxr[:, cs], xq[4][:, g4:g4 + 1], D_re[:, gp, :], ALU.add, ALU.mult, [pxrk, "xq4", "D_re"], ["s5t0"])
                    stt(s5t[1][:, cs], pxi[:, cs], xq[5][:, g4:g4 + 1], D_im[:, gp, :], ALU.add, ALU.mult, [pxik, "xq5", "D_im"], ["s5t1"])
                    stt(s5t[2][:, cs], pxr[:, cs], xq[4][:, g4:g4 + 1], D_im[:, gp, :], ALU.add, ALU.mult, [pxrk, "xq4", "D_im"], ["s5t2"])
                    stt(s5t[3][:, cs], pxi[:, cs], xq[5][:, g4:g4 + 1], D_re[:, gp, :], ALU.add, ALU.mult, [pxik, "xq5", "D_re"], ["s5t3"])
                tt("pool", X_re[:, gsl, :].rearrange("p a b -> p (a b)"), s5t[0][:, :], s5t[1][:, :], ALU.subtract, ["s5t0", "s5t1"], ["X_re"])
                tt("pool", X_im[:, gsl, :].rearrange("p a b -> p (a b)"), s5t[2][:, :], s5t[3][:, :], ALU.add, ["s5t2", "s5t3"], ["X_im"])
                last_r = pxr[:, :].rearrange("p (g t) -> p g t", t=128)[:, :, 127]
                last_i = pxi[:, :].rearrange("p (g t) -> p g t", t=128)[:, :, 127]
                tt("dve", xe_re[:, gsl], last_r, xq[4][:, :], ALU.add, [pxrk, "xq4"], ["xe_re"])
                tt("dve", xe_im[:, gsl], last_i, xq[5][:, :], ALU.add, [pxik, "xq5"], ["xe_im"])
            pys, pysk = ps_alloc()
            for ft in range(4):
                for g4 in range(4):
                    gp = ft * 4 + g4
                    mm(pys[:, ft * 128:(ft + 1) * 128], Cre_pad[:, gp, :], X_re[:, gp, :], g4 == 0, False, ["Cre_pad", "X_re"], [pysk])
                    mm(pys[:, ft * 128:(ft + 1) * 128], Cimn_pad[:, gp, :], X_im[:, gp, :], False, g4 == 3, ["Cimn_pad", "X_im"], [pysk])
            for ft in range(4):
                stt(ys_f[:, ft * 128:(ft + 1) * 128], z_fm[:, 26 + ft, tsl], dsk[:, ft:ft + 1], pys[:, ft * 128:(ft + 1) * 128],
                    ALU.mult, ALU.add, [("z_fm", 26 + ft), "dsk", pysk], ["s5t2"])
            act(ys_t[:, :], ys_f[:, :], AF.Square, ["s5t2"], ["s5t3"])
            ts("pool", ys_t[:, :], ys_t[:, :], 0.044715, ALU.mult, ["s5t3"], ["s5t3"], s2=1.0, op1=ALU.add)
            tt("pool", ys_t[:, :], ys_t[:, :], ys_f[:, :], ALU.mult, ["s5t3", "s5t2"], ["s5t3"])
            act(ys_s[:, :], ys_t[:, :], AF.Sigmoid, ["s5t3"], ["s5t0"], scale=1.5957691216057308)
            tt("pool", gel_bf[:, :, :].rearrange("p a b -> p (a b)"), ys_s[:, :], ys_f[:, :], ALU.mult, ["s5t0", "s5t2"], ["gel_bf"])
            pg = [ps_alloc(), ps_alloc()]
            for hf in range(2):
                for ft in range(4):
                    mm(pg[hf][0][:, :], gel_bf[:, ft, :], wglu_bf[:, ft, hf * 512:(hf + 1) * 512], ft == 0, ft == 3, ["gel_bf", "wglu_bf"], [pg[hf][1]])
            tt("dve", g1[:, :], pg[0][0][:, :], bglu_row[:, 0:512], ALU.add, [pg[0][1], "bglu_row"], ["s5t0"])
            tt("dve", g2[:, :], pg[1][0][:, :], bglu_row[:, 512:1024], ALU.add, [pg[1][1], "bglu_row"], ["s5t1"])
            act(g2[:, :], g2[:, :], AF.Sigmoid, ["s5t1"], ["s5t1"])
            tt("pool", g1[:, :], g1[:, :], g2[:, :], ALU.mult, ["s5t0", "s5t1"], ["s5t0"])
            tt("pool", yb_bf[:, :], g1[:, :], z_tm[:, c, 1024:1536], ALU.mult, ["s5t0", ("z_tm", "gb")], ["yb_bf"])
            pt, pk = ps_alloc()
            ptb = pt[:, :].bitcast(BF16)
            for kc in range(4):
                S.add("pe", lambda e, ptb=ptb, kc=kc: e.transpose(ptb[:, kc * 128:(kc + 1) * 128], yb_bf[:, kc * 128:(kc + 1) * 128], ident[:, :]),
                      ["yb_bf", "ident"], [pk])
            cp("act", ybT[:, :, :].rearrange("p a b -> p (a b)"), ptb[:, 0:512], [pk], ["ybT"])
            for half in range(2):
                po, pok = ps_alloc()
                hsl = slice(half * 512, (half + 1) * 512)
                for kc in range(4):
                    mm(po[:, :], ybT[:, kc, :], pb_bf[:, kc, hsl], kc == 0, kc == 3, ["ybT", "pb_bf"], [pok])
                tt("dve", g1[:, :], po[:, :], z_tm[:, c, 2560 + half * 512:2560 + (half + 1) * 512], ALU.mult, [pok, ("z_tm", "mb")], ["s5t0"])
                tt("pool", merged_bf[:, hsl], g1[:, :], merged_a[:, hsl], ALU.add, ["s5t0", "merged_a"], ["merged_bf"])
            pt, pk = ps_alloc()
            ptb = pt[:, :].bitcast(BF16)
            for kc in range(8):
                S.add("pe", lambda e, ptb=ptb, kc=kc: e.transpose(ptb[:, kc * 128:(kc + 1) * 128], merged_bf[:, kc * 128:(kc + 1) * 128], ident[:, :]),
                      ["merged_bf", "ident"], [pk])
            cp("act", mT[:, :, :].rearrange("p a b -> p (a b)"), ptb[:, :], [pk], ["yaT"])
            xi = 0
            dma("pool", xres[xi][:, :], x_d[tok0:tok0 + 128, :], [], ["ytmp"])
            for half in range(2):
                po, pok = ps_alloc()
                hsl = slice(half * 512, (half + 1) * 512)
                for kc in range(8):
                    mm(po[:, :], mT[:, kc, :], wout_bf[:, kc, hsl], kc == 0, kc == 7, ["yaT", "wout_bf"], [pok])
                tt("dve", xo[:, hsl], po[:, :], xres[xi][:, hsl], ALU.add, [pok, "ytmp"], ["ysb"])
            act(junk_bf[:, :], xo[:, :], AF.Square, ["ysb"], ["merged_bf", "ssq0"], accum=ssq[:, 0:1])
            ts("dve", ssq[:, 1:2], ssq[:, 0:1], 1.0 / D, ALU.mult, ["ssq0"], ["ssq1"], s2=1e-6, op1=ALU.add)
            act(ssq[:, 2:3], ssq[:, 1:2], AF.Ln, ["ssq1"], ["ssq2"])
            act(ssq[:, 3:4], ssq[:, 2:3], AF.Exp, ["ssq2"], ["ssq3"], scale=-0.5)
            stt(outt[:, :], xo[:, :], ssq[:, 3:4], fg_row[:, :], ALU.mult, ALU.mult, ["ysb", "ssq3", "fg_row"], ["ytmp"])
            ok = ("out", st, c)
            dma("sp", out_d[tok0:tok0 + 128, :], outt[:, :], ["ytmp"], [ok])
            out_keys.append(ok)
    S.add("sp", None, out_keys, [])
    print("SBUF bytes/partition:", sb_bytes[0])
    S.emit(nc, es)
    es.close()
    return nc


_NC_CACHE = {}


def kernel(**inputs):
    T = 8192
    if T not in _NC_CACHE:
        _NC_CACHE[T] = build_program(T)
    nc = _NC_CACHE[T]
    f = lambda a, shp: np.ascontiguousarray(np.asarray(a, dtype=np.float32).reshape(shp))
    common = {
        "norm_g": f(inputs["norm_g"], (1, D)), "w_in": f(inputs["w_in"], (D, 7296)),
        "mu_shift": f(inputs["mu_shift"], (1, 3200)), "w0": f(inputs["w0"], (1, D)),
        "w_up": f(inputs["w_up"], (64, D)), "a0": f(inputs["a0"], (1, D)), "a_up": f(inputs["a_up"], (64, D)),
        "k_k": f(inputs["k_k"], (1, D)), "k_a": f(inputs["k_a"], (1, D)), "r_k": f(inputs["r_k"], (1, D)),
        "lnx_g": f(inputs["lnx_g"], (1, D)), "lnx_b": f(inputs["lnx_b"], (1, D)),
        "lam_re": f(inputs["lam_re"], (32, 64)), "lam_im": f(inputs["lam_im"], (32, 64)),
        "log_dt": f(inputs["log_dt"], (32, 1)),
        "b_re": f(inputs["b_re"], (32, 1024)), "b_im": f(inputs["b_im"], (32, 1024)),
        "c_re": f(inputs["c_re"], (512, 64)), "c_im": f(inputs["c_im"], (512, 64)),
        "d_skip": f(inputs["d_skip"], (1, 512)), "w_glu": f(inputs["w_glu"], (512, D)),
        "b_glu": f(inputs["b_glu"], (1, D)), "p_a": f(inputs["p_a"], (D, D)), "p_b": f(inputs["p_b"], (512, D)),
        "w_out": f(inputs["w_out"], (D, D)), "final_g": f(inputs["final_g"], (1, D)),
    }
    x = np.asarray(inputs["x"], dtype=np.float32)
    in_maps = []
    for c in range(8):
        m = dict(common)
        m["x"] = np.ascontiguousarray(x[c % 4])
        in_maps.append(m)
    res = run_bass_kernel_spmd(nc, in_maps, core_ids=list(range(8)))
    out = np.stack([np.asarray(res.results[b]["out"], dtype=np.float32) for b in range(4)], axis=0)
    return out
```

```python
import numpy as np
from contextlib import ExitStack
import concourse.bass as bass
import concourse.mybir as mybir
from concourse.bass_utils import run_bass_kernel_spmd

F32, BF16, I32 = mybir.dt.float32, mybir.dt.bfloat16, mybir.dt.int32
AF = mybir.ActivationFunctionType
ALU = mybir.AluOpType
AX = mybir.AxisListType

D = 1024
NCT = 58
C0 = 0.6065306597126334
TWO_PI = 6.283185307179586
SAME_ENG_SYNC = {'pe': False, 'act': True, 'dve': True, 'pool': True, 'sp': True}
MAXV = 30000
NSLOT = 6


class Sched:
    ENGS = ["pe", "act", "dve", "pool", "sp"]
    NINST = 0

    def __init__(self):
        self.ops = []
        self.last_w = {}
        self.readers = {}

    def add(self, eng, fn, r=(), w=(), dma=False):
        idx = len(self.ops)
        deps = set()
        for k in r:
            if k in self.last_w:
                deps.add(self.last_w[k])
        for k in w:
            if k in self.last_w:
                deps.add(self.last_w[k])
            deps.update(self.readers.get(k, ()))
        for k in r:
            self.readers.setdefault(k, []).append(idx)
        for k in w:
            self.last_w[k] = idx
            self.readers[k] = []
        deps.discard(idx)
        self.ops.append(dict(eng=eng, fn=fn, deps=sorted(deps), dma=dma, marked=False))
        return idx

    def emit(self, nc, es, drain=False):
        ops = self.ops
        for o in ops:
            for d in o["deps"]:
                od = ops[d]
                if od["dma"]:
                    od["marked"] = True
                elif od["eng"] != o["eng"]:
                    od["marked"] = True
                elif SAME_ENG_SYNC[o["eng"]]:
                    od["marked"] = True
                elif o["dma"]:
                    od["marked"] = True
        cnt = {e: 0 for e in self.ENGS}
        slot_cnt = {e: 0 for e in self.ENGS}
        slot_val = {}
        nsem_needed = {e: 0 for e in self.ENGS}
        for o in ops:
            e = o["eng"]
            if o["dma"]:
                s = slot_cnt[e] % NSLOT
                slot_cnt[e] += 1
                key = ("dma", e, s)
                prev = slot_val.get(key, 0)
                o["prev_wait"] = (key, prev) if prev > 0 else None
                slot_val[key] = prev + 16
                o["sig"] = (key, prev + 16)
            elif o["marked"]:
                cnt[e] += 1
                ep = (cnt[e] - 1) // MAXV
                o["sig"] = (("c", e, ep), (cnt[e] - 1) % MAXV + 1)
                nsem_needed[e] = max(nsem_needed[e], ep + 1)
        sems = {}
        for e in self.ENGS:
            for ep in range(nsem_needed[e]):
                sems[("c", e, ep)] = es.enter_context(nc.semaphore(f"s{Sched.NINST}_{e}_{ep}"))
            if slot_cnt[e] > 0:
                for s in range(min(NSLOT, slot_cnt[e])):
                    sems[("dma", e, s)] = es.enter_context(nc.semaphore(f"sd{Sched.NINST}_{e}_{s}"))
        Sched.NINST += 1
        block_cm = nc.Block()
        block = block_cm.__enter__()
        regs = {"pe": block.tensor, "act": block.scalar, "dve": block.vector,
                "pool": block.gpsimd, "sp": block.sync}
        streams = {e: [o for o in ops if o["eng"] == e] for e in self.ENGS}

        def make(e_name):
            def body(e):
                known = {}
                for o in streams[e_name]:
                    waits = {}
                    for d in o["deps"]:
                        od = ops[d]
                        if not od["marked"]:
                            continue
                        if od["eng"] == e_name and not od["dma"]:
                            if not SAME_ENG_SYNC[e_name] and not o["dma"]:
                                continue
                        k, v = od["sig"]
                        waits[k] = max(waits.get(k, 0), v)
                    if o["dma"] and o["prev_wait"] is not None:
                        k, v = o["prev_wait"]
                        waits[k] = max(waits.get(k, 0), v)
                    for k, v in waits.items():
                        if known.get(k, 0) >= v:
                            continue
                        e.wait_ge(sems[k], v)
                        known[k] = v
                    if o["fn"] is None:
                        continue
                    ins = o["fn"](e)
                    if o["dma"]:
                        k, v = o["sig"]
                        ins.then_inc(sems[k], 16)
                    elif o["marked"]:
                        k, v = o["sig"]
                        ins.then_inc(sems[k], 1)
                if drain and e_name == "sp":
                    for k, v in slot_val.items():
                        e.wait_ge(sems[k], v)
            return body
        for e_name in self.ENGS:
            if streams[e_name]:
                regs[e_name](make(e_name))
        block_cm.__exit__(None, None, None)


def s5_bounded(g, live, n):
    for _ in range(n):
        if not live[0]:
            return
        try:
            next(g)
        except StopIteration:
            live[0] = False
            return
        yield


def build_program(T, debug=False):
    assert T % 256 == 0
    NST = T // 256
    nc = bass.Bass("TRN2", target_bir_lowering=False)
    S = Sched()
    es = ExitStack()

    def din(name, shape, dt=F32):
        return nc.dram_tensor(name, list(shape), dt, kind="ExternalInput").ap()

    x_d = din("x", [T, D])
    norm_g_d = din("norm_g", [1, D])
    w_in_d = din("w_in", [D, 7296])
    mu_d = din("mu_shift", [1, 3200])
    w0_d = din("w0", [1, D]); w_up_d = din("w_up", [64, D])
    a0_d = din("a0", [1, D]); a_up_d = din("a_up", [64, D])
    k_k_d = din("k_k", [1, D]); k_a_d = din("k_a", [1, D]); r_k_d = din("r_k", [1, D])
    lnx_g_d = din("lnx_g", [1, D]); lnx_b_d = din("lnx_b", [1, D])
    lam_re_d = din("lam_re", [32, 64]); lam_im_d = din("lam_im", [32, 64]); log_dt_d = din("log_dt", [32, 1])
    b_re_d = din("b_re", [32, 1024]); b_im_d = din("b_im", [32, 1024])
    c_re_d = din("c_re", [512, 64]); c_im_d = din("c_im", [512, 64])
    d_skip_d = din("d_skip", [1, 512])
    w_glu_d = din("w_glu", [512, D]); b_glu_d = din("b_glu", [1, D])
    p_a_d = din("p_a", [D, D]); p_b_d = din("p_b", [512, D]); w_out_d = din("w_out", [D, D])
    final_g_d = din("final_g", [1, D])
    out_d = nc.dram_tensor("out", [T, D], F32, kind="ExternalOutput").ap()
    wbf_d = nc.dram_tensor("wbf", [NCT, 128, 8 * 128], BF16, kind="Internal").ap()
    s_tab_d = nc.dram_tensor("s_tab", [2, 2048], F32, kind="Internal").ap()
    s_bb_d = nc.dram_tensor("s_bb", [2, 32, 64, 16], F32, kind="Internal").ap()
    s_bm_d = nc.dram_tensor("s_bm", [512, 1024], F32, kind="Internal").ap()
    s_ct_d = nc.dram_tensor("s_ct", [2, 64, 512], F32, kind="Internal").ap()
    dbg = {}

    sb_bytes = [0]

    sb_cache = {}

    def sb(name, shape, dt=F32):
        if name in sb_cache:
            return sb_cache[name]
        n = 1
        for d_ in shape[1:]:
            n *= d_
        sb_bytes[0] += n * (2 if dt == BF16 else 4)
        t_ = es.enter_context(nc.sbuf_tensor(name, list(shape), dt))
        sb_cache[name] = t_
        return t_

    for (n_, sh_, dt_) in [("ones_f", [128, 128], F32), ("ident", [128, 128], BF16), ("m_su", [128, 128], BF16),
                           ("m_iu", [128, 128], BF16), ("blk1", [128, 128], BF16), ("hsel", [128, 2], BF16),
                           ("mu_col", [128, 26], F32), ("w0c", [128, 8], F32), ("a0c", [128, 8], F32), ("kkc", [128, 8], F32),
                           ("kac", [128, 8], F32), ("rkc", [128, 8], F32), ("dsk", [128, 4], F32),
                           ("pa_bf", [128, 8, D], BF16), ("pb_bf", [128, 4, D], BF16), ("wglu_bf", [128, 4, D], BF16),
                           ("wout_bf", [128, 8, D], BF16), ("wup_bf", [64, D], BF16), ("aup_bf", [64, D], BF16),
                           ("lnxg_row", [128, D], BF16), ("lnxb_row", [128, D], BF16), ("bglu_row", [128, D], BF16),
                           ("fg_row", [128, D], F32),
                           ("E_re", [128, 2048], BF16), ("E_im", [128, 2048], BF16), ("D_re", [128, 16, 128], BF16),
                           ("D_im", [128, 16, 128], BF16), ("L_re", [128, 16], F32), ("L_im", [128, 16], F32),
                           ("Cre_pad", [128, 16, 128], BF16), ("Cimn_pad", [128, 16, 128], BF16), ("Bm_bd", [128, 4, 1024], BF16),
                           ("xe_re", [128, 16], F32), ("xe_im", [128, 16], F32)]:
        sb(n_, sh_, dt_)

    es_setup = ExitStack()

    def sbt(name, shape, dt=F32):
        return es_setup.enter_context(nc.sbuf_tensor(name, list(shape), dt))

    def mm(out, lhsT, rhs, start, stop, r, w):
        S.add("pe", lambda e: e.matmul(out, lhsT=lhsT, rhs=rhs, start=start, stop=stop), r, w)

    def act(out, in_, func, r, w, bias=None, scale=None, accum=None, eng="act"):
        kw = {}
        if bias is not None: kw["bias"] = bias
        if scale is not None: kw["scale"] = scale
        if accum is not None: kw["accum_out"] = accum
        S.add("act", lambda e: e.activation(out=out, in_=in_, func=func, **kw), r, w)

    def tt(eng, out, in0, in1, op, r, w):
        S.add(eng, lambda e: e.tensor_tensor(out=out, in0=in0, in1=in1, op=op), r, w)

    def ts(eng, out, in0, s1, op0, r, w, s2=None, op1=None):
        if op1 is None:
            S.add(eng, lambda e: e.tensor_scalar(out=out, in0=in0, scalar1=s1, scalar2=None, op0=op0), r, w)
        else:
            S.add(eng, lambda e: e.tensor_scalar(out=out, in0=in0, scalar1=s1, scalar2=s2, op0=op0, op1=op1), r, w)

    def stt(out, in0, scalar, in1, op0, op1, r, w):
        S.add("dve", lambda e: e.scalar_tensor_tensor(out=out, in0=in0, scalar=scalar, in1=in1, op0=op0, op1=op1), r, w)

    def cp(eng, out, in_, r, w):
        if eng == "act":
            S.add("act", lambda e: e.copy(out=out, in_=in_), r, w)
        else:
            S.add(eng, lambda e: e.tensor_copy(out=out, in_=in_), r, w)

    def dma(q, out, in_, r, w, slow=False):
        if slow:
            S.add(q, lambda e: e.dma_start(out=out, in_=in_, allow_slow_non_contiguous=True), r, w, dma=True)
        else:
            S.add(q, lambda e: e.dma_start(out=out, in_=in_), r, w, dma=True)

    def memset(eng, ap, val, w):
        S.add(eng, lambda e: e.memset(ap, val), (), w)

    def bc_last(ap2d, n):
        return ap2d.unsqueeze(2).to_broadcast([ap2d.shape[0], ap2d.shape[1], n])

    def bc_mid(ap2d, n):
        return ap2d.unsqueeze(1).to_broadcast([ap2d.shape[0], n, ap2d.shape[1]])

    psb = [es.enter_context(nc.psum_tensor(f"psb{i}", [128, 512], F32)) for i in range(8)]
    ps_rr = [0]

    def ps_alloc():
        i = 3 + ps_rr[0] % 5
        ps_rr[0] += 1
        return psb[i], ("ps", i)

    ones_f = sb("ones_f", [128, 128]); ident_f = sbt("ident_f", [128, 128])
    ident = sb("ident", [128, 128], BF16)
    m_su = sb("m_su", [128, 128], BF16)
    m_iu = sb("m_iu", [128, 128], BF16)
    blk1 = sb("blk1", [128, 128], BF16)
    hsel = sb("hsel", [128, 2], BF16)
    tmpc = sbt("tmpc", [128, 128])
    memset("pool", ones_f[:, :], 1.0, ["ones_f"])

    def aff(out_ap, pattern, cmp, base, cm, key, src=None):
        src_ap = ones_f[:, 0:out_ap.shape[1]] if src is None else src
        S.add("pool", lambda e: e.affine_select(out=out_ap, in_=src_ap, pattern=pattern, compare_op=cmp,
                                                fill=0.0, base=base, channel_multiplier=cm),
              ["ones_f"], [key])

    aff(ident_f[:, :], [[-1, 128]], ALU.is_equal, 0, 1, "ident_f")
    cp("pool", ident[:, :], ident_f[:, :], ["ident_f"], ["ident"])
    aff(tmpc[:, :], [[1, 128]], ALU.is_gt, 0, -1, "tmpc")
    cp("pool", m_su[:, :], tmpc[:, :], ["tmpc"], ["m_su"])
    aff(tmpc[:, :], [[1, 128]], ALU.is_ge, 0, -1, "tmpc")
    cp("pool", m_iu[:, :], tmpc[:, :], ["tmpc"], ["m_iu"])
    aff(tmpc[:, 0:64], [[0, 64]], ALU.is_ge, 63, -1, "tmpc")
    aff(tmpc[:, 64:128], [[0, 64]], ALU.is_ge, -64, 1, "tmpc")
    cp("pool", blk1[:, :], tmpc[:, :], ["tmpc"], ["blk1"])
    cp("pool", hsel[:, 0:1], tmpc[:, 0:1], ["tmpc"], ["hsel"])
    cp("pool", hsel[:, 1:2], tmpc[:, 64:65], ["tmpc"], ["hsel"])

    mu_col = sb("mu_col", [128, 26])
    w0c = sb("w0c", [128, 8]); a0c = sb("a0c", [128, 8]); kkc = sb("kkc", [128, 8])
    kac = sb("kac", [128, 8]); rkc = sb("rkc", [128, 8]); dsk = sb("dsk", [128, 4]); gcol = sbt("gcol", [128, 8])
    dma("sp", mu_col[:, 0:24], mu_d[0, 0:3072].rearrange("(c p) -> p c", p=128), [], ["mu_col"], slow=True)
    dma("sp", mu_col[0:64, 24:26], mu_d[0, 3072:3200].rearrange("(c p) -> p c", p=64), [], ["mu_col"], slow=True)
    for (t_, d_, n_) in [(w0c, w0_d, "w0c"), (a0c, a0_d, "a0c"), (kkc, k_k_d, "kkc"), (kac, k_a_d, "kac"),
                         (rkc, r_k_d, "rkc"), (gcol, norm_g_d, "gcol")]:
        dma("sp", t_[:, :], d_[0, :].rearrange("(c p) -> p c", p=128), [], [n_], slow=True)
    dma("sp", dsk[:, :], d_skip_d[0, :].rearrange("(c p) -> p c", p=128), [], ["dsk"], slow=True)
    stage = [sbt(f"stage{i}", [128, 1024]) for i in range(2)]
    stage_bf = [sbt(f"stagebf{i}", [128, 1024], BF16) for i in range(2)]
    st_rr = [0]

    def stage_next():
        i = st_rr[0] % 2
        st_rr[0] += 1
        return i

    pa_bf = sb("pa_bf", [128, 8, D], BF16); pb_bf = sb("pb_bf", [128, 4, D], BF16)
    wglu_bf = sb("wglu_bf", [128, 4, D], BF16); wout_bf = sb("wout_bf", [128, 8, D], BF16)
    wup_bf = sb("wup_bf", [64, D], BF16); aup_bf = sb("aup_bf", [64, D], BF16)

    def load_cast(dst_ap, src_ap, key, parts=128, rkeys=()):
        i = stage_next()
        dma("sp" if i == 0 else "pool", stage[i][0:parts, :], src_ap, list(rkeys), [("stage", i)])
        cp("act" if i == 0 else "dve", dst_ap, stage[i][0:parts, :], [("stage", i)], [key])

    for kc in range(8):
        load_cast(pa_bf[:, kc, :], p_a_d[kc * 128:(kc + 1) * 128, :], "pa_bf")
        load_cast(wout_bf[:, kc, :], w_out_d[kc * 128:(kc + 1) * 128, :], "wout_bf")
    for kc in range(4):
        load_cast(pb_bf[:, kc, :], p_b_d[kc * 128:(kc + 1) * 128, :], "pb_bf")
        load_cast(wglu_bf[:, kc, :], w_glu_d[kc * 128:(kc + 1) * 128, :], "wglu_bf")
    lnxg_row = sb("lnxg_row", [128, D], BF16); lnxb_row = sb("lnxb_row", [128, D], BF16)
    bglu_row = sb("bglu_row", [128, D], BF16); fg_row = sb("fg_row", [128, D])
    for (t_, d_, n_) in [(lnxg_row, lnx_g_d, "lnxg_row"), (lnxb_row, lnx_b_d, "lnxb_row"), (bglu_row, b_glu_d, "bglu_row")]:
        load_cast(t_[:, :], d_[0, :].partition_broadcast(128), n_)
    dma("pool", fg_row[:, :], final_g_d[0, :].partition_broadcast(128), [], ["fg_row"])
    load_cast(wup_bf[:, :], w_up_d[:, :], "wup_bf", parts=64)
    load_cast(aup_bf[:, :], a_up_d[:, :], "aup_bf", parts=64)

    ct_cols = []
    for ct in range(NCT):
        if ct < 24: ct_cols.append((ct * 128, 128))
        elif ct == 24: ct_cols.append((3072, 64))
        elif ct == 25: ct_cols.append((3136, 64))
        else: ct_cols.append((3200 + (ct - 26) * 128, 128))
    wst = [sbt(f"wst{i}", [128, 3712]) for i in range(2)]
    wsb = [sbt(f"wsb{i}", [128, 3712], BF16) for i in range(2)]
    for kc in range(8):
        for hw in range(2):
            i = hw
            q = "sp" if i == 0 else "pool"
            c0, c1 = (0, 3584) if hw == 0 else (3584, 7296)
            n = c1 - c0
            dma(q, wst[i][:, 0:n], w_in_d[kc * 128:(kc + 1) * 128, c0:c1], [], [("wst", i)])
            if hw == 0:
                ts("dve", wsb[i][:, 0:n], wst[i][:, 0:n], gcol[:, kc:kc + 1], ALU.mult, [("wst", i), "gcol"], [("wsb", i)])
                dma(q, wbf_d[0:24, :, kc * 128:(kc + 1) * 128].rearrange("t p c -> p t c"),
                    wsb[i][:, 0:3072].rearrange("p (t c) -> p t c", c=128), [("wsb", i)], [("wbfA", kc)])
                dma(q, wbf_d[24, :, kc * 128:kc * 128 + 64], wsb[i][:, 3072:3136], [("wsb", i)], [("wbfB", kc)])
                dma(q, wbf_d[25, :, kc * 128:kc * 128 + 64], wsb[i][:, 3136:3200], [("wsb", i)], [("wbfC", kc)])
                dma(q, wbf_d[26:29, :, kc * 128:(kc + 1) * 128].rearrange("t p c -> p t c"),
                    wsb[i][:, 3200:3584].rearrange("p (t c) -> p t c", c=128), [("wsb", i)], [("wbfD", kc)])
            else:
                act(wsb[i][:, 0:n], wst[i][:, 0:n], AF.Identity, [("wst", i), "gcol"], [("wsb", i)], scale=gcol[:, kc:kc + 1])
                dma(q, wbf_d[29:58, :, kc * 128:(kc + 1) * 128].rearrange("t p c -> p t c"),
                    wsb[i][:, 0:n].rearrange("p (t c) -> p t c", c=128), [("wsb", i)], [("wbfE", kc)])

    E_re = sb("E_re", [128, 2048], BF16); E_im = sb("E_im", [128, 2048], BF16)
    D_re = sb("D_re", [128, 16, 128], BF16); D_im = sb("D_im", [128, 16, 128], BF16)
    L_re = sb("L_re", [128, 16]); L_im = sb("L_im", [128, 16])
    Cre_pad = sb("Cre_pad", [128, 16, 128], BF16); Cimn_pad = sb("Cimn_pad", [128, 16, 128], BF16)
    Bm_bd = sb("Bm_bd", [128, 4, 1024], BF16)
    xe_re = sb("xe_re", [128, 16]); xe_im = sb("xe_im", [128, 16])
    tA = sbt("tA", [128, 512]); tB = sbt("tB", [128, 512]); tC = sbt("tC", [128, 512]); tDd = sbt("tD", [128, 512])
    tI = sbt("tI", [128, 512], I32); tM = sbt("tM", [128, 512])
    s5a = sbt("s5a", [32, 64]); s5b = sbt("s5b", [32, 64]); s5c = sbt("s5c", [32, 64]); s5d = sbt("s5d", [32, 64])
    s5e = sbt("s5e", [32, 64]); s5f = sbt("s5f", [32, 64]); s5g = sbt("s5g", [32, 64]); s5h = sbt("s5h", [32, 64])
    s5i = sbt("s5i", [32, 64]); s5i32 = sbt("s5i32", [32, 64], I32)
    ldt = sbt("ldt", [32, 1]); lre = sbt("lre", [32, 64]); lim = sbt("lim", [32, 64])
    dma("sp", ldt[:, :], log_dt_d[:, :], [], ["ldt"])
    dma("sp", lre[:, :], lam_re_d[:, :], [], ["lre"])
    dma("sp", lim[:, :], lam_im_d[:, :], [], ["lim"])
    act(ldt[:, :], ldt[:, :], AF.Exp, ["ldt"], ["ldt"])
    ts("dve", s5a[:, :], lre[:, :], ldt[:, 0:1], ALU.mult, ["lre", "ldt"], ["s5a"])
    ts("dve", s5b[:, :], lim[:, :], ldt[:, 0:1], ALU.mult, ["lim", "ldt"], ["s5b"], s2=1.0 / TWO_PI, op1=ALU.mult)
    dma("sp", s_tab_d[0, :].rearrange("(g p) -> g p", p=64), s5a[:, :], ["s5a"], ["s_tab"])
    dma("sp", s_tab_d[1, :].rearrange("(g p) -> g p", p=64), s5b[:, :], ["s5b"], ["s_tab"])

    def sincos(turn_ap, sin_out, cos_out, tmps, tkeys, keys_r, key_s, key_c):
        ta, tb, ti = tmps
        ka, kb, ki = tkeys
        cp("dve", ti, turn_ap, keys_r, [ki])
        cp("dve", ta, ti, [ki], [ka])
        tt("dve", ta, turn_ap, ta, ALU.subtract, keys_r + [ka], [ka])
        for thr, sgn in ((0.5, -1.0), (-0.5, 1.0)):
            ts("dve", tb, ta, thr, ALU.is_gt if sgn < 0 else ALU.is_lt, [ka], [kb])
            stt(ta, tb, sgn, ta, ALU.mult, ALU.add, [ka, kb], [ka])
        act(sin_out, ta, AF.Sin, [ka], [key_s], scale=TWO_PI * (1 - 1e-6))
        ts("dve", ta, ta, 0.25, ALU.add, [ka], [ka])
        ts("dve", tb, ta, 0.5, ALU.is_gt, [ka], [kb])
        stt(ta, tb, -1.0, ta, ALU.mult, ALU.add, [ka, kb], [ka])
        act(cos_out, ta, AF.Sin, [ka], [key_c], scale=TWO_PI * (1 - 1e-6))

    sincos(s5b[:, :], s5c[:, :], s5d[:, :], (s5h[:, :], s5i[:, :], s5i32[:, :]), ("s5h", "s5i", "s5i32"), ["s5b"], "s5c", "s5d")
    act(s5e[:, :], s5a[:, :], AF.Exp, ["s5a"], ["s5e"])
    tt("dve", s5d[:, :], s5d[:, :], s5e[:, :], ALU.mult, ["s5d", "s5e"], ["s5d"])
    tt("dve", s5c[:, :], s5c[:, :], s5e[:, :], ALU.mult, ["s5c", "s5e"], ["s5c"])
    tt("dve", s5e[:, :], lre[:, :], lre[:, :], ALU.mult, ["lre"], ["s5e"])
    tt("dve", s5f[:, :], lim[:, :], lim[:, :], ALU.mult, ["lim"], ["s5f"])
    tt("dve", s5e[:, :], s5e[:, :], s5f[:, :], ALU.add, ["s5e", "s5f"], ["s5e"])
    S.add("dve", lambda e: e.reciprocal(out=s5e[:, :], in_=s5e[:, :]), ["s5e"], ["s5e"])
    ts("dve", s5d[:, :], s5d[:, :], -1.0, ALU.add, ["s5d"], ["s5d"])
    tt("dve", s5f[:, :], s5d[:, :], lre[:, :], ALU.mult, ["s5d", "lre"], ["s5f"])
    tt("dve", s5g[:, :], s5c[:, :], lim[:, :], ALU.mult, ["s5c", "lim"], ["s5g"])
    tt("dve", s5f[:, :], s5f[:, :], s5g[:, :], ALU.add, ["s5f", "s5g"], ["s5f"])
    tt("dve", s5f[:, :], s5f[:, :], s5e[:, :], ALU.mult, ["s5f", "s5e"], ["s5f"])
    tt("dve", s5g[:, :], s5c[:, :], lre[:, :], ALU.mult, ["s5c", "lre"], ["s5g"])
    tt("dve", s5h[:, :], s5d[:, :], lim[:, :], ALU.mult, ["s5d", "lim"], ["s5h"])
    tt("dve", s5g[:, :], s5g[:, :], s5h[:, :], ALU.subtract, ["s5g", "s5h"], ["s5g"])
    tt("dve", s5g[:, :], s5g[:, :], s5e[:, :], ALU.mult, ["s5g", "s5e"], ["s5g"])
    fre_b = bc_last(s5f[:, 0:32], 16)
    for hfp in range(2):
        psl = slice(hfp * 32, (hfp + 1) * 32)
        bre, bim, bt1, bt2 = tA[0:32, :], tB[0:32, :], tC[0:32, :], tDd[0:32, :]
        dma("sp", bre, b_re_d[:, hfp * 512:(hfp + 1) * 512], [], ["tA"])
        dma("pool", bim, b_im_d[:, hfp * 512:(hfp + 1) * 512], [], ["tB"])
        v3 = lambda t: t.rearrange("g (p h) -> g p h", h=16)
        fre_b = bc_last(s5f[:, psl], 16); fim_b = bc_last(s5g[:, psl], 16)
        tt("dve", v3(bt1), v3(bre), fre_b, ALU.mult, ["tA", "s5f"], ["tC"])
        tt("dve", v3(bt2), v3(bim), fim_b, ALU.mult, ["tB", "s5g"], ["tD"])
        tt("dve", bt1, bt1, bt2, ALU.subtract, ["tC", "tD"], ["tC"])
        dma("sp", s_bb_d[0, :, psl, :].rearrange("g p h -> g (p h)"), bt1, ["tC"], [("s_bb", 0)])
        tt("dve", v3(bt2), v3(bim), fre_b, ALU.mult, ["tB", "s5f"], ["tD"])
        tt("dve", v3(bre), v3(bre), fim_b, ALU.mult, ["tA", "s5g"], ["tA"])
        tt("dve", bt2, bt2, bre, ALU.add, ["tD", "tA"], ["tD"])
        dma("sp", s_bb_d[1, :, psl, :].rearrange("g p h -> g (p h)"), bt2, ["tD"], [("s_bb", 1)])
    zt = stage_next()
    memset("dve", stage[zt][:, :], 0.0, [("stage", zt)])
    for r4 in range(4):
        dma("sp", s_bm_d[r4 * 128:(r4 + 1) * 128, :], stage[zt][:, :], [("stage", zt)], ["s_bm"])
    for g in range(32):
        gl = g % 8
        for c in range(2):
            dma("sp" if c == 0 else "pool",
                s_bm_d[g * 16:(g + 1) * 16, gl * 128 + c * 64: gl * 128 + c * 64 + 64],
                s_bb_d[c, g].rearrange("p h -> h p"), [("s_bb", c), "s_bm"], [("s_bmw", g, c)], slow=True)
    for ft in range(4):
        load_cast(Bm_bd[:, ft, :], s_bm_d[ft * 128:(ft + 1) * 128, :], "Bm_bd", rkeys=["s_bm"] + [("s_bmw", g_, c_) for g_ in range(32) for c_ in range(2)])
    cld = sbt("cld", [128, 64]); cT = sbt("cT", [64, 512])
    for c, cd in enumerate((c_re_d, c_im_d)):
        for q4 in range(4):
            dma("sp", cld[:, :], cd[q4 * 128:(q4 + 1) * 128, :], [], ["cld"])
            pt, pk = ps_alloc()
            S.add("pe", lambda e, pt=pt: e.transpose(pt[0:64, 0:128], cld[:, :], ident_f[:, :]), ["cld", "ident_f"], [pk])
            cp("act", cT[:, q4 * 128:(q4 + 1) * 128], pt[0:64, 0:128], [pk], ["cT"])
        dma("sp", s_ct_d[c], cT[:, :], ["cT"], [("s_ct", c)])
    cl2 = sbt("cl2", [128, 16, 16])
    memset("pool", Cre_pad[:, :, :], 0.0, ["Cre_pad"])
    memset("pool", Cimn_pad[:, :, :], 0.0, ["Cimn_pad"])
    for c, (dst, key) in enumerate(((Cre_pad, "Cre_pad"), (Cimn_pad, "Cimn_pad"))):
        for g2 in range(2):
            src = s_ct_d[c].rearrange("p (gp g2 h) -> p gp g2 h", g2=2, h=16)[:, :, g2, :]
            dma("sp", cl2[g2 * 64:(g2 + 1) * 64, :, :], src, [("s_ct", c)], ["cl2"])
        for g2 in range(2):
            for q in range(4):
                dstv = dst[g2 * 64:(g2 + 1) * 64, :, :].rearrange("p (ft q) c -> p ft q c", q=4)[:, :, q, q * 32 + g2 * 16: q * 32 + g2 * 16 + 16]
                srcv = cl2[g2 * 64:(g2 + 1) * 64, :, :].rearrange("p (ft q) h -> p ft q h", q=4)[:, :, q, :]
                ts("dve", dstv, srcv, 1.0 if c == 0 else -1.0, ALU.mult, ["cl2"], [key])
    lrd_col = sbt("lrd_col", [128, 16]); trn_col = sbt("trn_col", [128, 16])
    dma("sp", lrd_col[:, :], s_tab_d[0, :].rearrange("(gp q) -> q gp", q=128), ["s_tab"], ["lrd_col"], slow=True)
    dma("sp", trn_col[:, :], s_tab_d[1, :].rearrange("(gp q) -> q gp", q=128), ["s_tab"], ["trn_col"], slow=True)
    nrow = sbt("nrow", [128, 128]); ncol = sbt("ncol", [128, 1])
    S.add("pool", lambda e: e.iota(nrow[:, :], pattern=[[1, 128]], base=-127, channel_multiplier=0,
                                   allow_small_or_imprecise_dtypes=True), [], ["nrow"])
    S.add("pool", lambda e: e.iota(ncol[:, :], pattern=[[0, 1]], base=127, channel_multiplier=-1,
                                   allow_small_or_imprecise_dtypes=True), [], ["ncol"])

    tok = ["tabtok"]
    tE = sbt("tE", [128, 512]); tF = sbt("tF", [128, 512])
    for q in range(4):
        gsl = slice(q * 4, q * 4 + 4)
        tt("dve", tE[:, :].rearrange("p (a b) -> p a b", b=128), bc_mid(nrow[:, :], 4), bc_last(trn_col[:, gsl], 128),
           ALU.mult, ["nrow", "trn_col"] + tok, ["tE"])
        tt("dve", tF[:, :].rearrange("p (a b) -> p a b", b=128), bc_mid(nrow[:, :], 4), bc_last(lrd_col[:, gsl], 128),
           ALU.mult, ["nrow", "lrd_col"] + tok, ["tF"])
        sincos(tE[:, :], tB[:, :], tC[:, :], (tDd[:, :], tA[:, :], tI[:, :]), ("tD", "tA", "tI"), ["tE"], "tB", "tC")
        act(tM[:, :], tF[:, :], AF.Exp, ["tF"], ["tM"])
        tt("dve", D_re[:, gsl, :].rearrange("p a b -> p (a b)"), tC[:, :], tM[:, :], ALU.mult, ["tC", "tM"], ["D_re"] + tok)
        tt("dve", D_im[:, gsl, :].rearrange("p a b -> p (a b)"), tB[:, :], tM[:, :], ALU.mult, ["tB", "tM"], ["D_im"] + tok)
    ts("dve", tE[:, 0:16], trn_col[:, :], 128.0, ALU.mult, ["trn_col"] + tok, ["tE"])
    ts("dve", tF[:, 0:16], lrd_col[:, :], 128.0, ALU.mult, ["lrd_col"] + tok, ["tF"])
    sincos(tE[:, 0:16], tB[:, 0:16], tC[:, 0:16], (tDd[:, 0:16], tA[:, 0:16], tI[:, 0:16]), ("tD", "tA", "tI"), ["tE"], "tB", "tC")
    act(tM[:, 0:16], tF[:, 0:16], AF.Exp, ["tF"], ["tM"])
    tt("dve", L_re[:, :], tC[:, 0:16], tM[:, 0:16], ALU.mult, ["tC", "tM"], ["L_re"] + tok)
    tt("dve", L_im[:, :], tB[:, 0:16], tM[:, 0:16], ALU.mult, ["tB", "tM"], ["L_im"] + tok)
    rowt = stage[0][:, 0:512]; rowl = stage[1][:, 0:512]
    for q in range(4):
        csl = slice(q * 512, (q + 1) * 512)
        dma("sp", rowt, s_tab_d[1, csl].partition_broadcast(128), ["s_tab"] + tok, [("stage", 0)])
        dma("pool", rowl, s_tab_d[0, csl].partition_broadcast(128), ["s_tab"] + tok, [("stage", 1)])
        ts("dve", tE[:, :], rowt, ncol[:, 0:1], ALU.mult, [("stage", 0), "ncol"] + tok, ["tE"])
        ts("dve", tF[:, :], rowl, ncol[:, 0:1], ALU.mult, [("stage", 1), "ncol"] + tok, ["tF"])
        sincos(tE[:, :], tB[:, :], tC[:, :], (tDd[:, :], tA[:, :], tI[:, :]), ("tD", "tA", "tI"), ["tE"], "tB", "tC")
        act(tM[:, :], tF[:, :], AF.Exp, ["tF"], ["tM"])
        tt("dve", E_re[:, csl], tC[:, :], tM[:, :], ALU.mult, ["tC", "tM"], ["E_re"] + tok)
        tt("dve", E_im[:, csl], tB[:, :], tM[:, :], ALU.mult, ["tB", "tM"], ["E_im"] + tok)
    S.emit(nc, es, drain=True)
    es_setup.close()
    S = Sched()
    memset("pool", xe_re[:, :], 0.0, ["xe_re"])
    memset("pool", xe_im[:, :], 0.0, ["xe_im"])

    NXB = 1
    hT = sb("hT", [128, 8, 256], BF16)
    NWB = 2
    wblk = [sb(f"wblk{i}", [128, 8 * 128], BF16) for i in range(NWB)]
    zraw = [sb(f"zraw{i}", [128, 257]) for i in range(2)]
    zcar = sb("zcar", [128, 26])
    z_fm = sb("z_fm", [128, 30, 256], BF16)
    z_tm = sb("z_tm", [128, 2, 3584], BF16)
    zdiff = sb("zdiff", [128, 256])
    xin = [z_fm[:, 0:8, :].rearrange("p a b -> p (a b)").bitcast(F32)]
    xnA = z_fm[:, 8:12, :].rearrange("p a b -> p (a b)")
    XK = [("z_fm", i) for i in range(8)]
    NK = [("z_fm", i) for i in range(8, 12)]
    memset("pool", zcar[:, :], 0.0, ["zcar"])
    ssq = sb("ssq", [128, 4]);
    S_f = sb("S_f", [128, 8, 64]); S_bd = sb("S_bd", [128, 8, 128], BF16)
    memset("pool", S_f[:, :, :], 0.0, [("S_f", h) for h in range(8)])
    memset("pool", S_bd[:, :, :], 0.0, [("S_bd", h) for h in range(8)])
    NSET = 2
    def mkset(i):
        d = {}
        for nm in ["ld", "icl", "cl", "Wc", "Wi", "Wx", "kkn", "kh", "t1", "t2", "bt"]:
            d[nm] = sb(f"{nm}{i}", [128, 128])
        for nm in ["kk2", "a_p", "r_p", "rkp", "v_bf", "bh_p", "kh_p", "ApT"]:
            d[nm] = sb(f"{nm}{i}", [128, 128], BF16)
        for nm in ["b_bd", "k_bd"]:
            d[nm] = sb(f"{nm}{i}", [128, 2, 128], BF16)
        d["tm4"] = sb(f"tm4{i}", [128, 4, 128], BF16)
        for nm in ["AakT", "ArbT", "ArkT", "Pm", "Qm", "Zm", "Pn", "Qn", "Zn"]:
            d[nm] = sb(f"{nm}{i}", [128, 2, 128], BF16)
        d["W1"] = sb(f"W1{i}", [128, 2, 64], BF16)
        d["Vu"] = sb(f"Vu{i}", [128, 2, 64])
        d["U"] = sb(f"U{i}", [128, 128], BF16)
        d["wl"] = sb(f"wl{i}", [128, 1])
        d["i"] = i
        return d
    sets = [mkset(i) for i in range(NSET)]
    for st_ in sets:
        memset("pool", st_["b_bd"][:, :, :], 0.0, [("b_bd", st_["i"])])
        memset("pool", st_["k_bd"][:, :, :], 0.0, [("k_bd", st_["i"])])
    txw = sb("txw", [64, 128], BF16); xab = sb("xab", [64, 128], BF16)
    rk_sb = sb("rk_sb", [128, 16])
    Vall = sb("Vall", [128, 8, 128], BF16)
    ysb = sb("ysb", [128, D]); ytmp = sb("ytmp", [128, D]); ysq = ytmp
    ssqA = sb("ssqA", [128, 4])
    gst = sb("gst", [128, 4, 16])
    ya_bf = sb("ya_bf", [128, D], BF16); yaT = sb("yaT", [128, 8, 128], BF16)
    xn_bf = ya_bf
    merged_a = ya_bf
    xres = [ytmp]
    u_bf = z_fm
    BUp_re = sb("BUp_re", [128, 1024], BF16); BUp_im = sb("BUp_im", [128, 1024], BF16)
    s5t = [sb(f"s5t{i}", [128, 512]) for i in range(4)]
    X_re = sb("X_re", [128, 8, 128], BF16); X_im = sb("X_im", [128, 8, 128], BF16)
    xq = [sb(f"xq{i}", [128, 4]) for i in range(6)]
    ys_f = s5t[2]; ys_t = s5t[3]; ys_s = s5t[0]
    gel_bf = sb("gel_bf", [128, 4, 128], BF16)
    g1 = s5t[0]; g2 = s5t[1]
    yb_bf = sb("yb_bf", [128, 512], BF16); ybT = sb("ybT", [128, 4, 128], BF16)
    merged_bf = sb("merged_bf", [128, D], BF16); mT = yaT
    junk_bf = merged_bf
    xo = ysb; outt = ytmp

    wb_rr = [0]
    out_keys = []
    pending = []

    def step_pending(n):
        for _ in range(n):
            for g_ in list(pending):
                try:
                    next(g_)
                except StopIteration:
                    pending.remove(g_)

    def flush_pending():
        while pending:
            step_pending(1)
    for st in range(NST):
        for c in range(2):
            tok0 = st * 256 + c * 128
            dma("sp", xin[0], x_d[tok0:tok0 + 128, :], [], XK)
            act(xnA, xin[0], AF.Square, XK, NK + ["ssqA0"], accum=ssqA[:, 0:1])
            ts("dve", ssqA[:, 1:2], ssqA[:, 0:1], 1.0 / D, ALU.mult, ["ssqA0"], ["ssqA1"], s2=1e-6, op1=ALU.add)
            act(ssqA[:, 2:3], ssqA[:, 1:2], AF.Ln, ["ssqA1"], ["ssqA2"])
            act(ssqA[:, 3:4], ssqA[:, 2:3], AF.Exp, ["ssqA2"], ["ssqA3"], scale=-0.5)
            act(xnA, xin[0], AF.Identity, XK + ["ssqA3"], NK, scale=ssqA[:, 3:4])
            pt, pk = ps_alloc()
            ptb = pt[:, :].bitcast(BF16)
            for kc in range(8):
                S.add("pe", lambda e, ptb=ptb, kc=kc: e.transpose(ptb[:, kc * 128:(kc + 1) * 128],
                                                                   xnA[:, kc * 128:(kc + 1) * 128], ident[:, :]),
                      NK + ["ident"], [pk])
            cp("dve", hT[:, :, c * 128:(c + 1) * 128], ptb.rearrange("p (k t) -> p k t", t=128), [pk], ["hT"])
            step_pending(1)
        for ct in range(NCT):
            if ct < 26:
                step_pending(1)
            elif ct == 26:
                flush_pending()
            cs, cn = ct_cols[ct]
            wi = wb_rr[0] % NWB
            wb_rr[0] += 1
            wv = wblk[wi][:, :].rearrange("p (k c) -> p k c", c=128)
            dma("sp" if ct % 2 == 0 else "pool", wblk[wi][:, :] if cn == 128 else wv[:, :, 0:cn],
                wbf_d[ct] if cn == 128 else wbf_d[ct].rearrange("p (k c) -> p k c", c=128)[:, :, 0:cn],
                [], [("wblk", wi)])
            fm = ct < 26 or (30 <= ct + 0 and False)
            if ct < 26:
                kind, zi = "zs", ct
            elif ct < 34:
                kind, zi = "ga", ct - 26
            elif ct < 38:
                kind, zi = "u", ct - 34
            elif ct < 42:
                kind, zi = "gb", ct - 38
            elif ct < 50:
                kind, zi = "ma", ct - 42
            else:
                kind, zi = "mb", ct - 50
            if kind in ("zs", "u"):
                pt, pk = ps_alloc()
                for kc in range(8):
                    mm(pt[0:cn, 0:256], wv[:, kc, 0:cn], hT[:, kc, :], kc == 0, kc == 7, [("wblk", wi), "hT"], [pk])
                if kind == "u":
                    cp("act", z_fm[:, 26 + zi, :], pt[:, 0:256], [pk], [("z_fm", 26 + zi)])
                else:
                    zr = zraw[ct % 2]
                    zk = ("zraw", ct % 2)
                    cp("act", zr[0:cn, 1:257], pt[0:cn, 0:256], [pk], [zk])
                    cp("act", zr[0:cn, 0:1], zcar[0:cn, ct:ct + 1], ["zcar"], [zk])
                    cp("act", zcar[0:cn, ct:ct + 1], zr[0:cn, 256:257], [zk], ["zcar"])
                    tt("dve", zdiff[0:cn, :], zr[0:cn, 0:256], zr[0:cn, 1:257], ALU.subtract, [zk], ["zdiff"])
                    stt(z_fm[0:cn, zi, :], zdiff[0:cn, :], mu_col[0:cn, ct:ct + 1], zr[0:cn, 1:257], ALU.mult, ALU.add,
                        ["zdiff", zk, "mu_col"], [("z_fm", zi)])
            else:
                off = {"ga": 0, "gb": 1024, "ma": 1536, "mb": 2560}[kind] + zi * 128
                func = AF.Silu if kind in ("ga", "gb") else AF.Sigmoid
                pt, pk = ps_alloc()
                for c in range(2):
                    for kc in range(8):
                        mm(pt[:, c * 128:(c + 1) * 128], hT[:, kc, c * 128:(c + 1) * 128], wv[:, kc, :], kc == 0, kc == 7,
                           [("wblk", wi), "hT"], [pk])
                act(z_tm[:, :, off:off + 128], pt[:, 0:256].rearrange("p (c n) -> p c n", n=128), func, [pk],
                    [("z_tm", kind)])
        for c in range(2):
            tsl = slice(c * 128, (c + 1) * 128)
            tok0 = st * 256 + c * 128
            zr_all = [("z_fm", i) for i in range(30)]
            act(txw[:, :], z_fm[0:64, 24, tsl], AF.Tanh, [("z_fm", 24)], ["txw"])
            cp("act", xab[:, :], z_fm[0:64, 25, tsl], [("z_fm", 25)], ["xab"])
            rk_pt, rk_pk = psb[2], ("ps", 2)
            y_pt = [(psb[0], ("ps", 0)), (psb[1], ("ps", 1))]
            def s5_gen():
                for hS in range(2):
                    for ft in (2 * hS, 2 * hS + 1):
                        pbu = [ps_alloc(), ps_alloc()]
                        for hf in range(2):
                            mm(pbu[hf][0][:, :], z_fm[:, 26 + ft, tsl], Bm_bd[:, ft, hf * 512:(hf + 1) * 512], True, True,
                               [("z_fm", 26 + ft), "Bm_bd"], [pbu[hf][1]])
                        for hf in range(2):
                            pv4 = pbu[hf][0][:, :].rearrange("p (g c q) -> p g c q", c=2, q=64)
                            bre_v = pv4[:, :, 0, :]; bim_v = pv4[:, :, 1, :]
                            gbase = ft * 8 + hf * 4
                            esl = slice(gbase * 64, (gbase + 4) * 64)
                            lsl = slice((gbase - 16 * hS) * 64, (gbase - 16 * hS + 4) * 64)
                            e_re = E_re[:, esl].rearrange("p (g q) -> p g q", q=64); e_im = E_im[:, esl].rearrange("p (g q) -> p g q", q=64)
                            tv = [s5t[i][:, 0:256].rearrange("p (g q) -> p g q", q=64) for i in range(4)]
                            pk_ = pbu[hf][1]
                            tt("dve", tv[0], bre_v, e_re, ALU.mult, [pk_, "E_re"], ["s5t0"])
                            tt("dve", tv[1], bim_v, e_im, ALU.mult, [pk_, "E_im"], ["s5t1"])
                            tt("dve", tv[2], bre_v, e_im, ALU.mult, [pk_, "E_im"], ["s5t2"])
                            tt("dve", tv[3], bim_v, e_re, ALU.mult, [pk_, "E_re"], ["s5t3"])
                            tt("pool", BUp_re[:, lsl], s5t[0][:, 0:256], s5t[1][:, 0:256], ALU.subtract, ["s5t0", "s5t1"], ["BUp_re"])
                            tt("pool", BUp_im[:, lsl], s5t[2][:, 0:256], s5t[3][:, 0:256], ALU.add, ["s5t2", "s5t3"], ["BUp_im"])
                        yield
                    for q in (2 * hS, 2 * hS + 1):
                        pxr, pxrk = ps_alloc(); pxi, pxik = ps_alloc()
                        for g4 in range(4):
                            lg = (q - 2 * hS) * 4 + g4
                            mm(pxr[:, g4 * 128:(g4 + 1) * 128], BUp_re[:, lg * 128:(lg + 1) * 128], m_iu[:, :], True, True, ["BUp_re", "m_iu"], [pxrk])
                            mm(pxi[:, g4 * 128:(g4 + 1) * 128], BUp_im[:, lg * 128:(lg + 1) * 128], m_iu[:, :], True, True, ["BUp_im", "m_iu"], [pxik])
                        gsl = slice(q * 4, q * 4 + 4)
                        lgs = slice((q - 2 * hS) * 4, (q - 2 * hS) * 4 + 4)
                        tt("pool", xq[0][:, :], L_re[:, gsl], xe_re[:, gsl], ALU.mult, ["L_re", "xe_re"], ["xq0"])
                        tt("pool", xq[1][:, :], L_im[:, gsl], xe_im[:, gsl], ALU.mult, ["L_im", "xe_im"], ["xq1"])
                        tt("pool", xq[2][:, :], L_re[:, gsl], xe_im[:, gsl], ALU.mult, ["L_re", "xe_im"], ["xq2"])
                        tt("pool", xq[3][:, :], L_im[:, gsl], xe_re[:, gsl], ALU.mult, ["L_im", "xe_re"], ["xq3"])
                        tt("pool", xq[4][:, :], xq[0][:, :], xq[1][:, :], ALU.subtract, ["xq0", "xq1"], ["xq4"])
                        tt("pool", xq[5][:, :], xq[2][:, :], xq[3][:, :], ALU.add, ["xq2", "xq3"], ["xq5"])
                        for g4 in range(4):
                            gp = q * 4 + g4
                            cs = slice(g4 * 128, (g4 + 1) * 128)
                            stt(s5t[0][:, cs], pxr[:, cs], xq[4][:, g4:g4 + 1], D_re[:, gp, :], ALU.add, ALU.mult, [pxrk, "xq4", "D_re"], ["s5t0"])
                            stt(s5t[1][:, cs], pxi[:, cs], xq[5][:, g4:g4 + 1], D_im[:, gp, :], ALU.add, ALU.mult, [pxik, "xq5", "D_im"], ["s5t1"])
                            stt(s5t[2][:, cs], pxr[:, cs], xq[4][:, g4:g4 + 1], D_im[:, gp, :], ALU.add, ALU.mult, [pxrk, "xq4", "D_im"], ["s5t2"])
                            stt(s5t[3][:, cs], pxi[:, cs], xq[5][:, g4:g4 + 1], D_re[:, gp, :], ALU.add, ALU.mult, [pxik, "xq5", "D_re"], ["s5t3"])
                        tt("pool", X_re[:, lgs, :].rearrange("p a b -> p (a b)"), s5t[0][:, :], s5t[1][:, :], ALU.subtract, ["s5t0", "s5t1"], ["X_re"])
                        tt("pool", X_im[:, lgs, :].rearrange("p a b -> p (a b)"), s5t[2][:, :], s5t[3][:, :], ALU.add, ["s5t2", "s5t3"], ["X_im"])
                        last_r = pxr[:, :].rearrange("p (g t) -> p g t", t=128)[:, :, 127]
                        last_i = pxi[:, :].rearrange("p (g t) -> p g t", t=128)[:, :, 127]
                        tt("dve", xe_re[:, gsl], last_r, xq[4][:, :], ALU.add, [pxrk, "xq4"], ["xe_re"])
                        tt("dve", xe_im[:, gsl], last_i, xq[5][:, :], ALU.add, [pxik, "xq5"], ["xe_im"])
                        yield
                    pys, pysk = ps_alloc()
                    for fl in range(2):
                        ft = 2 * hS + fl
                        for g4 in range(4):
                            gp = ft * 4 + g4
                            lg = gp - 8 * hS
                            mm(pys[:, fl * 128:(fl + 1) * 128], Cre_pad[:, gp, :], X_re[:, lg, :], g4 == 0, False, ["Cre_pad", "X_re"], [pysk])
                            mm(pys[:, fl * 128:(fl + 1) * 128], Cimn_pad[:, gp, :], X_im[:, lg, :], False, g4 == 3, ["Cimn_pad", "X_im"], [pysk])
                    for fl in range(2):
                        ft = 2 * hS + fl
                        stt(ys_f[:, fl * 128:(fl + 1) * 128], z_fm[:, 26 + ft, tsl], dsk[:, ft:ft + 1], pys[:, fl * 128:(fl + 1) * 128],
                            ALU.mult, ALU.add, [("z_fm", 26 + ft), "dsk", pysk], ["s5t2"])
                    hh = slice(0, 256)
                    act(ys_t[:, hh], ys_f[:, hh], AF.Square, ["s5t2"], ["s5t3"])
                    ts("pool", ys_t[:, hh], ys_t[:, hh], 0.044715, ALU.mult, ["s5t3"], ["s5t3"], s2=1.0, op1=ALU.add)
                    tt("pool", ys_t[:, hh], ys_t[:, hh], ys_f[:, hh], ALU.mult, ["s5t3", "s5t2"], ["s5t3"])
                    act(ys_s[:, hh], ys_t[:, hh], AF.Sigmoid, ["s5t3"], ["s5t0"], scale=1.5957691216057308)
                    tt("pool", gel_bf[:, 2 * hS:2 * hS + 2, :].rearrange("p a b -> p (a b)"), ys_s[:, hh], ys_f[:, hh], ALU.mult,
                       ["s5t0", "s5t2"], ["gel_bf"])
                    yield
                pg = [ps_alloc(), ps_alloc()]
                for hf in range(2):
                    for ft in range(4):
                        mm(pg[hf][0][:, :], gel_bf[:, ft, :], wglu_bf[:, ft, hf * 512:(hf + 1) * 512], ft == 0, ft == 3, ["gel_bf", "wglu_bf"], [pg[hf][1]])
                tt("dve", g1[:, :], pg[0][0][:, :], bglu_row[:, 0:512], ALU.add, [pg[0][1], "bglu_row"], ["s5t0"])
                tt("dve", g2[:, :], pg[1][0][:, :], bglu_row[:, 512:1024], ALU.add, [pg[1][1], "bglu_row"], ["s5t1"])
                act(g2[:, :], g2[:, :], AF.Sigmoid, ["s5t1"], ["s5t1"])
                tt("pool", g1[:, :], g1[:, :], g2[:, :], ALU.mult, ["s5t0", "s5t1"], ["s5t0"])
                tt("pool", yb_bf[:, :], g1[:, :], z_tm[:, c, 1024:1536], ALU.mult, ["s5t0", ("z_tm", "gb")], ["yb_bf"])
                yield
                pt, pk = ps_alloc()
                ptb = pt[:, :].bitcast(BF16)
                for kc in range(4):
                    S.add("pe", lambda e, ptb=ptb, kc=kc: e.transpose(ptb[:, kc * 128:(kc + 1) * 128], yb_bf[:, kc * 128:(kc + 1) * 128], ident[:, :]),
                          ["yb_bf", "ident"], [pk])
                cp("act", ybT[:, :, :].rearrange("p a b -> p (a b)"), ptb[:, 0:512], [pk], ["ybT"])

            def wkv_gen(hp):
                W = sets[hp % NSET]
                si = W["i"]
                K = lambda nm: (nm, si)
                r_ap = z_fm[:, hp, tsl]; k_ap = z_fm[:, 8 + hp, tsl]; v_ap = z_fm[:, 16 + hp, tsl]
                fsl = slice(hp * 128, (hp + 1) * 128)
                pq, pqk = ps_alloc()
                mm(pq[:, 0:128], wup_bf[:, fsl], txw[:, :], True, True, ["wup_bf", "txw"], [pqk])
                mm(pq[:, 128:256], aup_bf[:, fsl], xab[:, :], True, True, ["aup_bf", "xab"], [pqk])
                act(W["ld"][:, :], pq[:, 0:128], AF.Sigmoid, [pqk, "w0c"], [K("ld")], bias=w0c[:, hp:hp + 1])
                act(W["icl"][:, :], pq[:, 128:256], AF.Sigmoid, [pqk, "a0c"], [K("icl")], bias=a0c[:, hp:hp + 1])
                S.add("dve", lambda e, W=W: e.tensor_tensor_scan(out=W["cl"][:, :], data0=ones_f[:, :], data1=W["ld"][:, :],
                                                                 initial=0.0, op0=ALU.mult, op1=ALU.add),
                      [K("ld"), "ones_f"], [K("cl")])
                tt("pool", W["t1"][:, :], W["cl"][:, :], W["ld"][:, :], ALU.subtract, [K("cl"), K("ld")], [K("t1")])
                act(W["Wc"][:, :], W["cl"][:, :], AF.Exp, [K("cl")], [K("Wc")], scale=-C0)
                act(W["Wi"][:, :], W["cl"][:, :], AF.Exp, [K("cl")], [K("Wi")], scale=C0)
                act(W["Wx"][:, :], W["t1"][:, :], AF.Exp, [K("t1")], [K("Wx")], scale=-C0)
                cp("act", W["wl"][:, :], W["Wc"][:, 127:128], [K("Wc")], [K("wl")])
                yield
                act(W["kk2"][:, :], k_ap, AF.Square, [("z_fm", 8 + hp), "kkc"], [K("kk2")], scale=kkc[:, hp:hp + 1])
                pn, pnk = ps_alloc()
                mm(pn[:, 0:128], blk1[:, :], W["kk2"][:, :], True, True, ["blk1", K("kk2")], [pnk])
                ts("dve", W["t2"][:, :], pn[:, 0:128], 1e-24, ALU.max, [pnk], [K("t2")])
                act(W["t2"][:, :], W["t2"][:, :], AF.Ln, [K("t2")], [K("t2")])
                act(W["t2"][:, :], W["t2"][:, :], AF.Exp, [K("t2")], [K("t2")], scale=-0.5)
                stt(W["kkn"][:, :], k_ap, kkc[:, hp:hp + 1], W["t2"][:, :], ALU.mult, ALU.mult,
                    [("z_fm", 8 + hp), "kkc", K("t2")], [K("kkn")])
                yield
                ts("pool", W["t1"][:, :], W["icl"][:, :], -1.0, ALU.add, [K("icl"), "kac"], [K("t1")],
                   s2=kac[:, hp:hp + 1], op1=ALU.mult)
                stt(W["kh"][:, :], W["t1"][:, :], 1.0, k_ap, ALU.add, ALU.mult, [K("t1"), ("z_fm", 8 + hp)], [K("kh")])
                stt(W["a_p"][:, :], W["kkn"][:, :], -1.0, W["Wx"][:, :], ALU.mult, ALU.mult, [K("kkn"), K("Wx")], [K("a_p")])
                tt("pool", W["bt"][:, :], W["kkn"][:, :], W["icl"][:, :], ALU.mult, [K("kkn"), K("icl")], [K("bt")])
                tt("dve", W["bt"][:, :], W["bt"][:, :], W["Wi"][:, :], ALU.mult, [K("bt"), K("Wi")], [K("bt")])
                tt("pool", W["t2"][:, :], W["kh"][:, :], W["Wi"][:, :], ALU.mult, [K("kh"), K("Wi")], [K("t2")])
                for h2 in range(2):
                    hs = slice(h2 * 64, (h2 + 1) * 64)
                    cp("act", W["b_bd"][hs, h2, :], W["bt"][hs, :], [K("bt")], [("b_bd", si)])
                    cp("act", W["k_bd"][hs, h2, :], W["t2"][hs, :], [K("t2")], [("k_bd", si)])
                ts("dve", W["bh_p"][:, :], W["bt"][:, :], W["wl"][:, 0:1], ALU.mult, [K("bt"), K("wl")], [K("bh_p")])
                ts("pool", W["kh_p"][:, :], W["t2"][:, :], W["wl"][:, 0:1], ALU.mult, [K("t2"), K("wl")], [K("kh_p")])
                tt("dve", W["r_p"][:, :], r_ap, W["Wc"][:, :], ALU.mult, [("z_fm", hp), K("Wc")], [K("r_p")])
                stt(W["rkp"][:, :], r_ap, rkc[:, hp:hp + 1], W["kh"][:, :], ALU.mult, ALU.mult,
                    [("z_fm", hp), "rkc", K("kh")], [K("rkp")])
                mm(rk_pt[:, hp * 2:hp * 2 + 2], W["rkp"][:, :], hsel[:, :], True, True, [K("rkp"), "hsel"], [rk_pk])
                yield
                ptm, ptk = ps_alloc()
                ptmb = ptm[:, :].bitcast(BF16)
                for j, (src, key) in enumerate(((W["a_p"][:, :], K("a_p")), (W["bh_p"][:, :], K("bh_p")),
                                                (W["kh_p"][:, :], K("kh_p")), (v_ap, ("z_fm", 16 + hp)))):
                    S.add("pe", lambda e, ptmb=ptmb, j=j, src=src: e.transpose(ptmb[:, j * 128:(j + 1) * 128], src, ident[:, :]),
                          [key, "ident"], [ptk])
                cp("act", W["tm4"][:, :, :].rearrange("p a b -> p (a b)"), ptmb[:, 0:512], [ptk], [K("tm4")])
                cp("act", Vall[:, hp, :], W["tm4"][:, 3, :], [K("tm4")], [("Vall", hp)])
                yield
                pa1, pa1k = ps_alloc()
                pa2, pa2k = ps_alloc()
                for h2 in range(2):
                    mm(pa1[:, h2 * 128:(h2 + 1) * 128], W["b_bd"][:, h2, :], W["a_p"][:, :], True, True, [("b_bd", si), K("a_p")], [pa1k])
                    mm(pa1[:, 256 + h2 * 128:256 + (h2 + 1) * 128], W["k_bd"][:, h2, :], W["a_p"][:, :], True, True, [("k_bd", si), K("a_p")], [pa1k])
                    mm(pa2[:, h2 * 128:(h2 + 1) * 128], W["b_bd"][:, h2, :], W["r_p"][:, :], True, True, [("b_bd", si), K("r_p")], [pa2k])
                    mm(pa2[:, 256 + h2 * 128:256 + (h2 + 1) * 128], W["k_bd"][:, h2, :], W["r_p"][:, :], True, True, [("k_bd", si), K("r_p")], [pa2k])
                msu_b = bc_mid(m_su[:, :], 2); miu_b = bc_mid(m_iu[:, :], 2)
                v2 = lambda ap: ap.rearrange("p (a b) -> p a b", b=128)
                tt("dve", W["Pm"][:, :, :], v2(pa1[:, 0:256]), msu_b, ALU.mult, [pa1k, "m_su"], [K("Pm")])
                tt("dve", W["AakT"][:, :, :], v2(pa1[:, 256:512]), msu_b, ALU.mult, [pa1k, "m_su"], [K("AakT")])
                tt("dve", W["ArbT"][:, :, :], v2(pa2[:, 0:256]), miu_b, ALU.mult, [pa2k, "m_iu"], [K("ArbT")])
                tt("dve", W["ArkT"][:, :, :], v2(pa2[:, 256:512]), miu_b, ALU.mult, [pa2k, "m_iu"], [K("ArkT")])
                yield
                pq0, pq0k = ps_alloc()
                pq0b = pq0[:, :].bitcast(BF16)
                for h2 in range(2):
                    S.add("pe", lambda e, pq0b=pq0b, h2=h2, W=W: e.transpose(pq0b[:, h2 * 128:(h2 + 1) * 128], W["Pm"][:, h2, :], ident[:, :]),
                          [K("Pm"), "ident"], [pq0k])
                cp("act", W["Qm"][:, :, :].rearrange("p a b -> p (a b)"), pq0b[:, 0:256], [pq0k], [K("Qm")])
                tt("pool", W["Zm"][:, :, :], W["Pm"][:, :, :], bc_mid(ident[:, :], 2), ALU.add, [K("Pm"), "ident"], [K("Zm")])
                yield
                cur = ("Pm", "Qm", "Zm"); nxt = ("Pn", "Qn", "Zn")
                for lvl in range(1, 7):
                    pl, plk = ps_alloc()
                    last = lvl == 6
                    for h2 in range(2):
                        if not last:
                            mm(pl[:, h2 * 128:(h2 + 1) * 128], W[cur[1]][:, h2, :], W[cur[0]][:, h2, :], True, True,
                               [K(cur[0]), K(cur[1])], [plk])
                        mm(pl[:, 256 + h2 * 128:256 + (h2 + 1) * 128], W[cur[0]][:, h2, :], W[cur[1]][:, h2, :], True, True,
                           [K(cur[0]), K(cur[1])], [plk])
                    if not last:
                        cp("act", W[nxt[0]][:, :, :].rearrange("p a b -> p (a b)"), pl[:, 0:256], [plk], [K(nxt[0])])
                    cp("act", W[nxt[1]][:, :, :].rearrange("p a b -> p (a b)"), pl[:, 256:512], [plk], [K(nxt[1])])
                    yield
                    pz, pzk = ps_alloc()
                    for h2 in range(2):
                        mm(pz[:, h2 * 128:(h2 + 1) * 128], W[nxt[1]][:, h2, :], W[cur[2]][:, h2, :], True, True,
                           [K(nxt[1]), K(cur[2])], [pzk])
                    tt("dve", W[nxt[2]][:, :, :], v2(pz[:, 0:256]), W[cur[2]][:, :, :], ALU.add, [pzk, K(cur[2])], [K(nxt[2])])
                    cur, nxt = nxt, cur
                    yield
                Tt = W[cur[2]]; Tk = K(cur[2])
                pap, papk = ps_alloc()
                for h2 in range(2):
                    mm(pap[:, h2 * 128:(h2 + 1) * 128], W["tm4"][:, 0, :], Tt[:, h2, :], True, True, [K("tm4"), Tk], [papk])
                for h2 in range(2):
                    hs = slice(h2 * 64, (h2 + 1) * 64)
                    cp("act", W["ApT"][hs, :], pap[hs, h2 * 128:(h2 + 1) * 128], [papk], [K("ApT")])
                yield
                pw, pwk = ps_alloc()
                for h2 in range(2):
                    mm(pw[:, h2 * 64:(h2 + 1) * 64], W["AakT"][:, h2, :], W["tm4"][:, 3, h2 * 64:(h2 + 1) * 64], True, True,
                       [K("AakT"), K("tm4")], [pwk])
                cp("act", W["W1"][:, :, :].rearrange("p a b -> p (a b)"), pw[:, 0:128], [pwk], [K("W1")])
                yield
                pv, pvk = ps_alloc()
                for h2 in range(2):
                    mm(pv[:, h2 * 64:(h2 + 1) * 64], Tt[:, h2, :], W["W1"][:, h2, :], True, True, [Tk, K("W1")], [pvk])
                cp("act", W["Vu"][:, :, :].rearrange("p a b -> p (a b)"), pv[:, 0:128], [pvk], [K("Vu")])
                yield
                p1, p1k = ps_alloc()
                mm(p1[:, 0:128], W["ApT"][:, :], S_bd[:, hp, :], True, True, [K("ApT"), ("S_bd", hp)], [p1k])
                tt("dve", W["U"][:, :], p1[:, 0:128], W["Vu"][:, :, :].rearrange("p a b -> p (a b)"), ALU.add, [p1k, K("Vu")], [K("U")])
                yield
                ypt, ypk = y_pt[hp // 4]
                for h2 in range(2):
                    col = (hp % 4) * 128 + h2 * 64
                    hsl = slice(h2 * 64, (h2 + 1) * 64)
                    mm(ypt[:, col:col + 64], W["r_p"][:, :], S_bd[:, hp, hsl], True, False, [K("r_p"), ("S_bd", hp)], [ypk])
                    mm(ypt[:, col:col + 64], W["ArbT"][:, h2, :], W["U"][:, hsl], False, False, [K("ArbT"), K("U")], [ypk])
                    mm(ypt[:, col:col + 64], W["ArkT"][:, h2, :], W["tm4"][:, 3, hsl], False, True, [K("ArkT"), K("tm4")], [ypk])
                p2, p2k = ps_alloc()
                mm(p2[:, 0:128], W["tm4"][:, 1, :], W["U"][:, :], True, False, [K("tm4"), K("U")], [p2k])
                mm(p2[:, 0:128], W["tm4"][:, 2, :], W["tm4"][:, 3, :], False, True, [K("tm4")], [p2k])
                for h2 in range(2):
                    hs = slice(h2 * 64, (h2 + 1) * 64)
                    stt(S_f[hs, hp, :], S_f[hs, hp, :], W["wl"][hs, 0:1], p2[hs, h2 * 64:(h2 + 1) * 64], ALU.mult, ALU.add,
                        [("S_f", hp), K("wl"), p2k], [("S_f", hp)])
                    cp("act", S_bd[hs, hp, h2 * 64:(h2 + 1) * 64], S_f[hs, hp, :], [("S_f", hp)], [("S_bd", hp)])
            def run_rr(gens):
                gens = list(gens)
                while gens:
                    for g_ in list(gens):
                        try:
                            next(g_)
                        except StopIteration:
                            gens.remove(g_)
            s5g = s5_gen()
            s5_live = [True]

            def paced(n_every, fn):
                k_ = 0
                while True:
                    k_ += 1
                    if k_ % n_every == 0:
                        fn()
                    yield

            def s5_one():
                if s5_live[0]:
                    try:
                        next(s5g)
                    except StopIteration:
                        s5_live[0] = False
            for hp0 in range(0, 8, NSET):
                gens = [wkv_gen(hp0 + j) for j in range(NSET)]
                sp_ = paced(7, s5_one)
                while gens:
                    for g_ in list(gens):
                        try:
                            next(g_)
                        except StopIteration:
                            gens.remove(g_)
                    next(sp_)
                    step_pending(1)
                flush_pending()
            while s5_live[0]:
                s5_one()
            def tail_gen(c, tok0, st):
                ykeys = [y_pt[0][1], y_pt[1][1]]
                cp("act", ysb[:, 0:512], y_pt[0][0][:, :], [ykeys[0]], ["ysb"])
                cp("act", ysb[:, 512:1024], y_pt[1][0][:, :], [ykeys[1]], ["ysb"])
                cp("act", rk_sb[:, :], rk_pt[:, 0:16], [rk_pk], ["rk_sb"])
                y3 = ysb[:, :].rearrange("p (h i) -> p h i", i=64)
                S.add("dve", lambda e, y3=y3: e.tensor_reduce(out=gst[:, 0, :], in_=y3, axis=AX.X, op=ALU.add), ["ysb"], ["gst0"])
                act(ysq[:, :], ysb[:, :], AF.Square, ["ysb"], ["ytmp"])
                S.add("dve", lambda e: e.tensor_reduce(out=gst[:, 1, :], in_=ysq[:, :].rearrange("p (h i) -> p h i", i=64), axis=AX.X, op=ALU.add),
                      ["ytmp"], ["gst1"])
                tt("dve", ytmp[:, :].rearrange("p (h i) -> p h i", i=64), Vall[:, :, :].rearrange("p a (b i) -> p (a b) i", i=64),
                   bc_last(rk_sb[:, :], 64), ALU.mult, [("Vall", h) for h in range(8)] + ["rk_sb"], ["ytmp"])
                yield
                ts("dve", gst[:, 0, :], gst[:, 0, :], 1.0 / 64, ALU.mult, ["gst0"], ["gst0"])
                tt("dve", gst[:, 2, :], gst[:, 0, :], gst[:, 0, :], ALU.mult, ["gst0"], ["gst2"])
                stt(gst[:, 1, :], gst[:, 1, :], 1.0 / 64, gst[:, 2, :], ALU.mult, ALU.subtract, ["gst1", "gst2"], ["gst1"])
                ts("dve", gst[:, 1, :], gst[:, 1, :], 64e-5, ALU.add, ["gst1"], ["gst1"])
                act(gst[:, 1, :], gst[:, 1, :], AF.Ln, ["gst1"], ["gst1"])
                act(gst[:, 1, :], gst[:, 1, :], AF.Exp, ["gst1"], ["gst1"], scale=-0.5)
                tt("dve", y3, y3, bc_last(gst[:, 0, :], 64), ALU.subtract, ["ysb", "gst0"], ["ysb"])
                yield
                tt("dve", y3, y3, bc_last(gst[:, 1, :], 64), ALU.mult, ["ysb", "gst1"], ["ysb"])
                tt("pool", ysb[:, :], ysb[:, :], lnxg_row[:, :], ALU.mult, ["ysb", "lnxg_row"], ["ysb"])
                yield
                tt("pool", ysb[:, :], ysb[:, :], lnxb_row[:, :], ALU.add, ["ysb", "lnxb_row"], ["ysb"])
                tt("dve", ysb[:, :], ysb[:, :], ytmp[:, :], ALU.add, ["ysb", "ytmp"], ["ysb"])
                tt("dve", ya_bf[:, :], ysb[:, :], z_tm[:, c, 0:1024], ALU.mult, ["ysb", ("z_tm", "ga")], ["ya_bf"])
                yield
                pt, pk = ps_alloc()
                ptb = pt[:, :].bitcast(BF16)
                for kc in range(8):
                    S.add("pe", lambda e, ptb=ptb, kc=kc: e.transpose(ptb[:, kc * 128:(kc + 1) * 128], ya_bf[:, kc * 128:(kc + 1) * 128], ident[:, :]),
                          ["ya_bf", "ident"], [pk])
                cp("act", yaT[:, :, :].rearrange("p a b -> p (a b)"), ptb[:, :], [pk], ["yaT"])
                yield
                for half in range(2):
                    po, pok = ps_alloc()
                    for kc in range(8):
                        mm(po[:, :], yaT[:, kc, :], pa_bf[:, kc, half * 512:(half + 1) * 512], kc == 0, kc == 7, ["yaT", "pa_bf"], [pok])
                    tt("dve", merged_a[:, half * 512:(half + 1) * 512], po[:, :], z_tm[:, c, 1536 + half * 512:1536 + (half + 1) * 512],
                       ALU.mult, [pok, ("z_tm", "ma")], ["ya_bf"])
                    yield
                for half in range(2):
                    po, pok = ps_alloc()
                    hsl = slice(half * 512, (half + 1) * 512)
                    for kc in range(4):
                        mm(po[:, :], ybT[:, kc, :], pb_bf[:, kc, hsl], kc == 0, kc == 3, ["ybT", "pb_bf"], [pok])
                    tt("dve", ysb[:, hsl], po[:, :], z_tm[:, c, 2560 + half * 512:2560 + (half + 1) * 512], ALU.mult, [pok, ("z_tm", "mb")], ["ysb"])
                    tt("pool", merged_bf[:, hsl], ysb[:, hsl], merged_a[:, hsl], ALU.add, ["ysb", "ya_bf"], ["merged_bf"])
                    yield
                pt, pk = ps_alloc()
                ptb = pt[:, :].bitcast(BF16)
                for kc in range(8):
                    S.add("pe", lambda e, ptb=ptb, kc=kc: e.transpose(ptb[:, kc * 128:(kc + 1) * 128], merged_bf[:, kc * 128:(kc + 1) * 128], ident[:, :]),
                          ["merged_bf", "ident"], [pk])
                cp("act", mT[:, :, :].rearrange("p a b -> p (a b)"), ptb[:, :], [pk], ["yaT"])
                dma("pool", xres[0][:, :], x_d[tok0:tok0 + 128, :], [], ["ytmp"])
                yield
                for half in range(2):
                    po, pok = ps_alloc()
                    hsl = slice(half * 512, (half + 1) * 512)
                    for kc in range(8):
                        mm(po[:, :], mT[:, kc, :], wout_bf[:, kc, hsl], kc == 0, kc == 7, ["yaT", "wout_bf"], [pok])
                    tt("dve", xo[:, hsl], po[:, :], xres[0][:, hsl], ALU.add, [pok, "ytmp"], ["ysb"])
                    yield
                act(junk_bf[:, :], xo[:, :], AF.Square, ["ysb"], ["merged_bf", "ssq0"], accum=ssq[:, 0:1])
                ts("dve", ssq[:, 1:2], ssq[:, 0:1], 1.0 / D, ALU.mult, ["ssq0"], ["ssq1"], s2=1e-6, op1=ALU.add)
                act(ssq[:, 2:3], ssq[:, 1:2], AF.Ln, ["ssq1"], ["ssq2"])
                act(ssq[:, 3:4], ssq[:, 2:3], AF.Exp, ["ssq2"], ["ssq3"], scale=-0.5)
                stt(outt[:, :], xo[:, :], ssq[:, 3:4], fg_row[:, :], ALU.mult, ALU.mult, ["ysb", "ssq3", "fg_row"], ["ytmp"])
                ok = ("out", st, c)
                dma("sp", out_d[tok0:tok0 + 128, :], outt[:, :], ["ytmp"], [ok])
                out_keys.append(ok)

            flush_pending()
            tg = tail_gen(c, tok0, st)
            next(tg)
            pending.append(tg)
    flush_pending()
    S.add("sp", None, out_keys, [])
    print("SBUF bytes/partition:", sb_bytes[0])
    S.emit(nc, es)
    es.close()
    return nc


_NC_CACHE = {}


def kernel(**inputs):
    T = 8192
    if T not in _NC_CACHE:
        _NC_CACHE[T] = build_program(T)
    nc = _NC_CACHE[T]
    f = lambda a, shp: np.ascontiguousarray(np.asarray(a, dtype=np.float32).reshape(shp))
    common = {
        "norm_g": f(inputs["norm_g"], (1, D)), "w_in": f(inputs["w_in"], (D, 7296)),
        "mu_shift": f(inputs["mu_shift"], (1, 3200)), "w0": f(inputs["w0"], (1, D)),
        "w_up": f(inputs["w_up"], (64, D)), "a0": f(inputs["a0"], (1, D)), "a_up": f(inputs["a_up"], (64, D)),
        "k_k": f(inputs["k_k"], (1, D)), "k_a": f(inputs["k_a"], (1, D)), "r_k": f(inputs["r_k"], (1, D)),
        "lnx_g": f(inputs["lnx_g"], (1, D)), "lnx_b": f(inputs["lnx_b"], (1, D)),
        "lam_re": f(inputs["lam_re"], (32, 64)), "lam_im": f(inputs["lam_im"], (32, 64)),
        "log_dt": f(inputs["log_dt"], (32, 1)),
        "b_re": f(inputs["b_re"], (32, 1024)), "b_im": f(inputs["b_im"], (32, 1024)),
        "c_re": f(inputs["c_re"], (512, 64)), "c_im": f(inputs["c_im"], (512, 64)),
        "d_skip": f(inputs["d_skip"], (1, 512)), "w_glu": f(inputs["w_glu"], (512, D)),
        "b_glu": f(inputs["b_glu"], (1, D)), "p_a": f(inputs["p_a"], (D, D)), "p_b": f(inputs["p_b"], (512, D)),
        "w_out": f(inputs["w_out"], (D, D)), "final_g": f(inputs["final_g"], (1, D)),
    }
    x = np.asarray(inputs["x"], dtype=np.float32)
    in_maps = []
    for c in range(8):
        m = dict(common)
        m["x"] = np.ascontiguousarray(x[c % 4])
        in_maps.append(m)
    res = run_bass_kernel_spmd(nc, in_maps, core_ids=list(range(8)))
    out = np.stack([np.asarray(res.results[b]["out"], dtype=np.float32) for b in range(4)], axis=0)
    return out
```

```python
import numpy as np
from contextlib import ExitStack
import concourse.bass as bass
import concourse.mybir as mybir
from concourse.bass_utils import run_bass_kernel_spmd

F32, BF16, I32 = mybir.dt.float32, mybir.dt.bfloat16, mybir.dt.int32
AF = mybir.ActivationFunctionType
ALU = mybir.AluOpType
AX = mybir.AxisListType

D = 1024
NCT = 58
C0 = 0.6065306597126334
TWO_PI = 6.283185307179586
SAME_ENG_SYNC = {'pe': False, 'act': True, 'dve': True, 'pool': True, 'sp': True}
MAXV = 30000
NSLOT = 6


class Sched:
    ENGS = ["pe", "act", "dve", "pool", "sp"]
    NINST = 0

    def __init__(self):
        self.ops = []
        self.last_w = {}
        self.readers = {}

    def add(self, eng, fn, r=(), w=(), dma=False):
        idx = len(self.ops)
        deps = set()
        for k in r:
            if k in self.last_w:
                deps.add(self.last_w[k])
        for k in w:
            if k in self.last_w:
                deps.add(self.last_w[k])
            deps.update(self.readers.get(k, ()))
        for k in r:
            self.readers.setdefault(k, []).append(idx)
        for k in w:
            self.last_w[k] = idx
            self.readers[k] = []
        deps.discard(idx)
        self.ops.append(dict(eng=eng, fn=fn, deps=sorted(deps), dma=dma, marked=False))
        return idx

    def emit(self, nc, es, drain=False):
        ops = self.ops
        for o in ops:
            for d in o["deps"]:
                od = ops[d]
                if od["dma"]:
                    od["marked"] = True
                elif od["eng"] != o["eng"]:
                    od["marked"] = True
                elif SAME_ENG_SYNC[o["eng"]]:
                    od["marked"] = True
                elif o["dma"]:
                    od["marked"] = True
        cnt = {e: 0 for e in self.ENGS}
        slot_cnt = {e: 0 for e in self.ENGS}
        slot_val = {}
        nsem_needed = {e: 0 for e in self.ENGS}
        for o in ops:
            e = o["eng"]
            if o["dma"]:
                s = slot_cnt[e] % NSLOT
                slot_cnt[e] += 1
                key = ("dma", e, s)
                prev = slot_val.get(key, 0)
                o["prev_wait"] = (key, prev) if prev > 0 else None
                slot_val[key] = prev + 16
                o["sig"] = (key, prev + 16)
            elif o["marked"]:
                cnt[e] += 1
                ep = (cnt[e] - 1) // MAXV
                o["sig"] = (("c", e, ep), (cnt[e] - 1) % MAXV + 1)
                nsem_needed[e] = max(nsem_needed[e], ep + 1)
        sems = {}
        for e in self.ENGS:
            for ep in range(nsem_needed[e]):
                sems[("c", e, ep)] = es.enter_context(nc.semaphore(f"s{Sched.NINST}_{e}_{ep}"))
            if slot_cnt[e] > 0:
                for s in range(min(NSLOT, slot_cnt[e])):
                    sems[("dma", e, s)] = es.enter_context(nc.semaphore(f"sd{Sched.NINST}_{e}_{s}"))
        Sched.NINST += 1
        block_cm = nc.Block()
        block = block_cm.__enter__()
        regs = {"pe": block.tensor, "act": block.scalar, "dve": block.vector,
                "pool": block.gpsimd, "sp": block.sync}
        streams = {e: [o for o in ops if o["eng"] == e] for e in self.ENGS}

        def make(e_name):
            def body(e):
                known = {}
                for o in streams[e_name]:
                    waits = {}
                    for d in o["deps"]:
                        od = ops[d]
                        if not od["marked"]:
                            continue
                        if od["eng"] == e_name and not od["dma"]:
                            if not SAME_ENG_SYNC[e_name] and not o["dma"]:
                                continue
                        k, v = od["sig"]
                        waits[k] = max(waits.get(k, 0), v)
                    if o["dma"] and o["prev_wait"] is not None:
                        k, v = o["prev_wait"]
                        waits[k] = max(waits.get(k, 0), v)
                    for k, v in waits.items():
                        if known.get(k, 0) >= v:
                            continue
                        e.wait_ge(sems[k], v)
                        known[k] = v
                    if o["fn"] is None:
                        continue
                    ins = o["fn"](e)
                    if o["dma"]:
                        k, v = o["sig"]
                        ins.then_inc(sems[k], 16)
                    elif o["marked"]:
                        k, v = o["sig"]
                        ins.then_inc(sems[k], 1)
                if drain and e_name == "sp":
                    for k, v in slot_val.items():
                        e.wait_ge(sems[k], v)
            return body
        for e_name in self.ENGS:
            if streams[e_name]:
                regs[e_name](make(e_name))
        block_cm.__exit__(None, None, None)


def s5_bounded(g, live, n):
    for _ in range(n):
        if not live[0]:
            return
        try:
            next(g)
        except StopIteration:
            live[0] = False
            return
        yield


def build_program(T, debug=False):
    assert T % 256 == 0
    NST = T // 256
    nc = bass.Bass("TRN2", target_bir_lowering=False)
    S = Sched()
    es = ExitStack()

    def din(name, shape, dt=F32):
        return nc.dram_tensor(name, list(shape), dt, kind="ExternalInput").ap()

    x_d = din("x", [T, D])
    norm_g_d = din("norm_g", [1, D])
    w_in_d = din("w_in", [D, 7296])
    mu_d = din("mu_shift", [1, 3200])
    w0_d = din("w0", [1, D]); w_up_d = din("w_up", [64, D])
    a0_d = din("a0", [1, D]); a_up_d = din("a_up", [64, D])
    k_k_d = din("k_k", [1, D]); k_a_d = din("k_a", [1, D]); r_k_d = din("r_k", [1, D])
    lnx_g_d = din("lnx_g", [1, D]); lnx_b_d = din("lnx_b", [1, D])
    lam_re_d = din("lam_re", [32, 64]); lam_im_d = din("lam_im", [32, 64]); log_dt_d = din("log_dt", [32, 1])
    b_re_d = din("b_re", [32, 1024]); b_im_d = din("b_im", [32, 1024])
    c_re_d = din("c_re", [512, 64]); c_im_d = din("c_im", [512, 64])
    d_skip_d = din("d_skip", [1, 512])
    w_glu_d = din("w_glu", [512, D]); b_glu_d = din("b_glu", [1, D])
    p_a_d = din("p_a", [D, D]); p_b_d = din("p_b", [512, D]); w_out_d = din("w_out", [D, D])
    final_g_d = din("final_g", [1, D])
    out_d = nc.dram_tensor("out", [T, D], F32, kind="ExternalOutput").ap()
    wbf_d = nc.dram_tensor("wbf", [NCT, 128, 8 * 128], BF16, kind="Internal").ap()
    s_tab_d = nc.dram_tensor("s_tab", [2, 2048], F32, kind="Internal").ap()
    s_bb_d = nc.dram_tensor("s_bb", [2, 32, 64, 16], F32, kind="Internal").ap()
    s_bm_d = nc.dram_tensor("s_bm", [512, 1024], F32, kind="Internal").ap()
    s_ct_d = nc.dram_tensor("s_ct", [2, 64, 512], F32, kind="Internal").ap()
    dbg = {}

    sb_bytes = [0]

    sb_cache = {}

    def sb(name, shape, dt=F32):
        if name in sb_cache:
            return sb_cache[name]
        n = 1
        for d_ in shape[1:]:
            n *= d_
        sb_bytes[0] += n * (2 if dt == BF16 else 4)
        t_ = es.enter_context(nc.sbuf_tensor(name, list(shape), dt))
        sb_cache[name] = t_
        return t_

    for (n_, sh_, dt_) in [("ones_f", [128, 128], F32), ("ident", [128, 128], BF16), ("m_su", [128, 128], BF16),
                           ("m_iu", [128, 128], BF16), ("blk1", [128, 128], BF16), ("hsel", [128, 2], BF16),
                           ("mu_col", [128, 26], F32), ("w0c", [128, 8], F32), ("a0c", [128, 8], F32), ("kkc", [128, 8], F32),
                           ("kac", [128, 8], F32), ("rkc", [128, 8], F32), ("dsk", [128, 4], F32),
                           ("pa_bf", [128, 8, D], BF16), ("pb_bf", [128, 4, D], BF16), ("wglu_bf", [128, 4, D], BF16),
                           ("wout_bf", [128, 8, D], BF16), ("wup_bf", [64, D], BF16), ("aup_bf", [64, D], BF16),
                           ("lnxg_row", [128, D], BF16), ("lnxb_row", [128, D], BF16), ("bglu_row", [128, D], BF16),
                           ("fg_row", [128, D], F32),
                           ("E_re", [128, 2048], BF16), ("E_im", [128, 2048], BF16), ("D_re", [128, 16, 128], BF16),
                           ("D_im", [128, 16, 128], BF16), ("L_re", [128, 16], F32), ("L_im", [128, 16], F32),
                           ("Cre_pad", [128, 16, 128], BF16), ("Cimn_pad", [128, 16, 128], BF16), ("Bm_bd", [128, 4, 1024], BF16),
                           ("xe_re", [128, 16], F32), ("xe_im", [128, 16], F32)]:
        sb(n_, sh_, dt_)

    es_setup = ExitStack()

    def sbt(name, shape, dt=F32):
        return es_setup.enter_context(nc.sbuf_tensor(name, list(shape), dt))

    def mm(out, lhsT, rhs, start, stop, r, w):
        S.add("pe", lambda e: e.matmul(out, lhsT=lhsT, rhs=rhs, start=start, stop=stop), r, w)

    def act(out, in_, func, r, w, bias=None, scale=None, accum=None, eng="act"):
        kw = {}
        if bias is not None: kw["bias"] = bias
        if scale is not None: kw["scale"] = scale
        if accum is not None: kw["accum_out"] = accum
        S.add("act", lambda e: e.activation(out=out, in_=in_, func=func, **kw), r, w)

    def tt(eng, out, in0, in1, op, r, w):
        S.add(eng, lambda e: e.tensor_tensor(out=out, in0=in0, in1=in1, op=op), r, w)

    def ts(eng, out, in0, s1, op0, r, w, s2=None, op1=None):
        if op1 is None:
            S.add(eng, lambda e: e.tensor_scalar(out=out, in0=in0, scalar1=s1, scalar2=None, op0=op0), r, w)
        else:
            S.add(eng, lambda e: e.tensor_scalar(out=out, in0=in0, scalar1=s1, scalar2=s2, op0=op0, op1=op1), r, w)

    def stt(out, in0, scalar, in1, op0, op1, r, w):
        S.add("dve", lambda e: e.scalar_tensor_tensor(out=out, in0=in0, scalar=scalar, in1=in1, op0=op0, op1=op1), r, w)

    def cp(eng, out, in_, r, w):
        if eng == "act":
            S.add("act", lambda e: e.copy(out=out, in_=in_), r, w)
        else:
            S.add(eng, lambda e: e.tensor_copy(out=out, in_=in_), r, w)

    def dma(q, out, in_, r, w, slow=False):
        if slow:
            S.add(q, lambda e: e.dma_start(out=out, in_=in_, allow_slow_non_contiguous=True), r, w, dma=True)
        else:
            S.add(q, lambda e: e.dma_start(out=out, in_=in_), r, w, dma=True)

    def memset(eng, ap, val, w):
        S.add(eng, lambda e: e.memset(ap, val), (), w)

    def bc_last(ap2d, n):
        return ap2d.unsqueeze(2).to_broadcast([ap2d.shape[0], ap2d.shape[1], n])

    def bc_mid(ap2d, n):
        return ap2d.unsqueeze(1).to_broadcast([ap2d.shape[0], n, ap2d.shape[1]])

    psb = [es.enter_context(nc.psum_tensor(f"psb{i}", [128, 512], F32)) for i in range(8)]
    ps_rr = [0]

    def ps_alloc():
        i = 3 + ps_rr[0] % 5
        ps_rr[0] += 1
        return psb[i], ("ps", i)

    ones_f = sb("ones_f", [128, 128]); ident_f = sbt("ident_f", [128, 128])
    ident = sb("ident", [128, 128], BF16)
    m_su = sb("m_su", [128, 128], BF16)
    m_iu = sb("m_iu", [128, 128], BF16)
    blk1 = sb("blk1", [128, 128], BF16)
    hsel = sb("hsel", [128, 2], BF16)
    tmpc = sbt("tmpc", [128, 128])
    memset("pool", ones_f[:, :], 1.0, ["ones_f"])

    def aff(out_ap, pattern, cmp, base, cm, key, src=None):
        src_ap = ones_f[:, 0:out_ap.shape[1]] if src is None else src
        S.add("pool", lambda e: e.affine_select(out=out_ap, in_=src_ap, pattern=pattern, compare_op=cmp,
                                                fill=0.0, base=base, channel_multiplier=cm),
              ["ones_f"], [key])

    aff(ident_f[:, :], [[-1, 128]], ALU.is_equal, 0, 1, "ident_f")
    cp("pool", ident[:, :], ident_f[:, :], ["ident_f"], ["ident"])
    aff(tmpc[:, :], [[1, 128]], ALU.is_gt, 0, -1, "tmpc")
    cp("pool", m_su[:, :], tmpc[:, :], ["tmpc"], ["m_su"])
    aff(tmpc[:, :], [[1, 128]], ALU.is_ge, 0, -1, "tmpc")
    cp("pool", m_iu[:, :], tmpc[:, :], ["tmpc"], ["m_iu"])
    aff(tmpc[:, 0:64], [[0, 64]], ALU.is_ge, 63, -1, "tmpc")
    aff(tmpc[:, 64:128], [[0, 64]], ALU.is_ge, -64, 1, "tmpc")
    cp("pool", blk1[:, :], tmpc[:, :], ["tmpc"], ["blk1"])
    cp("pool", hsel[:, 0:1], tmpc[:, 0:1], ["tmpc"], ["hsel"])
    cp("pool", hsel[:, 1:2], tmpc[:, 64:65], ["tmpc"], ["hsel"])

    mu_col = sb("mu_col", [128, 26])
    w0c = sb("w0c", [128, 8]); a0c = sb("a0c", [128, 8]); kkc = sb("kkc", [128, 8])
    kac = sb("kac", [128, 8]); rkc = sb("rkc", [128, 8]); dsk = sb("dsk", [128, 4]); gcol = sbt("gcol", [128, 8])
    dma("sp", mu_col[:, 0:24], mu_d[0, 0:3072].rearrange("(c p) -> p c", p=128), [], ["mu_col"], slow=True)
    dma("sp", mu_col[0:64, 24:26], mu_d[0, 3072:3200].rearrange("(c p) -> p c", p=64), [], ["mu_col"], slow=True)
    for (t_, d_, n_) in [(w0c, w0_d, "w0c"), (a0c, a0_d, "a0c"), (kkc, k_k_d, "kkc"), (kac, k_a_d, "kac"),
                         (rkc, r_k_d, "rkc"), (gcol, norm_g_d, "gcol")]:
        dma("sp", t_[:, :], d_[0, :].rearrange("(c p) -> p c", p=128), [], [n_], slow=True)
    dma("sp", dsk[:, :], d_skip_d[0, :].rearrange("(c p) -> p c", p=128), [], ["dsk"], slow=True)
    stage = [sbt(f"stage{i}", [128, 1024]) for i in range(2)]
    stage_bf = [sbt(f"stagebf{i}", [128, 1024], BF16) for i in range(2)]
    st_rr = [0]

    def stage_next():
        i = st_rr[0] % 2
        st_rr[0] += 1
        return i

    pa_bf = sb("pa_bf", [128, 8, D], BF16); pb_bf = sb("pb_bf", [128, 4, D], BF16)
    wglu_bf = sb("wglu_bf", [128, 4, D], BF16); wout_bf = sb("wout_bf", [128, 8, D], BF16)
    wup_bf = sb("wup_bf", [64, D], BF16); aup_bf = sb("aup_bf", [64, D], BF16)

    def load_cast(dst_ap, src_ap, key, parts=128, rkeys=()):
        i = stage_next()
        dma("sp" if i == 0 else "pool", stage[i][0:parts, :], src_ap, list(rkeys), [("stage", i)])
        cp("act" if i == 0 else "dve", dst_ap, stage[i][0:parts, :], [("stage", i)], [key])

    for kc in range(8):
        load_cast(pa_bf[:, kc, :], p_a_d[kc * 128:(kc + 1) * 128, :], "pa_bf")
        load_cast(wout_bf[:, kc, :], w_out_d[kc * 128:(kc + 1) * 128, :], "wout_bf")
    for kc in range(4):
        load_cast(pb_bf[:, kc, :], p_b_d[kc * 128:(kc + 1) * 128, :], "pb_bf")
        load_cast(wglu_bf[:, kc, :], w_glu_d[kc * 128:(kc + 1) * 128, :], "wglu_bf")
    lnxg_row = sb("lnxg_row", [128, D], BF16); lnxb_row = sb("lnxb_row", [128, D], BF16)
    bglu_row = sb("bglu_row", [128, D], BF16); fg_row = sb("fg_row", [128, D])
    for (t_, d_, n_) in [(lnxg_row, lnx_g_d, "lnxg_row"), (lnxb_row, lnx_b_d, "lnxb_row"), (bglu_row, b_glu_d, "bglu_row")]:
        load_cast(t_[:, :], d_[0, :].partition_broadcast(128), n_)
    dma("pool", fg_row[:, :], final_g_d[0, :].partition_broadcast(128), [], ["fg_row"])
    load_cast(wup_bf[:, :], w_up_d[:, :], "wup_bf", parts=64)
    load_cast(aup_bf[:, :], a_up_d[:, :], "aup_bf", parts=64)

    ct_cols = []
    for ct in range(NCT):
        if ct < 24: ct_cols.append((ct * 128, 128))
        elif ct == 24: ct_cols.append((3072, 64))
        elif ct == 25: ct_cols.append((3136, 64))
        else: ct_cols.append((3200 + (ct - 26) * 128, 128))
    wst = [sbt(f"wst{i}", [128, 3712]) for i in range(2)]
    wsb = [sbt(f"wsb{i}", [128, 3712], BF16) for i in range(2)]
    for kc in range(8):
        for hw in range(2):
            i = hw
            q = "sp" if i == 0 else "pool"
            c0, c1 = (0, 3584) if hw == 0 else (3584, 7296)
            n = c1 - c0
            dma(q, wst[i][:, 0:n], w_in_d[kc * 128:(kc + 1) * 128, c0:c1], [], [("wst", i)])
            if hw == 0:
                ts("dve", wsb[i][:, 0:n], wst[i][:, 0:n], gcol[:, kc:kc + 1], ALU.mult, [("wst", i), "gcol"], [("wsb", i)])
                dma(q, wbf_d[0:24, :, kc * 128:(kc + 1) * 128].rearrange("t p c -> p t c"),
                    wsb[i][:, 0:3072].rearrange("p (t c) -> p t c", c=128), [("wsb", i)], [("wbfA", kc)])
                dma(q, wbf_d[24, :, kc * 128:kc * 128 + 64], wsb[i][:, 3072:3136], [("wsb", i)], [("wbfB", kc)])
                dma(q, wbf_d[25, :, kc * 128:kc * 128 + 64], wsb[i][:, 3136:3200], [("wsb", i)], [("wbfC", kc)])
                dma(q, wbf_d[26:29, :, kc * 128:(kc + 1) * 128].rearrange("t p c -> p t c"),
                    wsb[i][:, 3200:3584].rearrange("p (t c) -> p t c", c=128), [("wsb", i)], [("wbfD", kc)])
            else:
                act(wsb[i][:, 0:n], wst[i][:, 0:n], AF.Identity, [("wst", i), "gcol"], [("wsb", i)], scale=gcol[:, kc:kc + 1])
                dma(q, wbf_d[29:58, :, kc * 128:(kc + 1) * 128].rearrange("t p c -> p t c"),
                    wsb[i][:, 0:n].rearrange("p (t c) -> p t c", c=128), [("wsb", i)], [("wbfE", kc)])

    E_re = sb("E_re", [128, 2048], BF16); E_im = sb("E_im", [128, 2048], BF16)
    D_re = sb("D_re", [128, 16, 128], BF16); D_im = sb("D_im", [128, 16, 128], BF16)
    L_re = sb("L_re", [128, 16]); L_im = sb("L_im", [128, 16])
    Cre_pad = sb("Cre_pad", [128, 16, 128], BF16); Cimn_pad = sb("Cimn_pad", [128, 16, 128], BF16)
    Bm_bd = sb("Bm_bd", [128, 4, 1024], BF16)
    xe_re = sb("xe_re", [128, 16]); xe_im = sb("xe_im", [128, 16])
    tA = sbt("tA", [128, 512]); tB = sbt("tB", [128, 512]); tC = sbt("tC", [128, 512]); tDd = sbt("tD", [128, 512])
    tI = sbt("tI", [128, 512], I32); tM = sbt("tM", [128, 512])
    s5a = sbt("s5a", [32, 64]); s5b = sbt("s5b", [32, 64]); s5c = sbt("s5c", [32, 64]); s5d = sbt("s5d", [32, 64])
    s5e = sbt("s5e", [32, 64]); s5f = sbt("s5f", [32, 64]); s5g = sbt("s5g", [32, 64]); s5h = sbt("s5h", [32, 64])
    s5i = sbt("s5i", [32, 64]); s5i32 = sbt("s5i32", [32, 64], I32)
    ldt = sbt("ldt", [32, 1]); lre = sbt("lre", [32, 64]); lim = sbt("lim", [32, 64])
    dma("sp", ldt[:, :], log_dt_d[:, :], [], ["ldt"])
    dma("sp", lre[:, :], lam_re_d[:, :], [], ["lre"])
    dma("sp", lim[:, :], lam_im_d[:, :], [], ["lim"])
    act(ldt[:, :], ldt[:, :], AF.Exp, ["ldt"], ["ldt"])
    ts("dve", s5a[:, :], lre[:, :], ldt[:, 0:1], ALU.mult, ["lre", "ldt"], ["s5a"])
    ts("dve", s5b[:, :], lim[:, :], ldt[:, 0:1], ALU.mult, ["lim", "ldt"], ["s5b"], s2=1.0 / TWO_PI, op1=ALU.mult)
    dma("sp", s_tab_d[0, :].rearrange("(g p) -> g p", p=64), s5a[:, :], ["s5a"], ["s_tab"])
    dma("sp", s_tab_d[1, :].rearrange("(g p) -> g p", p=64), s5b[:, :], ["s5b"], ["s_tab"])

    def sincos(turn_ap, sin_out, cos_out, tmps, tkeys, keys_r, key_s, key_c):
        ta, tb, ti = tmps
        ka, kb, ki = tkeys
        cp("dve", ti, turn_ap, keys_r, [ki])
        cp("dve", ta, ti, [ki], [ka])
        tt("dve", ta, turn_ap, ta, ALU.subtract, keys_r + [ka], [ka])
        for thr, sgn in ((0.5, -1.0), (-0.5, 1.0)):
            ts("dve", tb, ta, thr, ALU.is_gt if sgn < 0 else ALU.is_lt, [ka], [kb])
            stt(ta, tb, sgn, ta, ALU.mult, ALU.add, [ka, kb], [ka])
        act(sin_out, ta, AF.Sin, [ka], [key_s], scale=TWO_PI * (1 - 1e-6))
        ts("dve", ta, ta, 0.25, ALU.add, [ka], [ka])
        ts("dve", tb, ta, 0.5, ALU.is_gt, [ka], [kb])
        stt(ta, tb, -1.0, ta, ALU.mult, ALU.add, [ka, kb], [ka])
        act(cos_out, ta, AF.Sin, [ka], [key_c], scale=TWO_PI * (1 - 1e-6))

    sincos(s5b[:, :], s5c[:, :], s5d[:, :], (s5h[:, :], s5i[:, :], s5i32[:, :]), ("s5h", "s5i", "s5i32"), ["s5b"], "s5c", "s5d")
    act(s5e[:, :], s5a[:, :], AF.Exp, ["s5a"], ["s5e"])
    tt("dve", s5d[:, :], s5d[:, :], s5e[:, :], ALU.mult, ["s5d", "s5e"], ["s5d"])
    tt("dve", s5c[:, :], s5c[:, :], s5e[:, :], ALU.mult, ["s5c", "s5e"], ["s5c"])
    tt("dve", s5e[:, :], lre[:, :], lre[:, :], ALU.mult, ["lre"], ["s5e"])
    tt("dve", s5f[:, :], lim[:, :], lim[:, :], ALU.mult, ["lim"], ["s5f"])
    tt("dve", s5e[:, :], s5e[:, :], s5f[:, :], ALU.add, ["s5e", "s5f"], ["s5e"])
    S.add("dve", lambda e: e.reciprocal(out=s5e[:, :], in_=s5e[:, :]), ["s5e"], ["s5e"])
    ts("dve", s5d[:, :], s5d[:, :], -1.0, ALU.add, ["s5d"], ["s5d"])
    tt("dve", s5f[:, :], s5d[:, :], lre[:, :], ALU.mult, ["s5d", "lre"], ["s5f"])
    tt("dve", s5g[:, :], s5c[:, :], lim[:, :], ALU.mult, ["s5c", "lim"], ["s5g"])
    tt("dve", s5f[:, :], s5f[:, :], s5g[:, :], ALU.add, ["s5f", "s5g"], ["s5f"])
    tt("dve", s5f[:, :], s5f[:, :], s5e[:, :], ALU.mult, ["s5f", "s5e"], ["s5f"])
    tt("dve", s5g[:, :], s5c[:, :], lre[:, :], ALU.mult, ["s5c", "lre"], ["s5g"])
    tt("dve", s5h[:, :], s5d[:, :], lim[:, :], ALU.mult, ["s5d", "lim"], ["s5h"])
    tt("dve", s5g[:, :], s5g[:, :], s5h[:, :], ALU.subtract, ["s5g", "s5h"], ["s5g"])
    tt("dve", s5g[:, :], s5g[:, :], s5e[:, :], ALU.mult, ["s5g", "s5e"], ["s5g"])
    fre_b = bc_last(s5f[:, 0:32], 16)
    for hfp in range(2):
        psl = slice(hfp * 32, (hfp + 1) * 32)
        bre, bim, bt1, bt2 = tA[0:32, :], tB[0:32, :], tC[0:32, :], tDd[0:32, :]
        dma("sp", bre, b_re_d[:, hfp * 512:(hfp + 1) * 512], [], ["tA"])
        dma("pool", bim, b_im_d[:, hfp * 512:(hfp + 1) * 512], [], ["tB"])
        v3 = lambda t: t.rearrange("g (p h) -> g p h", h=16)
        fre_b = bc_last(s5f[:, psl], 16); fim_b = bc_last(s5g[:, psl], 16)
        tt("dve", v3(bt1), v3(bre), fre_b, ALU.mult, ["tA", "s5f"], ["tC"])
        tt("dve", v3(bt2), v3(bim), fim_b, ALU.mult, ["tB", "s5g"], ["tD"])
        tt("dve", bt1, bt1, bt2, ALU.subtract, ["tC", "tD"], ["tC"])
        dma("sp", s_bb_d[0, :, psl, :].rearrange("g p h -> g (p h)"), bt1, ["tC"], [("s_bb", 0)])
        tt("dve", v3(bt2), v3(bim), fre_b, ALU.mult, ["tB", "s5f"], ["tD"])
        tt("dve", v3(bre), v3(bre), fim_b, ALU.mult, ["tA", "s5g"], ["tA"])
        tt("dve", bt2, bt2, bre, ALU.add, ["tD", "tA"], ["tD"])
        dma("sp", s_bb_d[1, :, psl, :].rearrange("g p h -> g (p h)"), bt2, ["tD"], [("s_bb", 1)])
    zt = stage_next()
    memset("dve", stage[zt][:, :], 0.0, [("stage", zt)])
    for r4 in range(4):
        dma("sp", s_bm_d[r4 * 128:(r4 + 1) * 128, :], stage[zt][:, :], [("stage", zt)], ["s_bm"])
    for g in range(32):
        gl = g % 8
        for c in range(2):
            dma("sp" if c == 0 else "pool",
                s_bm_d[g * 16:(g + 1) * 16, gl * 128 + c * 64: gl * 128 + c * 64 + 64],
                s_bb_d[c, g].rearrange("p h -> h p"), [("s_bb", c), "s_bm"], [("s_bmw", g, c)], slow=True)
    for ft in range(4):
        load_cast(Bm_bd[:, ft, :], s_bm_d[ft * 128:(ft + 1) * 128, :], "Bm_bd", rkeys=["s_bm"] + [("s_bmw", g_, c_) for g_ in range(32) for c_ in range(2)])
    cld = sbt("cld", [128, 64]); cT = sbt("cT", [64, 512])
    for c, cd in enumerate((c_re_d, c_im_d)):
        for q4 in range(4):
            dma("sp", cld[:, :], cd[q4 * 128:(q4 + 1) * 128, :], [], ["cld"])
            pt, pk = ps_alloc()
            S.add("pe", lambda e, pt=pt: e.transpose(pt[0:64, 0:128], cld[:, :], ident_f[:, :]), ["cld", "ident_f"], [pk])
            cp("act", cT[:, q4 * 128:(q4 + 1) * 128], pt[0:64, 0:128], [pk], ["cT"])
        dma("sp", s_ct_d[c], cT[:, :], ["cT"], [("s_ct", c)])
    cl2 = sbt("cl2", [128, 16, 16])
    memset("pool", Cre_pad[:, :, :], 0.0, ["Cre_pad"])
    memset("pool", Cimn_pad[:, :, :], 0.0, ["Cimn_pad"])
    for c, (dst, key) in enumerate(((Cre_pad, "Cre_pad"), (Cimn_pad, "Cimn_pad"))):
        for g2 in range(2):
            src = s_ct_d[c].rearrange("p (gp g2 h) -> p gp g2 h", g2=2, h=16)[:, :, g2, :]
            dma("sp", cl2[g2 * 64:(g2 + 1) * 64, :, :], src, [("s_ct", c)], ["cl2"])
        for g2 in range(2):
            for q in range(4):
                dstv = dst[g2 * 64:(g2 + 1) * 64, :, :].rearrange("p (ft q) c -> p ft q c", q=4)[:, :, q, q * 32 + g2 * 16: q * 32 + g2 * 16 + 16]
                srcv = cl2[g2 * 64:(g2 + 1) * 64, :, :].rearrange("p (ft q) h -> p ft q h", q=4)[:, :, q, :]
                ts("dve", dstv, srcv, 1.0 if c == 0 else -1.0, ALU.mult, ["cl2"], [key])
    lrd_col = sbt("lrd_col", [128, 16]); trn_col = sbt("trn_col", [128, 16])
    dma("sp", lrd_col[:, :], s_tab_d[0, :].rearrange("(gp q) -> q gp", q=128), ["s_tab"], ["lrd_col"], slow=True)
    dma("sp", trn_col[:, :], s_tab_d[1, :].rearrange("(gp q) -> q gp", q=128), ["s_tab"], ["trn_col"], slow=True)
    nrow = sbt("nrow", [128, 128]); ncol = sbt("ncol", [128, 1])
    S.add("pool", lambda e: e.iota(nrow[:, :], pattern=[[1, 128]], base=-127, channel_multiplier=0,
                                   allow_small_or_imprecise_dtypes=True), [], ["nrow"])
    S.add("pool", lambda e: e.iota(ncol[:, :], pattern=[[0, 1]], base=127, channel_multiplier=-1,
                                   allow_small_or_imprecise_dtypes=True), [], ["ncol"])

    tok = ["tabtok"]
    tE = sbt("tE", [128, 512]); tF = sbt("tF", [128, 512])
    for q in range(4):
        gsl = slice(q * 4, q * 4 + 4)
        tt("dve", tE[:, :].rearrange("p (a b) -> p a b", b=128), bc_mid(nrow[:, :], 4), bc_last(trn_col[:, gsl], 128),
           ALU.mult, ["nrow", "trn_col"] + tok, ["tE"])
        tt("dve", tF[:, :].rearrange("p (a b) -> p a b", b=128), bc_mid(nrow[:, :], 4), bc_last(lrd_col[:, gsl], 128),
           ALU.mult, ["nrow", "lrd_col"] + tok, ["tF"])
        sincos(tE[:, :], tB[:, :], tC[:, :], (tDd[:, :], tA[:, :], tI[:, :]), ("tD", "tA", "tI"), ["tE"], "tB", "tC")
        act(tM[:, :], tF[:, :], AF.Exp, ["tF"], ["tM"])
        tt("dve", D_re[:, gsl, :].rearrange("p a b -> p (a b)"), tC[:, :], tM[:, :], ALU.mult, ["tC", "tM"], ["D_re"] + tok)
        tt("dve", D_im[:, gsl, :].rearrange("p a b -> p (a b)"), tB[:, :], tM[:, :], ALU.mult, ["tB", "tM"], ["D_im"] + tok)
    ts("dve", tE[:, 0:16], trn_col[:, :], 128.0, ALU.mult, ["trn_col"] + tok, ["tE"])
    ts("dve", tF[:, 0:16], lrd_col[:, :], 128.0, ALU.mult, ["lrd_col"] + tok, ["tF"])
    sincos(tE[:, 0:16], tB[:, 0:16], tC[:, 0:16], (tDd[:, 0:16], tA[:, 0:16], tI[:, 0:16]), ("tD", "tA", "tI"), ["tE"], "tB", "tC")
    act(tM[:, 0:16], tF[:, 0:16], AF.Exp, ["tF"], ["tM"])
    tt("dve", L_re[:, :], tC[:, 0:16], tM[:, 0:16], ALU.mult, ["tC", "tM"], ["L_re"] + tok)
    tt("dve", L_im[:, :], tB[:, 0:16], tM[:, 0:16], ALU.mult, ["tB", "tM"], ["L_im"] + tok)
    rowt = stage[0][:, 0:512]; rowl = stage[1][:, 0:512]
    for q in range(4):
        csl = slice(q * 512, (q + 1) * 512)
        dma("sp", rowt, s_tab_d[1, csl].partition_broadcast(128), ["s_tab"] + tok, [("stage", 0)])
        dma("pool", rowl, s_tab_d[0, csl].partition_broadcast(128), ["s_tab"] + tok, [("stage", 1)])
        ts("dve", tE[:, :], rowt, ncol[:, 0:1], ALU.mult, [("stage", 0), "ncol"] + tok, ["tE"])
        ts("dve", tF[:, :], rowl, ncol[:, 0:1], ALU.mult, [("stage", 1), "ncol"] + tok, ["tF"])
        sincos(tE[:, :], tB[:, :], tC[:, :], (tDd[:, :], tA[:, :], tI[:, :]), ("tD", "tA", "tI"), ["tE"], "tB", "tC")
        act(tM[:, :], tF[:, :], AF.Exp, ["tF"], ["tM"])
        tt("dve", E_re[:, csl], tC[:, :], tM[:, :], ALU.mult, ["tC", "tM"], ["E_re"] + tok)
        tt("dve", E_im[:, csl], tB[:, :], tM[:, :], ALU.mult, ["tB", "tM"], ["E_im"] + tok)
    S.emit(nc, es, drain=True)
    es_setup.close()
    S = Sched()
    memset("pool", xe_re[:, :], 0.0, ["xe_re"])
    memset("pool", xe_im[:, :], 0.0, ["xe_im"])

    NXB = 1
    hT = sb("hT", [128, 8, 256], BF16)
    NWB = 2
    wblk = [sb(f"wblk{i}", [128, 8 * 128], BF16) for i in range(NWB)]
    zraw = [sb(f"zraw{i}", [128, 257]) for i in range(2)]
    zcar = sb("zcar", [128, 26])
    z_fm = sb("z_fm", [128, 30, 256], BF16)
    z_tm = sb("z_tm", [128, 2, 3584], BF16)
    zdiff = sb("zdiff", [128, 256])
    xin = [z_fm[:, 0:8, :].rearrange("p a b -> p (a b)").bitcast(F32)]
    xnA = z_fm[:, 8:12, :].rearrange("p a b -> p (a b)")
    XK = [("z_fm", i) for i in range(8)]
    NK = [("z_fm", i) for i in range(8, 12)]
    memset("pool", zcar[:, :], 0.0, ["zcar"])
    ssq = sb("ssq", [128, 4]);
    S_f = sb("S_f", [128, 8, 64]); S_bd = sb("S_bd", [128, 8, 128], BF16)
    memset("pool", S_f[:, :, :], 0.0, [("S_f", h) for h in range(8)])
    memset("pool", S_bd[:, :, :], 0.0, [("S_bd", h) for h in range(8)])
    NSET = 2
    def mkset(i):
        d = {}
        for nm in ["ld", "icl", "cl", "Wc", "Wi", "Wx", "kkn", "kh", "t1", "t2", "bt"]:
            d[nm] = sb(f"{nm}{i}", [128, 128])
        for nm in ["kk2", "a_p", "r_p", "rkp", "v_bf", "bh_p", "kh_p", "ApT"]:
            d[nm] = sb(f"{nm}{i}", [128, 128], BF16)
        for nm in ["b_bd", "k_bd"]:
            d[nm] = sb(f"{nm}{i}", [128, 2, 128], BF16)
        d["tm4"] = sb(f"tm4{i}", [128, 4, 128], BF16)
        for nm in ["AakT", "ArbT", "ArkT", "Pm", "Qm", "Zm", "Pn", "Qn", "Zn"]:
            d[nm] = sb(f"{nm}{i}", [128, 2, 128], BF16)
        d["W1"] = sb(f"W1{i}", [128, 2, 64], BF16)
        d["Vu"] = sb(f"Vu{i}", [128, 2, 64])
        d["U"] = sb(f"U{i}", [128, 128], BF16)
        d["wl"] = sb(f"wl{i}", [128, 1])
        d["i"] = i
        return d
    sets = [mkset(i) for i in range(NSET)]
    for st_ in sets:
        memset("pool", st_["b_bd"][:, :, :], 0.0, [("b_bd", st_["i"])])
        memset("pool", st_["k_bd"][:, :, :], 0.0, [("k_bd", st_["i"])])
    txw = sb("txw", [64, 128], BF16); xab = sb("xab", [64, 128], BF16)
    rk_sb = sb("rk_sb", [128, 16])
    Vall = sb("Vall", [128, 8, 128], BF16)
    ysb = sb("ysb", [128, D]); ytmp = sb("ytmp", [128, D]); ysq = ytmp
    ssqA = sb("ssqA", [128, 4])
    gst = sb("gst", [128, 4, 16])
    ya_bf = sb("ya_bf", [128, D], BF16); yaT = sb("yaT", [128, 8, 128], BF16)
    xn_bf = ya_bf
    merged_a = ya_bf
    xres = [ytmp]
    u_bf = z_fm
    BUp_re = sb("BUp_re", [128, 1024], BF16); BUp_im = sb("BUp_im", [128, 1024], BF16)
    s5t = [sb(f"s5t{i}", [128, 512]) for i in range(4)]
    X_re = sb("X_re", [128, 8, 128], BF16); X_im = sb("X_im", [128, 8, 128], BF16)
    xq = [sb(f"xq{i}", [128, 4]) for i in range(6)]
    ys_f = s5t[2]; ys_t = s5t[3]; ys_s = s5t[0]
    gel_bf = sb("gel_bf", [128, 4, 128], BF16)
    g1 = s5t[0]; g2 = s5t[1]
    yb_bf = sb("yb_bf", [128, 512], BF16); ybT = sb("ybT", [128, 4, 128], BF16)
    merged_bf = sb("merged_bf", [128, D], BF16); mT = yaT
    junk_bf = merged_bf
    xo = ysb; outt = ytmp

    wb_rr = [0]
    out_keys = []
    pending = []

    def step_pending(n):
        for _ in range(n):
            for g_ in list(pending):
                try:
                    next(g_)
                except StopIteration:
                    pending.remove(g_)

    def flush_pending():
        while pending:
            step_pending(1)
    for st in range(NST):
        for c in range(2):
            tok0 = st * 256 + c * 128
            dma("sp", xin[0], x_d[tok0:tok0 + 128, :], [], XK)
            act(xnA, xin[0], AF.Square, XK, NK + ["ssqA0"], accum=ssqA[:, 0:1])
            ts("dve", ssqA[:, 1:2], ssqA[:, 0:1], 1.0 / D, ALU.mult, ["ssqA0"], ["ssqA1"], s2=1e-6, op1=ALU.add)
            act(ssqA[:, 2:3], ssqA[:, 1:2], AF.Ln, ["ssqA1"], ["ssqA2"])
            act(ssqA[:, 3:4], ssqA[:, 2:3], AF.Exp, ["ssqA2"], ["ssqA3"], scale=-0.5)
            act(xnA, xin[0], AF.Identity, XK + ["ssqA3"], NK, scale=ssqA[:, 3:4])
            pt, pk = ps_alloc()
            ptb = pt[:, :].bitcast(BF16)
            for kc in range(8):
                S.add("pe", lambda e, ptb=ptb, kc=kc: e.transpose(ptb[:, kc * 128:(kc + 1) * 128],
                                                                   xnA[:, kc * 128:(kc + 1) * 128], ident[:, :]),
                      NK + ["ident"], [pk])
            cp("dve", hT[:, :, c * 128:(c + 1) * 128], ptb.rearrange("p (k t) -> p k t", t=128), [pk], ["hT"])
            step_pending(1)
        for ct in range(NCT):
            if ct < 26:
                step_pending(1)
            elif ct == 26:
                flush_pending()
            cs, cn = ct_cols[ct]
            wi = wb_rr[0] % NWB
            wb_rr[0] += 1
            wv = wblk[wi][:, :].rearrange("p (k c) -> p k c", c=128)
            dma("sp" if ct % 2 == 0 else "pool", wblk[wi][:, :] if cn == 128 else wv[:, :, 0:cn],
                wbf_d[ct] if cn == 128 else wbf_d[ct].rearrange("p (k c) -> p k c", c=128)[:, :, 0:cn],
                [], [("wblk", wi)])
            fm = ct < 26 or (30 <= ct + 0 and False)
            if ct < 26:
                kind, zi = "zs", ct
            elif ct < 34:
                kind, zi = "ga", ct - 26
            elif ct < 38:
                kind, zi = "u", ct - 34
            elif ct < 42:
                kind, zi = "gb", ct - 38
            elif ct < 50:
                kind, zi = "ma", ct - 42
            else:
                kind, zi = "mb", ct - 50
            if kind in ("zs", "u"):
                pt, pk = ps_alloc()
                for kc in range(8):
                    mm(pt[0:cn, 0:256], wv[:, kc, 0:cn], hT[:, kc, :], kc == 0, kc == 7, [("wblk", wi), "hT"], [pk])
                if kind == "u":
                    cp("act", z_fm[:, 26 + zi, :], pt[:, 0:256], [pk], [("z_fm", 26 + zi)])
                else:
                    zr = zraw[ct % 2]
                    zk = ("zraw", ct % 2)
                    cp("act", zr[0:cn, 1:257], pt[0:cn, 0:256], [pk], [zk])
                    cp("act", zr[0:cn, 0:1], zcar[0:cn, ct:ct + 1], ["zcar"], [zk])
                    cp("act", zcar[0:cn, ct:ct + 1], zr[0:cn, 256:257], [zk], ["zcar"])
                    tt("dve", zdiff[0:cn, :], zr[0:cn, 0:256], zr[0:cn, 1:257], ALU.subtract, [zk], ["zdiff"])
                    stt(z_fm[0:cn, zi, :], zdiff[0:cn, :], mu_col[0:cn, ct:ct + 1], zr[0:cn, 1:257], ALU.mult, ALU.add,
                        ["zdiff", zk, "mu_col"], [("z_fm", zi)])
            else:
                off = {"ga": 0, "gb": 1024, "ma": 1536, "mb": 2560}[kind] + zi * 128
                func = AF.Silu if kind in ("ga", "gb") else AF.Sigmoid
                pt, pk = ps_alloc()
                for c in range(2):
                    for kc in range(8):
                        mm(pt[:, c * 128:(c + 1) * 128], hT[:, kc, c * 128:(c + 1) * 128], wv[:, kc, :], kc == 0, kc == 7,
                           [("wblk", wi), "hT"], [pk])
                act(z_tm[:, :, off:off + 128], pt[:, 0:256].rearrange("p (c n) -> p c n", n=128), func, [pk],
                    [("z_tm", kind)])
        for c in range(2):
            tsl = slice(c * 128, (c + 1) * 128)
            tok0 = st * 256 + c * 128
            zr_all = [("z_fm", i) for i in range(30)]
            act(txw[:, :], z_fm[0:64, 24, tsl], AF.Tanh, [("z_fm", 24)], ["txw"])
            cp("act", xab[:, :], z_fm[0:64, 25, tsl], [("z_fm", 25)], ["xab"])
            rk_pt, rk_pk = psb[2], ("ps", 2)
            y_pt = [(psb[0], ("ps", 0)), (psb[1], ("ps", 1))]
            def s5_gen():
                for hS in range(2):
                    for ft in (2 * hS, 2 * hS + 1):
                        pbu = [ps_alloc(), ps_alloc()]
                        for hf in range(2):
                            mm(pbu[hf][0][:, :], z_fm[:, 26 + ft, tsl], Bm_bd[:, ft, hf * 512:(hf + 1) * 512], True, True,
                               [("z_fm", 26 + ft), "Bm_bd"], [pbu[hf][1]])
                        for hf in range(2):
                            pv4 = pbu[hf][0][:, :].rearrange("p (g c q) -> p g c q", c=2, q=64)
                            bre_v = pv4[:, :, 0, :]; bim_v = pv4[:, :, 1, :]
                            gbase = ft * 8 + hf * 4
                            esl = slice(gbase * 64, (gbase + 4) * 64)
                            lsl = slice((gbase - 16 * hS) * 64, (gbase - 16 * hS + 4) * 64)
                            e_re = E_re[:, esl].rearrange("p (g q) -> p g q", q=64); e_im = E_im[:, esl].rearrange("p (g q) -> p g q", q=64)
                            tv = [s5t[i][:, 0:256].rearrange("p (g q) -> p g q", q=64) for i in range(4)]
                            pk_ = pbu[hf][1]
                            tt("dve", tv[0], bre_v, e_re, ALU.mult, [pk_, "E_re"], ["s5t0"])
                            tt("dve", tv[1], bim_v, e_im, ALU.mult, [pk_, "E_im"], ["s5t1"])
                            tt("dve", tv[2], bre_v, e_im, ALU.mult, [pk_, "E_im"], ["s5t2"])
                            tt("dve", tv[3], bim_v, e_re, ALU.mult, [pk_, "E_re"], ["s5t3"])
                            tt("pool", BUp_re[:, lsl], s5t[0][:, 0:256], s5t[1][:, 0:256], ALU.subtract, ["s5t0", "s5t1"], ["BUp_re"])
                            tt("pool", BUp_im[:, lsl], s5t[2][:, 0:256], s5t[3][:, 0:256], ALU.add, ["s5t2", "s5t3"], ["BUp_im"])
                        yield
                    for q in (2 * hS, 2 * hS + 1):
                        pxr, pxrk = ps_alloc(); pxi, pxik = ps_alloc()
                        for g4 in range(4):
                            lg = (q - 2 * hS) * 4 + g4
                            mm(pxr[:, g4 * 128:(g4 + 1) * 128], BUp_re[:, lg * 128:(lg + 1) * 128], m_iu[:, :], True, True, ["BUp_re", "m_iu"], [pxrk])
                            mm(pxi[:, g4 * 128:(g4 + 1) * 128], BUp_im[:, lg * 128:(lg + 1) * 128], m_iu[:, :], True, True, ["BUp_im", "m_iu"], [pxik])
                        gsl = slice(q * 4, q * 4 + 4)
                        lgs = slice((q - 2 * hS) * 4, (q - 2 * hS) * 4 + 4)
                        tt("pool", xq[0][:, :], L_re[:, gsl], xe_re[:, gsl], ALU.mult, ["L_re", "xe_re"], ["xq0"])
                        tt("pool", xq[1][:, :], L_im[:, gsl], xe_im[:, gsl], ALU.mult, ["L_im", "xe_im"], ["xq1"])
                        tt("pool", xq[2][:, :], L_re[:, gsl], xe_im[:, gsl], ALU.mult, ["L_re", "xe_im"], ["xq2"])
                        tt("pool", xq[3][:, :], L_im[:, gsl], xe_re[:, gsl], ALU.mult, ["L_im", "xe_re"], ["xq3"])
                        tt("pool", xq[4][:, :], xq[0][:, :], xq[1][:, :], ALU.subtract, ["xq0", "xq1"], ["xq4"])
                        tt("pool", xq[5][:, :], xq[2][:, :], xq[3][:, :], ALU.add, ["xq2", "xq3"], ["xq5"])
                        for g4 in range(4):
                            gp = q * 4 + g4
                            cs = slice(g4 * 128, (g4 + 1) * 128)
                            stt(s5t[0][:, cs], pxr[:, cs], xq[4][:, g4:g4 + 1], D_re[:, gp, :], ALU.add, ALU.mult, [pxrk, "xq4", "D_re"], ["s5t0"])
                            stt(s5t[1][:, cs], pxi[:, cs], xq[5][:, g4:g4 + 1], D_im[:, gp, :], ALU.add, ALU.mult, [pxik, "xq5", "D_im"], ["s5t1"])
                            stt(s5t[2][:, cs], pxr[:, cs], xq[4][:, g4:g4 + 1], D_im[:, gp, :], ALU.add, ALU.mult, [pxrk, "xq4", "D_im"], ["s5t2"])
                            stt(s5t[3][:, cs], pxi[:, cs], xq[5][:, g4:g4 + 1], D_re[:, gp, :], ALU.add, ALU.mult, [pxik, "xq5", "D_re"], ["s5t3"])
                        tt("pool", X_re[:, lgs, :].rearrange("p a b -> p (a b)"), s5t[0][:, :], s5t[1][:, :], ALU.subtract, ["s5t0", "s5t1"], ["X_re"])
                        tt("pool", X_im[:, lgs, :].rearrange("p a b -> p (a b)"), s5t[2][:, :], s5t[3][:, :], ALU.add, ["s5t2", "s5t3"], ["X_im"])
                        last_r = pxr[:, :].rearrange("p (g t) -> p g t", t=128)[:, :, 127]
                        last_i = pxi[:, :].rearrange("p (g t) -> p g t", t=128)[:, :, 127]
                        tt("dve", xe_re[:, gsl], last_r, xq[4][:, :], ALU.add, [pxrk, "xq4"], ["xe_re"])
                        tt("dve", xe_im[:, gsl], last_i, xq[5][:, :], ALU.add, [pxik, "xq5"], ["xe_im"])
                        yield
                    pys, pysk = ps_alloc()
                    for fl in range(2):
                        ft = 2 * hS + fl
                        for g4 in range(4):
                            gp = ft * 4 + g4
                            lg = gp - 8 * hS
                            mm(pys[:, fl * 128:(fl + 1) * 128], Cre_pad[:, gp, :], X_re[:, lg, :], g4 == 0, False, ["Cre_pad", "X_re"], [pysk])
                            mm(pys[:, fl * 128:(fl + 1) * 128], Cimn_pad[:, gp, :], X_im[:, lg, :], False, g4 == 3, ["Cimn_pad", "X_im"], [pysk])
                    for fl in range(2):
                        ft = 2 * hS + fl
                        stt(ys_f[:, fl * 128:(fl + 1) * 128], z_fm[:, 26 + ft, tsl], dsk[:, ft:ft + 1], pys[:, fl * 128:(fl + 1) * 128],
                            ALU.mult, ALU.add, [("z_fm", 26 + ft), "dsk", pysk], ["s5t2"])
                    hh = slice(0, 256)
                    act(ys_t[:, hh], ys_f[:, hh], AF.Square, ["s5t2"], ["s5t3"])
                    ts("pool", ys_t[:, hh], ys_t[:, hh], 0.044715, ALU.mult, ["s5t3"], ["s5t3"], s2=1.0, op1=ALU.add)
                    tt("pool", ys_t[:, hh], ys_t[:, hh], ys_f[:, hh], ALU.mult, ["s5t3", "s5t2"], ["s5t3"])
                    act(ys_s[:, hh], ys_t[:, hh], AF.Sigmoid, ["s5t3"], ["s5t0"], scale=1.5957691216057308)
                    tt("pool", gel_bf[:, 2 * hS:2 * hS + 2, :].rearrange("p a b -> p (a b)"), ys_s[:, hh], ys_f[:, hh], ALU.mult,
                       ["s5t0", "s5t2"], ["gel_bf"])
                    yield
                pg = [ps_alloc(), ps_alloc()]
                for hf in range(2):
                    for ft in range(4):
                        mm(pg[hf][0][:, :], gel_bf[:, ft, :], wglu_bf[:, ft, hf * 512:(hf + 1) * 512], ft == 0, ft == 3, ["gel_bf", "wglu_bf"], [pg[hf][1]])
                tt("dve", g1[:, :], pg[0][0][:, :], bglu_row[:, 0:512], ALU.add, [pg[0][1], "bglu_row"], ["s5t0"])
                tt("dve", g2[:, :], pg[1][0][:, :], bglu_row[:, 512:1024], ALU.add, [pg[1][1], "bglu_row"], ["s5t1"])
                act(g2[:, :], g2[:, :], AF.Sigmoid, ["s5t1"], ["s5t1"])
                tt("pool", g1[:, :], g1[:, :], g2[:, :], ALU.mult, ["s5t0", "s5t1"], ["s5t0"])
                tt("pool", yb_bf[:, :], g1[:, :], z_tm[:, c, 1024:1536], ALU.mult, ["s5t0", ("z_tm", "gb")], ["yb_bf"])
                yield
                pt, pk = ps_alloc()
                ptb = pt[:, :].bitcast(BF16)
                for kc in range(4):
                    S.add("pe", lambda e, ptb=ptb, kc=kc: e.transpose(ptb[:, kc * 128:(kc + 1) * 128], yb_bf[:, kc * 128:(kc + 1) * 128], ident[:, :]),
                          ["yb_bf", "ident"], [pk])
                cp("act", ybT[:, :, :].rearrange("p a b -> p (a b)"), ptb[:, 0:512], [pk], ["ybT"])

            def wkv_gen(hp):
                W = sets[hp % NSET]
                si = W["i"]
                K = lambda nm: (nm, si)
                r_ap = z_fm[:, hp, tsl]; k_ap = z_fm[:, 8 + hp, tsl]; v_ap = z_fm[:, 16 + hp, tsl]
                fsl = slice(hp * 128, (hp + 1) * 128)
                pq, pqk = ps_alloc()
                mm(pq[:, 0:128], wup_bf[:, fsl], txw[:, :], True, True, ["wup_bf", "txw"], [pqk])
                mm(pq[:, 128:256], aup_bf[:, fsl], xab[:, :], True, True, ["aup_bf", "xab"], [pqk])
                act(W["ld"][:, :], pq[:, 0:128], AF.Sigmoid, [pqk, "w0c"], [K("ld")], bias=w0c[:, hp:hp + 1])
                act(W["icl"][:, :], pq[:, 128:256], AF.Sigmoid, [pqk, "a0c"], [K("icl")], bias=a0c[:, hp:hp + 1])
                S.add("dve", lambda e, W=W: e.tensor_tensor_scan(out=W["cl"][:, :], data0=ones_f[:, :], data1=W["ld"][:, :],
                                                                 initial=0.0, op0=ALU.mult, op1=ALU.add),
                      [K("ld"), "ones_f"], [K("cl")])
                tt("pool", W["t1"][:, :], W["cl"][:, :], W["ld"][:, :], ALU.subtract, [K("cl"), K("ld")], [K("t1")])
                act(W["Wc"][:, :], W["cl"][:, :], AF.Exp, [K("cl")], [K("Wc")], scale=-C0)
                act(W["Wi"][:, :], W["cl"][:, :], AF.Exp, [K("cl")], [K("Wi")], scale=C0)
                act(W["Wx"][:, :], W["t1"][:, :], AF.Exp, [K("t1")], [K("Wx")], scale=-C0)
                cp("act", W["wl"][:, :], W["Wc"][:, 127:128], [K("Wc")], [K("wl")])
                yield
                act(W["kk2"][:, :], k_ap, AF.Square, [("z_fm", 8 + hp), "kkc"], [K("kk2")], scale=kkc[:, hp:hp + 1])
                pn, pnk = ps_alloc()
                mm(pn[:, 0:128], blk1[:, :], W["kk2"][:, :], True, True, ["blk1", K("kk2")], [pnk])
                ts("dve", W["t2"][:, :], pn[:, 0:128], 1e-24, ALU.max, [pnk], [K("t2")])
                act(W["t2"][:, :], W["t2"][:, :], AF.Ln, [K("t2")], [K("t2")])
                act(W["t2"][:, :], W["t2"][:, :], AF.Exp, [K("t2")], [K("t2")], scale=-0.5)
                stt(W["kkn"][:, :], k_ap, kkc[:, hp:hp + 1], W["t2"][:, :], ALU.mult, ALU.mult,
                    [("z_fm", 8 + hp), "kkc", K("t2")], [K("kkn")])
                yield
                ts("pool", W["t1"][:, :], W["icl"][:, :], -1.0, ALU.add, [K("icl"), "kac"], [K("t1")],
                   s2=kac[:, hp:hp + 1], op1=ALU.mult)
                stt(W["kh"][:, :], W["t1"][:, :], 1.0, k_ap, ALU.add, ALU.mult, [K("t1"), ("z_fm", 8 + hp)], [K("kh")])
                stt(W["a_p"][:, :], W["kkn"][:, :], -1.0, W["Wx"][:, :], ALU.mult, ALU.mult, [K("kkn"), K("Wx")], [K("a_p")])
                tt("pool", W["bt"][:, :], W["kkn"][:, :], W["icl"][:, :], ALU.mult, [K("kkn"), K("icl")], [K("bt")])
                tt("dve", W["bt"][:, :], W["bt"][:, :], W["Wi"][:, :], ALU.mult, [K("bt"), K("Wi")], [K("bt")])
                tt("pool", W["t2"][:, :], W["kh"][:, :], W["Wi"][:, :], ALU.mult, [K("kh"), K("Wi")], [K("t2")])
                for h2 in range(2):
                    hs = slice(h2 * 64, (h2 + 1) * 64)
                    cp("dve", W["b_bd"][hs, h2, :], W["bt"][hs, :], [K("bt")], [("b_bd", si)])
                    cp("dve", W["k_bd"][hs, h2, :], W["t2"][hs, :], [K("t2")], [("k_bd", si)])
                ts("dve", W["bh_p"][:, :], W["bt"][:, :], W["wl"][:, 0:1], ALU.mult, [K("bt"), K("wl")], [K("bh_p")])
                ts("pool", W["kh_p"][:, :], W["t2"][:, :], W["wl"][:, 0:1], ALU.mult, [K("t2"), K("wl")], [K("kh_p")])
                tt("dve", W["r_p"][:, :], r_ap, W["Wc"][:, :], ALU.mult, [("z_fm", hp), K("Wc")], [K("r_p")])
                stt(W["rkp"][:, :], r_ap, rkc[:, hp:hp + 1], W["kh"][:, :], ALU.mult, ALU.mult,
                    [("z_fm", hp), "rkc", K("kh")], [K("rkp")])
                mm(rk_pt[:, hp * 2:hp * 2 + 2], W["rkp"][:, :], hsel[:, :], True, True, [K("rkp"), "hsel"], [rk_pk])
                yield
                ptm, ptk = ps_alloc()
                ptmb = ptm[:, :].bitcast(BF16)
                for j, (src, key) in enumerate(((W["a_p"][:, :], K("a_p")), (W["bh_p"][:, :], K("bh_p")),
                                                (W["kh_p"][:, :], K("kh_p")), (v_ap, ("z_fm", 16 + hp)))):
                    S.add("pe", lambda e, ptmb=ptmb, j=j, src=src: e.transpose(ptmb[:, j * 128:(j + 1) * 128], src, ident[:, :]),
                          [key, "ident"], [ptk])
                cp("act", W["tm4"][:, :, :].rearrange("p a b -> p (a b)"), ptmb[:, 0:512], [ptk], [K("tm4")])
                cp("act", Vall[:, hp, :], W["tm4"][:, 3, :], [K("tm4")], [("Vall", hp)])
                yield
                pa1, pa1k = ps_alloc()
                pa2, pa2k = ps_alloc()
                for h2 in range(2):
                    mm(pa1[:, h2 * 128:(h2 + 1) * 128], W["b_bd"][:, h2, :], W["a_p"][:, :], True, True, [("b_bd", si), K("a_p")], [pa1k])
                    mm(pa1[:, 256 + h2 * 128:256 + (h2 + 1) * 128], W["k_bd"][:, h2, :], W["a_p"][:, :], True, True, [("k_bd", si), K("a_p")], [pa1k])
                    mm(pa2[:, h2 * 128:(h2 + 1) * 128], W["b_bd"][:, h2, :], W["r_p"][:, :], True, True, [("b_bd", si), K("r_p")], [pa2k])
                    mm(pa2[:, 256 + h2 * 128:256 + (h2 + 1) * 128], W["k_bd"][:, h2, :], W["r_p"][:, :], True, True, [("k_bd", si), K("r_p")], [pa2k])
                msu_b = bc_mid(m_su[:, :], 2); miu_b = bc_mid(m_iu[:, :], 2)
                v2 = lambda ap: ap.rearrange("p (a b) -> p a b", b=128)
                tt("dve", W["Pm"][:, :, :], v2(pa1[:, 0:256]), msu_b, ALU.mult, [pa1k, "m_su"], [K("Pm")])
                tt("dve", W["AakT"][:, :, :], v2(pa1[:, 256:512]), msu_b, ALU.mult, [pa1k, "m_su"], [K("AakT")])
                tt("dve", W["ArbT"][:, :, :], v2(pa2[:, 0:256]), miu_b, ALU.mult, [pa2k, "m_iu"], [K("ArbT")])
                tt("dve", W["ArkT"][:, :, :], v2(pa2[:, 256:512]), miu_b, ALU.mult, [pa2k, "m_iu"], [K("ArkT")])
                yield
                pq0, pq0k = ps_alloc()
                pq0b = pq0[:, :].bitcast(BF16)
                for h2 in range(2):
                    S.add("pe", lambda e, pq0b=pq0b, h2=h2, W=W: e.transpose(pq0b[:, h2 * 128:(h2 + 1) * 128], W["Pm"][:, h2, :], ident[:, :]),
                          [K("Pm"), "ident"], [pq0k])
                cp("act", W["Qm"][:, :, :].rearrange("p a b -> p (a b)"), pq0b[:, 0:256], [pq0k], [K("Qm")])
                tt("pool", W["Zm"][:, :, :], W["Pm"][:, :, :], bc_mid(ident[:, :], 2), ALU.add, [K("Pm"), "ident"], [K("Zm")])
                yield
                cur = ("Pm", "Qm", "Zm"); nxt = ("Pn", "Qn", "Zn")
                for lvl in range(1, 7):
                    pl, plk = ps_alloc()
                    last = lvl == 6
                    for h2 in range(2):
                        if not last:
                            mm(pl[:, h2 * 128:(h2 + 1) * 128], W[cur[1]][:, h2, :], W[cur[0]][:, h2, :], True, True,
                               [K(cur[0]), K(cur[1])], [plk])
                        mm(pl[:, 256 + h2 * 128:256 + (h2 + 1) * 128], W[cur[0]][:, h2, :], W[cur[1]][:, h2, :], True, True,
                           [K(cur[0]), K(cur[1])], [plk])
                    if not last:
                        cp("act", W[nxt[0]][:, :, :].rearrange("p a b -> p (a b)"), pl[:, 0:256], [plk], [K(nxt[0])])
                    cp("act", W[nxt[1]][:, :, :].rearrange("p a b -> p (a b)"), pl[:, 256:512], [plk], [K(nxt[1])])
                    yield
                    pz, pzk = ps_alloc()
                    for h2 in range(2):
                        mm(pz[:, h2 * 128:(h2 + 1) * 128], W[nxt[1]][:, h2, :], W[cur[2]][:, h2, :], True, True,
                           [K(nxt[1]), K(cur[2])], [pzk])
                    tt("dve", W[nxt[2]][:, :, :], v2(pz[:, 0:256]), W[cur[2]][:, :, :], ALU.add, [pzk, K(cur[2])], [K(nxt[2])])
                    cur, nxt = nxt, cur
                    yield
                Tt = W[cur[2]]; Tk = K(cur[2])
                pap, papk = ps_alloc()
                for h2 in range(2):
                    mm(pap[:, h2 * 128:(h2 + 1) * 128], W["tm4"][:, 0, :], Tt[:, h2, :], True, True, [K("tm4"), Tk], [papk])
                for h2 in range(2):
                    hs = slice(h2 * 64, (h2 + 1) * 64)
                    cp("act", W["ApT"][hs, :], pap[hs, h2 * 128:(h2 + 1) * 128], [papk], [K("ApT")])
                yield
                pw, pwk = ps_alloc()
                for h2 in range(2):
                    mm(pw[:, h2 * 64:(h2 + 1) * 64], W["AakT"][:, h2, :], W["tm4"][:, 3, h2 * 64:(h2 + 1) * 64], True, True,
                       [K("AakT"), K("tm4")], [pwk])
                cp("act", W["W1"][:, :, :].rearrange("p a b -> p (a b)"), pw[:, 0:128], [pwk], [K("W1")])
                yield
                pv, pvk = ps_alloc()
                for h2 in range(2):
                    mm(pv[:, h2 * 64:(h2 + 1) * 64], Tt[:, h2, :], W["W1"][:, h2, :], True, True, [Tk, K("W1")], [pvk])
                cp("act", W["Vu"][:, :, :].rearrange("p a b -> p (a b)"), pv[:, 0:128], [pvk], [K("Vu")])
                yield
                p1, p1k = ps_alloc()
                mm(p1[:, 0:128], W["ApT"][:, :], S_bd[:, hp, :], True, True, [K("ApT"), ("S_bd", hp)], [p1k])
                tt("dve", W["U"][:, :], p1[:, 0:128], W["Vu"][:, :, :].rearrange("p a b -> p (a b)"), ALU.add, [p1k, K("Vu")], [K("U")])
                yield
                ypt, ypk = y_pt[hp // 4]
                for h2 in range(2):
                    col = (hp % 4) * 128 + h2 * 64
                    hsl = slice(h2 * 64, (h2 + 1) * 64)
                    mm(ypt[:, col:col + 64], W["r_p"][:, :], S_bd[:, hp, hsl], True, False, [K("r_p"), ("S_bd", hp)], [ypk])
                    mm(ypt[:, col:col + 64], W["ArbT"][:, h2, :], W["U"][:, hsl], False, False, [K("ArbT"), K("U")], [ypk])
                    mm(ypt[:, col:col + 64], W["ArkT"][:, h2, :], W["tm4"][:, 3, hsl], False, True, [K("ArkT"), K("tm4")], [ypk])
                p2, p2k = ps_alloc()
                mm(p2[:, 0:128], W["tm4"][:, 1, :], W["U"][:, :], True, False, [K("tm4"), K("U")], [p2k])
                mm(p2[:, 0:128], W["tm4"][:, 2, :], W["tm4"][:, 3, :], False, True, [K("tm4")], [p2k])
                for h2 in range(2):
                    hs = slice(h2 * 64, (h2 + 1) * 64)
                    stt(S_f[hs, hp, :], S_f[hs, hp, :], W["wl"][hs, 0:1], p2[hs, h2 * 64:(h2 + 1) * 64], ALU.mult, ALU.add,
                        [("S_f", hp), K("wl"), p2k], [("S_f", hp)])
                    cp("dve", S_bd[hs, hp, h2 * 64:(h2 + 1) * 64], S_f[hs, hp, :], [("S_f", hp)], [("S_bd", hp)])
            def run_rr(gens):
                gens = list(gens)
                while gens:
                    for g_ in list(gens):
                        try:
                            next(g_)
                        except StopIteration:
                            gens.remove(g_)
            s5g = s5_gen()
            s5_live = [True]

            def paced(n_every, fn):
                k_ = 0
                while True:
                    k_ += 1
                    if k_ % n_every == 0:
                        fn()
                    yield

            def s5_one():
                if s5_live[0]:
                    try:
                        next(s5g)
                    except StopIteration:
                        s5_live[0] = False
            for hp0 in range(0, 8, NSET):
                gens = [wkv_gen(hp0 + j) for j in range(NSET)]
                sp_ = paced(7, s5_one)
                while gens:
                    for g_ in list(gens):
                        try:
                            next(g_)
                        except StopIteration:
                            gens.remove(g_)
                    next(sp_)
                    step_pending(1)
                flush_pending()
            while s5_live[0]:
                s5_one()
            def tail_gen(c, tok0, st):
                ykeys = [y_pt[0][1], y_pt[1][1]]
                cp("act", ysb[:, 0:512], y_pt[0][0][:, :], [ykeys[0]], ["ysb"])
                cp("act", ysb[:, 512:1024], y_pt[1][0][:, :], [ykeys[1]], ["ysb"])
                cp("act", rk_sb[:, :], rk_pt[:, 0:16], [rk_pk], ["rk_sb"])
                y3 = ysb[:, :].rearrange("p (h i) -> p h i", i=64)
                S.add("dve", lambda e, y3=y3: e.tensor_reduce(out=gst[:, 0, :], in_=y3, axis=AX.X, op=ALU.add), ["ysb"], ["gst0"])
                act(ysq[:, :], ysb[:, :], AF.Square, ["ysb"], ["ytmp"])
                S.add("dve", lambda e: e.tensor_reduce(out=gst[:, 1, :], in_=ysq[:, :].rearrange("p (h i) -> p h i", i=64), axis=AX.X, op=ALU.add),
                      ["ytmp"], ["gst1"])
                tt("dve", ytmp[:, :].rearrange("p (h i) -> p h i", i=64), Vall[:, :, :].rearrange("p a (b i) -> p (a b) i", i=64),
                   bc_last(rk_sb[:, :], 64), ALU.mult, [("Vall", h) for h in range(8)] + ["rk_sb"], ["ytmp"])
                yield
                ts("dve", gst[:, 0, :], gst[:, 0, :], 1.0 / 64, ALU.mult, ["gst0"], ["gst0"])
                tt("dve", gst[:, 2, :], gst[:, 0, :], gst[:, 0, :], ALU.mult, ["gst0"], ["gst2"])
                stt(gst[:, 1, :], gst[:, 1, :], 1.0 / 64, gst[:, 2, :], ALU.mult, ALU.subtract, ["gst1", "gst2"], ["gst1"])
                ts("dve", gst[:, 1, :], gst[:, 1, :], 64e-5, ALU.add, ["gst1"], ["gst1"])
                act(gst[:, 1, :], gst[:, 1, :], AF.Ln, ["gst1"], ["gst1"])
                act(gst[:, 1, :], gst[:, 1, :], AF.Exp, ["gst1"], ["gst1"], scale=-0.5)
                tt("dve", y3, y3, bc_last(gst[:, 0, :], 64), ALU.subtract, ["ysb", "gst0"], ["ysb"])
                yield
                tt("dve", y3, y3, bc_last(gst[:, 1, :], 64), ALU.mult, ["ysb", "gst1"], ["ysb"])
                tt("pool", ysb[:, :], ysb[:, :], lnxg_row[:, :], ALU.mult, ["ysb", "lnxg_row"], ["ysb"])
                yield
                tt("pool", ysb[:, :], ysb[:, :], lnxb_row[:, :], ALU.add, ["ysb", "lnxb_row"], ["ysb"])
                tt("dve", ysb[:, :], ysb[:, :], ytmp[:, :], ALU.add, ["ysb", "ytmp"], ["ysb"])
                tt("dve", ya_bf[:, :], ysb[:, :], z_tm[:, c, 0:1024], ALU.mult, ["ysb", ("z_tm", "ga")], ["ya_bf"])
                yield
                pt, pk = ps_alloc()
                ptb = pt[:, :].bitcast(BF16)
                for kc in range(8):
                    S.add("pe", lambda e, ptb=ptb, kc=kc: e.transpose(ptb[:, kc * 128:(kc + 1) * 128], ya_bf[:, kc * 128:(kc + 1) * 128], ident[:, :]),
                          ["ya_bf", "ident"], [pk])
                cp("act", yaT[:, :, :].rearrange("p a b -> p (a b)"), ptb[:, :], [pk], ["yaT"])
                yield
                for half in range(2):
                    po, pok = ps_alloc()
                    for kc in range(8):
                        mm(po[:, :], yaT[:, kc, :], pa_bf[:, kc, half * 512:(half + 1) * 512], kc == 0, kc == 7, ["yaT", "pa_bf"], [pok])
                    tt("dve", merged_a[:, half * 512:(half + 1) * 512], po[:, :], z_tm[:, c, 1536 + half * 512:1536 + (half + 1) * 512],
                       ALU.mult, [pok, ("z_tm", "ma")], ["ya_bf"])
                    yield
                for half in range(2):
                    po, pok = ps_alloc()
                    hsl = slice(half * 512, (half + 1) * 512)
                    for kc in range(4):
                        mm(po[:, :], ybT[:, kc, :], pb_bf[:, kc, hsl], kc == 0, kc == 3, ["ybT", "pb_bf"], [pok])
                    tt("dve", ysb[:, hsl], po[:, :], z_tm[:, c, 2560 + half * 512:2560 + (half + 1) * 512], ALU.mult, [pok, ("z_tm", "mb")], ["ysb"])
                    tt("pool", merged_bf[:, hsl], ysb[:, hsl], merged_a[:, hsl], ALU.add, ["ysb", "ya_bf"], ["merged_bf"])
                    yield
                pt, pk = ps_alloc()
                ptb = pt[:, :].bitcast(BF16)
                for kc in range(8):
                    S.add("pe", lambda e, ptb=ptb, kc=kc: e.transpose(ptb[:, kc * 128:(kc + 1) * 128], merged_bf[:, kc * 128:(kc + 1) * 128], ident[:, :]),
                          ["merged_bf", "ident"], [pk])
                cp("act", mT[:, :, :].rearrange("p a b -> p (a b)"), ptb[:, :], [pk], ["yaT"])
                dma("pool", xres[0][:, :], x_d[tok0:tok0 + 128, :], [], ["ytmp"])
                yield
                for half in range(2):
                    po, pok = ps_alloc()
                    hsl = slice(half * 512, (half + 1) * 512)
                    for kc in range(8):
                        mm(po[:, :], mT[:, kc, :], wout_bf[:, kc, hsl], kc == 0, kc == 7, ["yaT", "wout_bf"], [pok])
                    tt("dve", xo[:, hsl], po[:, :], xres[0][:, hsl], ALU.add, [pok, "ytmp"], ["ysb"])
                    yield
                act(junk_bf[:, :], xo[:, :], AF.Square, ["ysb"], ["merged_bf", "ssq0"], accum=ssq[:, 0:1])
                ts("dve", ssq[:, 1:2], ssq[:, 0:1], 1.0 / D, ALU.mult, ["ssq0"], ["ssq1"], s2=1e-6, op1=ALU.add)
                act(ssq[:, 2:3], ssq[:, 1:2], AF.Ln, ["ssq1"], ["ssq2"])
                act(ssq[:, 3:4], ssq[:, 2:3], AF.Exp, ["ssq2"], ["ssq3"], scale=-0.5)
                stt(outt[:, :], xo[:, :], ssq[:, 3:4], fg_row[:, :], ALU.mult, ALU.mult, ["ysb", "ssq3", "fg_row"], ["ytmp"])
                ok = ("out", st, c)
                dma("sp", out_d[tok0:tok0 + 128, :], outt[:, :], ["ytmp"], [ok])
                out_keys.append(ok)

            flush_pending()
            tg = tail_gen(c, tok0, st)
            next(tg)
            pending.append(tg)
    flush_pending()
    S.add("sp", None, out_keys, [])
    print("SBUF bytes/partition:", sb_bytes[0])
    S.emit(nc, es)
    es.close()
    return nc


_NC_CACHE = {}


def kernel(**inputs):
    T = 8192
    if T not in _NC_CACHE:
        _NC_CACHE[T] = build_program(T)
    nc = _NC_CACHE[T]
    f = lambda a, shp: np.ascontiguousarray(np.asarray(a, dtype=np.float32).reshape(shp))
    common = {
        "norm_g": f(inputs["norm_g"], (1, D)), "w_in": f(inputs["w_in"], (D, 7296)),
        "mu_shift": f(inputs["mu_shift"], (1, 3200)), "w0": f(inputs["w0"], (1, D)),
        "w_up": f(inputs["w_up"], (64, D)), "a0": f(inputs["a0"], (1, D)), "a_up": f(inputs["a_up"], (64, D)),
        "k_k": f(inputs["k_k"], (1, D)), "k_a": f(inputs["k_a"], (1, D)), "r_k": f(inputs["r_k"], (1, D)),
        "lnx_g": f(inputs["lnx_g"], (1, D)), "lnx_b": f(inputs["lnx_b"], (1, D)),
        "lam_re": f(inputs["lam_re"], (32, 64)), "lam_im": f(inputs["lam_im"], (32, 64)),
        "log_dt": f(inputs["log_dt"], (32, 1)),
        "b_re": f(inputs["b_re"], (32, 1024)), "b_im": f(inputs["b_im"], (32, 1024)),
        "c_re": f(inputs["c_re"], (512, 64)), "c_im": f(inputs["c_im"], (512, 64)),
        "d_skip": f(inputs["d_skip"], (1, 512)), "w_glu": f(inputs["w_glu"], (512, D)),
        "b_glu": f(inputs["b_glu"], (1, D)), "p_a": f(inputs["p_a"], (D, D)), "p_b": f(inputs["p_b"], (512, D)),
        "w_out": f(inputs["w_out"], (D, D)), "final_g": f(inputs["final_g"], (1, D)),
    }
    x = np.asarray(inputs["x"], dtype=np.float32)
    in_maps = []
    for c in range(8):
        m = dict(common)
        m["x"] = np.ascontiguousarray(x[c % 4])
        in_maps.append(m)
    res = run_bass_kernel_spmd(nc, in_maps, core_ids=list(range(8)))
    out = np.stack([np.asarray(res.results[b]["out"], dtype=np.float32) for b in range(4)], axis=0)
    return out
```
